# Optimizing a Trainium2 kernel written in Bass

```python
import jax, jax.numpy as jnp
from jax import lax
import numpy as np

D_MODEL = 1024
BATCH = 16
SEQ = 4096
DEPTH = 4

GRID_W = 64
CHUNK = 128
Q_BLOCK = 128
N_HEADS = 8
N_KV_HEADS = 2
HEAD_DIM = 64
GQA_GROUP = N_HEADS // N_KV_HEADS
ATTN_WIDTH = N_HEADS * HEAD_DIM
KV_WIDTH = N_KV_HEADS * HEAD_DIM
SG_HEADS = 8
SG_HEAD_DIM = 64
SG_WIDTH = SG_HEADS * SG_HEAD_DIM
MIX_WIDTH = ATTN_WIDTH + SG_WIDTH
IN_WIDTH = ATTN_WIDTH + 2 * KV_WIDTH + 2 * SG_WIDTH
D_FF = 2816
CONV_W = 3
ROPE_THETA = 10000.0
EPS = 1e-6

kernel_name = "hybrid_gmlp_gqa_axialrope_convffn"


def rms_norm(x, g):
    xf = x.astype(jnp.float32)
    y = xf * lax.rsqrt(jnp.mean(xf * xf, axis=-1, keepdims=True) + EPS)
    return (y * g.astype(jnp.float32)).astype(x.dtype)


def axial_rope_tables(seq_len):
    rows = seq_len // GRID_W
    row = jnp.repeat(jnp.arange(rows, dtype=jnp.float32), GRID_W)
    col = jnp.tile(jnp.arange(GRID_W, dtype=jnp.float32), rows)
    axis_dim = HEAD_DIM // 2
    inv_freq = 1.0 / (ROPE_THETA ** (jnp.arange(0, axis_dim, 2, dtype=jnp.float32) / axis_dim))
    ang_r = row[:, None] * inv_freq[None, :]
    ang_c = col[:, None] * inv_freq[None, :]
    return jnp.cos(ang_r), jnp.sin(ang_r), jnp.cos(ang_c), jnp.sin(ang_c)


def rotate(x, cos, sin):
    n = x.shape[-1] // 2
    x1, x2 = x[..., :n], x[..., n:]
    c = cos[None, :, None, :].astype(x.dtype)
    s = sin[None, :, None, :].astype(x.dtype)
    return jnp.concatenate([x1 * c - x2 * s, x2 * c + x1 * s], axis=-1)


def axial_rope(x, tables):
    cr, sr, cc, sc = tables
    half = HEAD_DIM // 2
    return jnp.concatenate([rotate(x[..., :half], cr, sr), rotate(x[..., half:], cc, sc)], axis=-1)


def gqa_attention(q, k, v):
    b, s, _, dh = q.shape
    nblk = s // Q_BLOCK
    qb = q.reshape(b, nblk, Q_BLOCK, N_KV_HEADS, GQA_GROUP, dh)
    qb = jnp.moveaxis(qb, 1, 0)
    scale = HEAD_DIM ** -0.5

    def one_block(qblk):
        sc = jnp.einsum('bqkgd,bskd->bkgqs', qblk, k).astype(jnp.float32) * scale
        p = jax.nn.softmax(sc, axis=-1).astype(v.dtype)
        return jnp.einsum('bkgqs,bskd->bqkgd', p, v)

    o = lax.map(one_block, qb)
    return jnp.moveaxis(o, 0, 1).reshape(b, s, ATTN_WIDTH)


def spatial_gating(u, vv, sg_norm_g, sg_w, sg_b):
    b, s, _ = u.shape
    nc = s // CHUNK
    vv = rms_norm(vv.reshape(b, s, SG_HEADS, SG_HEAD_DIM), sg_norm_g)
    vv = vv.reshape(b, nc, CHUNK, SG_HEADS, SG_HEAD_DIM)
    mixed = jnp.einsum('hpq,bnqhd->bnphd', sg_w, vv) + sg_b.T[:, :, None]
    out = u.reshape(b, nc, CHUNK, SG_HEADS, SG_HEAD_DIM) * mixed
    return out.reshape(b, s, SG_WIDTH)


def depthwise_conv(h, w, bias):
    c = h.shape[-1]
    y = lax.conv_general_dilated(h, w[:, None, :].astype(h.dtype), window_strides=(1,),
                                 padding=[(CONV_W // 2, CONV_W // 2)],
                                 dimension_numbers=('NWC', 'WIO', 'NWC'),
                                 feature_group_count=c)
    return y + bias


def setup_inputs(seed: int = 0) -> dict:
    key = jax.random.key(seed)
    ks = jax.random.split(key, 17)
    f32 = jnp.float32
    nrm = lambda k, shape, scale: jax.random.normal(k, shape, f32) * scale
    gain = lambda k, shape: 1.0 + 0.05 * jax.random.normal(k, shape, f32)
    return {
        "x": jax.random.normal(ks[0], (BATCH, SEQ, D_MODEL), f32),
        "attn_norm_g": gain(ks[1], (DEPTH, D_MODEL)),
        "w_in": nrm(ks[2], (DEPTH, D_MODEL, IN_WIDTH), D_MODEL ** -0.5),
        "q_norm_g": gain(ks[3], (DEPTH, HEAD_DIM)),
        "k_norm_g": gain(ks[4], (DEPTH, HEAD_DIM)),
        "sg_norm_g": gain(ks[5], (DEPTH, SG_HEADS, SG_HEAD_DIM)),
        "sg_w": nrm(ks[6], (DEPTH, SG_HEADS, CHUNK, CHUNK), 0.5 * CHUNK ** -0.5),
        "sg_b": gain(ks[7], (DEPTH, SG_HEADS, CHUNK)),
        "attn_out_g": gain(ks[8], (DEPTH, ATTN_WIDTH)),
        "sg_out_g": gain(ks[9], (DEPTH, SG_WIDTH)),
        "w_o": nrm(ks[10], (DEPTH, MIX_WIDTH, D_MODEL), MIX_WIDTH ** -0.5),
        "ffn_norm_g": gain(ks[11], (DEPTH, D_MODEL)),
        "w_up": nrm(ks[12], (DEPTH, D_MODEL, 2 * D_FF), D_MODEL ** -0.5),
        "conv_w": nrm(ks[13], (DEPTH, CONV_W, 2 * D_FF), CONV_W ** -0.5),
        "conv_b": nrm(ks[14], (DEPTH, 2 * D_FF), 0.02),
        "w_down": nrm(ks[15], (DEPTH, D_FF, D_MODEL), D_FF ** -0.5),
    }


def reference(x, attn_norm_g, w_in, q_norm_g, k_norm_g, sg_norm_g, sg_w, sg_b,
              attn_out_g, sg_out_g, w_o, ffn_norm_g, w_up, conv_w, conv_b, w_down):
    b, s, _ = x.shape
    tables = axial_rope_tables(s)
    split_at = [ATTN_WIDTH, ATTN_WIDTH + KV_WIDTH, ATTN_WIDTH + 2 * KV_WIDTH,
                ATTN_WIDTH + 2 * KV_WIDTH + SG_WIDTH]
    for i in range(DEPTH):
        h = rms_norm(x, attn_norm_g[i])
        proj = h @ w_in[i]
        q, k, v, u, vv = jnp.split(proj, split_at, axis=-1)
        q = rms_norm(q.reshape(b, s, N_HEADS, HEAD_DIM), q_norm_g[i])
        k = rms_norm(k.reshape(b, s, N_KV_HEADS, HEAD_DIM), k_norm_g[i])
        v = v.reshape(b, s, N_KV_HEADS, HEAD_DIM)
        q = axial_rope(q, tables)
        k = axial_rope(k, tables)
        attn_o = gqa_attention(q, k, v)
        sg_o = spatial_gating(jax.nn.gelu(u), jax.nn.gelu(vv),
                              sg_norm_g[i], sg_w[i], sg_b[i])
        merged = jnp.concatenate([rms_norm(attn_o, attn_out_g[i]),
                                  rms_norm(sg_o, sg_out_g[i])], axis=-1)
        x = x + merged @ w_o[i]
        h = rms_norm(x, ffn_norm_g[i]) @ w_up[i]
        h = depthwise_conv(h, conv_w[i], conv_b[i])
        gate, up = jnp.split(h, 2, axis=-1)
        x = x + (jax.nn.silu(gate) * up) @ w_down[i]
    return x
```

```python
import numpy as np
import concourse.bass as bass
import concourse.mybir as mybir
from concourse.bass_utils import run_bass_kernel_spmd

F32 = mybir.dt.float32
BF16 = mybir.dt.bfloat16
AF = mybir.ActivationFunctionType
ALU = mybir.AluOpType
AX = mybir.AxisListType

D = 1024
NDK = 8
DFF = 2816
NCJ = 22
EPS = 1e-6
NPAR = 208
PAGE = 512
GELU_C = 0.7978845608028654

U4 = 128 * 8 * 256
UV = 128 * 8 * 128
UDN = 128 * 11 * 128


def layer_layout():
    off = {}
    o = 0
    off["kk"] = o; o += U4
    off["v"] = o; o += UV
    a_end = o
    for c in range(4):
        off[f"q{c}"] = o; o += U4
    for h in range(2):
        off[f"vv{h}"] = o; o += U4
    off["sgw"] = o; o += UV
    for h in range(2):
        off[f"u{h}"] = o; o += U4
    for h in range(4):
        off[f"wo{h}"] = o; o += U4
    b_end = o
    for j in range(NCJ):
        off[f"up{j}"] = o; o += U4
    for j in range(16):
        off[f"dn{j}"] = o; o += UDN
    c_end = o
    return off, (0, a_end, b_end, c_end)


LOFF, LPARTS = layer_layout()
LTOT = LPARTS[3]


def ffn_tiles(S):
    n = -(-S // 510)
    base = -(-S // n)
    tiles = []
    o = 0
    while o < S:
        w = min(base, S - o)
        tiles.append((o, o + w))
        o += w
    return tiles


class Tracker:
    def __init__(self):
        self.pages = {}

    def _page(self, i):
        p = self.pages.get(i)
        if p is None:
            p = ({}, {})
            self.pages[i] = p
        return p

    def read(self, ranges, key, ref, deps):
        for lo, hi in ranges:
            for i in range(lo // PAGE, (hi - 1) // PAGE + 1):
                w, r = self._page(i)
                for v in w.values():
                    deps.append((v, True))
                r[key] = ref

    def write(self, ranges, key, ref, deps):
        for lo, hi in ranges:
            for i in range(lo // PAGE, (hi - 1) // PAGE + 1):
                w, r = self._page(i)
                for v in w.values():
                    deps.append((v, False))
                for v in r.values():
                    deps.append((v, False))
                if lo <= i * PAGE and hi >= (i + 1) * PAGE:
                    w.clear(); r.clear()
                w[key] = ref


class View:
    __slots__ = ("ap", "trk", "ranges")

    def __init__(self, ap, trk, ranges):
        self.ap = ap; self.trk = trk; self.ranges = ranges

    def map(self, f):
        return View(f(self.ap), self.trk, self.ranges)


class Tens:
    def __init__(self, handle, shape, esize, off, trk):
        self.h = handle; self.shape = list(shape); self.esize = esize; self.off = off; self.trk = trk
        self.fstr = []
        s = 1
        for n in reversed(self.shape[1:]):
            self.fstr.insert(0, s); s *= n
        self.fsize = s

    def v(self, *key):
        key = list(key)
        while len(key) < len(self.shape):
            key.append(slice(None))
        ap = self.h[tuple(key)]
        fk = key[1:]
        spans = []
        for k, n in zip(fk, self.shape[1:]):
            if isinstance(k, slice):
                a = 0 if k.start is None else k.start
                b = n if k.stop is None else k.stop
            else:
                a, b = k, k + 1
            assert 0 <= a < b <= n, (key, self.shape)
            spans.append((a, b))
        ranges = []

        def rec(d, base):
            if d == len(spans):
                ranges.append((base, base + 1)); return
            a, b = spans[d]
            full_tail = all(spans[j] == (0, self.shape[1 + j]) for j in range(d + 1, len(spans)))
            if full_tail:
                ranges.append((base + a * self.fstr[d], base + b * self.fstr[d]))
            else:
                for i in range(a, b):
                    rec(d + 1, base + i * self.fstr[d])
        rec(0, 0)
        br = [(self.off + lo * self.esize, self.off + hi * self.esize) for lo, hi in ranges]
        return View(ap, self.trk, br)


class Sem:
    def __init__(self, h):
        self.h = h; self.count = 0


class EngQ:
    def __init__(self, name):
        self.name = name
        self.instrs = []
        self.sem = None


class Prog:
    def __init__(self, nc):
        self.nc = nc
        self.q = {n: EngQ(n) for n in ("pe", "act", "dve", "pool", "sp")}
        self.sb = Tracker(); self.ps = Tracker()
        self.dram = {}
        self.sems = []

    def dram_view(self, ap, name):
        if name not in self.dram:
            self.dram[name] = Tracker()
        return View(ap, self.dram[name], [(0, PAGE)])

    def _deps(self, eng, R, W, key, ref):
        deps = []
        for v in R:
            v.trk.read(v.ranges, key, ref, deps)
        for v in W:
            v.trk.write(v.ranges, key, ref, deps)
        out = {}
        self_idx = ref[2] if ref[0] == "e" else -1
        for (d, is_raw) in deps:
            if d[0] == "e":
                if d[1] == eng:
                    if eng == "pe":
                        continue
                    if d[2] == self_idx:
                        continue
                k = ("e", d[1])
                if k not in out or out[k] < d[2]:
                    out[k] = d[2]
            else:
                k = ("d", id(d[1]))
                if k not in out or out[k][1] < d[2]:
                    out[k] = (d[1], d[2])
        return out

    def op(self, eng, fn, R=(), W=()):
        q = self.q[eng]
        idx = len(q.instrs)
        ref = ("e", eng, idx)
        deps = self._deps(eng, R, W, eng, ref)
        waits = []
        for k, v in deps.items():
            if k[0] == "e":
                self.q[k[1]].instrs[v][2] = True
                waits.append(("e", k[1], v))
            else:
                waits.append(("d", v[0], v[1]))
        q.instrs.append([fn, waits, False, None])
        return ref

    def dma(self, eng, sem, fns, R=(), W=()):
        q = self.q[eng]
        val = sem.count + 16 * len(fns)
        ref = ("d", sem, val)
        deps = self._deps(eng, R, W, ("d", id(sem)), ref)
        waits = []
        for k, v in deps.items():
            if k[0] == "e":
                self.q[k[1]].instrs[v][2] = True
                waits.append(("e", k[1], v))
            else:
                if v[0] is sem and v[1] >= val:
                    continue
                waits.append(("d", v[0], v[1]))
        for i, fn in enumerate(fns):
            q.instrs.append([fn, waits if i == 0 else [], False, sem])
        sem.count = val
        return ref

    def final_wait(self, eng, sem):
        self.q[eng].instrs.append([None, [("d", sem, sem.count)], False, None])

    def replay(self, engines):
        vals = {}
        for n, q in self.q.items():
            c = 0
            vv = []
            for ins in q.instrs:
                if ins[2]:
                    c += 1
                vv.append(c)
            vals[n] = vv
        for n, q in self.q.items():
            e = engines[n]
            waited = {}
            for ins in q.instrs:
                fn, waits, sig, dsem = ins
                for w in waits:
                    if w[0] == "e":
                        semh = self.q[w[1]].sem; v = vals[w[1]][w[2]]; k = w[1]
                    else:
                        semh = w[1].h; v = w[2]; k = id(w[1])
                    if waited.get(k, 0) >= v:
                        continue
                    waited[k] = v
                    e.wait_ge(semh, v)
                if fn is None:
                    continue
                i = fn(e)
                if dsem is not None:
                    i.then_inc(dsem.h, 16)
                elif sig:
                    i.then_inc(q.sem, 1)


def build_program(S, NSEQ, L):
    nc = bass.Bass("TRN2", target_bir_lowering=False)
    P = Prog(nc)
    NQT = S // 512
    NKC = S // 128
    FT = ffn_tiles(S)
    WOM = max(b - a for a, b in FT)
    WIM = WOM + 2

    xT_d = nc.dram_tensor("xT", [NSEQ, D, S], F32, kind="ExternalInput").ap()
    oT_d = nc.dram_tensor("oT", [NSEQ, D, S], F32, kind="ExternalOutput").ap()
    wf_d = [nc.dram_tensor(f"wf{l}", [LTOT // 4096, 4096], F32, kind="ExternalInput").ap() for l in range(L)]
    wb_d = [nc.dram_tensor(f"wb{l}", [LTOT // 4096, 4096], BF16, kind="Internal").ap() for l in range(L)]
    par_d = nc.dram_tensor("par", [128, L * NPAR], F32, kind="ExternalInput").ap()
    bcg_d = nc.dram_tensor("bcg", [L, 2, 512], F32, kind="ExternalInput").ap()
    rc_d = nc.dram_tensor("ropeC", [128, S], F32, kind="ExternalInput").ap()
    rs_d = nc.dram_tensor("ropeS", [128, S], F32, kind="ExternalInput").ap()
    cst_d = nc.dram_tensor("cst", [3, 128, 128], F32, kind="ExternalInput").ap()

    base = (nc.sbuf_base + 31) // 32 * 32
    top = nc.sbuf_top
    cur = [base]

    def alloc(name, shape, dt, at=None):
        es = 4 if dt == F32 else 2
        n = 1
        for s in shape[1:]:
            n *= s
        nbytes = (n * es + 31) // 32 * 32
        if at is None:
            off = cur[0]; cur[0] += nbytes
        else:
            off = at
        assert off + nbytes <= top, (name, off, nbytes, top)
        h = nc.alloc_sbuf_tensor_at(name, list(shape), dt, offset=off)
        return Tens(h, shape, es, off, P.sb)

    xT = alloc("xT_sb", [128, NDK, S], F32)
    ones_b = alloc("ones_b", [128, 128], BF16)
    perm_b = alloc("perm_b", [128, 128], BF16)
    blk_b = alloc("blk_b", [128, 128], BF16)
    idn_b = alloc("idn_b", [128, 128], BF16)
    par = alloc("par_sb", [128, L, NPAR], F32)
    phase_base = cur[0]

    NRB = 3
    KT = alloc("KT", [128, S], BF16)
    Vx = alloc("Vx", [128, NKC, 192], BF16)
    ringB = [alloc(f"ringB{i}", [128, 2048], BF16) for i in range(NRB)]
    hnT = alloc("hnT", [128, NDK, 512], BF16)
    sqT = alloc("sqT", [128, NDK, 512], BF16)
    mergedT = Tens(sqT.h, [128, NDK, 512], 2, sqT.off, P.sb)
    QT = alloc("QT", [128, 4, 512], BF16)
    tabC = alloc("tabC", [128, 512], F32)
    tabS = alloc("tabS", [128, 512], F32)
    t1 = alloc("t1", [128, 512], F32)
    t2 = alloc("t2", [128, 512], F32)
    vvn = alloc("vvn", [128, 4, 512], BF16)
    t3 = alloc("t3", [128, 512], BF16)
    gsg_bc = alloc("gsg_bc", [128, 512], F32)
    gout_bc = alloc("gout_bc", [128, 512], F32)
    sqq = alloc("sqq", [128, 512], BF16)
    rq = alloc("rq", [128, 512], F32)
    rstd = alloc("rstd", [128, 512], F32)
    sm8 = alloc("sm8", [128, 8], F32)
    sm1 = alloc("sm1", [128, 8], F32)
    endB = cur[0]
    aoT = Tens(nc.alloc_sbuf_tensor_at("aoT", [128, 4, 512], F32, offset=hnT.off), [128, 4, 512], 4, hnT.off, P.sb)
    NPT = 3
    PT = [Tens(nc.alloc_sbuf_tensor_at(f"PT{i}", [128, 512], BF16, offset=t1.off + 1024 * i), [128, 512], 2,
               t1.off + 1024 * i, P.sb) for i in range(NPT)]
    rD = Tens(nc.alloc_sbuf_tensor_at("rD", [128, 512], F32, offset=tabC.off), [128, 512], 4, tabC.off, P.sb)
    rhi = Tens(nc.alloc_sbuf_tensor_at("rhi", [128, 512], BF16, offset=tabS.off), [128, 512], 2, tabS.off, P.sb)
    rlo = Tens(nc.alloc_sbuf_tensor_at("rlo", [128, 512], BF16, offset=tabS.off + 1024), [128, 512], 2,
               tabS.off + 1024, P.sb)
    sqa = Tens(nc.alloc_sbuf_tensor_at("sqa", [128, 4, 512], BF16, offset=vvn.off), [128, 4, 512], 2, vvn.off, P.sb)

    cur[0] = phase_base
    NRC = 5
    ringC = [alloc(f"ringC{i}", [128, 2048], BF16) for i in range(NRC)]
    hnC = [alloc(f"hnC{i}", [128, NDK, WIM], BF16) for i in range(2)]
    sqC = alloc("sqC", [128, NDK, WIM], BF16)
    rstdC = alloc("rstdC", [128, WIM], F32)
    gT = alloc("gT", [128, NCJ, WOM], BF16)
    ctmp = [[alloc(f"c{n}{i}", [128, WOM], F32) for n in ("ag", "au", "th")] for i in range(2)]
    endC = cur[0]
    assert max(endB, endC) <= top, (endB, endC, top)

    psh = nc.alloc_psum_tensor("ps", [128, 4096], F32)
    ps = Tens(psh, [128, 4096], 4, 0, P.ps)

    def bank(b, p=slice(None), c0=0, c1=512):
        return ps.v(p, slice(512 * b + c0, 512 * b + c1))

    def sem(name):
        s = Sem(nc.alloc_semaphore(name))
        P.sems.append(s)
        return s

    for n in P.q:
        P.q[n].sem = nc.alloc_semaphore("prog_" + n)
    s_cst = sem("s_cst"); s_par = sem("s_par"); s_x = sem("s_x"); s_out = sem("s_out")
    s_tab = sem("s_tab"); s_bcg = sem("s_bcg")
    s_cast = [[sem(f"s_cast{l}_{p}") for p in range(3)] for l in range(L)]
    s_ringB = [sem(f"s_rb{i}") for i in range(NRB)]
    s_ringC = [sem(f"s_rc{i}") for i in range(NRC)]

    wbv = [[P.dram_view(wb_d[l], f"wb{l}_{p}") for p in range(3)] for l in range(L)]
    ring_state = {"B": [ringB, s_ringB, 0], "C": [ringC, s_ringC, 0]}

    def wload(l, name, which):
        ring, sems, i = ring_state[which]
        slot = i % len(ring)
        ring_state[which][2] = i + 1
        off = LOFF[name]
        part = 0 if off < LPARTS[1] else (1 if off < LPARTS[2] else 2)
        n = UV if name in ("v", "sgw") else (UDN if name.startswith("dn") else U4)
        F = n // 128
        src = wb_d[l].rearrange("r c -> (r c)")[off:off + n].rearrange("(p f) -> p f", p=128)
        dst = ring[slot].v(slice(None), slice(0, F))
        P.dma("sp", sems[slot], [lambda e, d=dst.ap, s=src: e.dma_start(out=d, in_=s)],
              R=[wbv[l][part]], W=[dst])
        return ring[slot], F

    P.dma("pool", s_cst,
          [lambda e: e.dma_start(out=perm_b.h[:], in_=cst_d[0]),
           lambda e: e.dma_start(out=blk_b.h[:], in_=cst_d[1]),
           lambda e: e.dma_start(out=idn_b.h[:], in_=cst_d[2])],
          W=[perm_b.v(), blk_b.v(), idn_b.v()])
    P.dma("pool", s_par, [lambda e: e.dma_start(out=par.h[:].rearrange("p l n -> p (l n)"), in_=par_d)], W=[par.v()])
    P.op("dve", lambda e: e.memset(ones_b.h[:], 1.0), W=[ones_b.v()])

    def cast(l, p):
        r0, r1 = LPARTS[p] // 4096, LPARTS[p + 1] // 4096
        P.dma("pool", s_cast[l][p], [lambda e: e.dma_start(out=wb_d[l][r0:r1, :], in_=wf_d[l][r0:r1, :])],
              W=[wbv[l][p]])

    def xload(s):
        P.dma("pool", s_x,
              [lambda e, dk=dk: e.dma_start(out=xT.h[:, dk, :], in_=xT_d[s, dk * 128:(dk + 1) * 128, :])
               for dk in range(NDK)], W=[xT.v()])

    cast(0, 0)
    xload(0)
    for l in range(L):
        for p in range(3):
            if (l, p) != (0, 0):
                cast(l, p)

    def pcol(l, i):
        return par.v(slice(None), l, slice(i, i + 1))

    def rms_front(l, gofs, xcols, dst, dst_c0, width, sq, rs, nbank):
        c0, c1 = xcols
        xs = xT.v(slice(None), slice(None), slice(c0, c1))
        sqv = sq.v(slice(None), slice(None), slice(0, width))
        P.op("act", lambda e: e.activation(out=sqv.ap, in_=xs.ap, func=AF.Square), R=[xs], W=[sqv])
        nb = bank(nbank, c1=width)
        for dk in range(NDK):
            s_ = sq.v(slice(None), dk, slice(0, width))
            P.op("pe", lambda e, s_=s_, dk=dk: e.matmul(nb.ap, ones_b.h[:], s_.ap, start=(dk == 0), stop=(dk == NDK - 1)),
                 R=[ones_b.v(), s_], W=[nb])
        rv = rs.v(slice(None), slice(0, width))
        P.op("act", lambda e: e.activation(out=rv.ap, in_=nb.ap, func=AF.Sqrt, scale=1.0 / D, bias=EPS), R=[nb], W=[rv])
        P.op("dve", lambda e: e.reciprocal(out=rv.ap, in_=rv.ap), R=[rv], W=[rv])
        for dk in range(NDK):
            xv = xT.v(slice(None), dk, slice(c0, c1))
            dv = dst.v(slice(None), dk, slice(dst_c0, dst_c0 + width))
            g = pcol(l, gofs + dk)
            P.op("dve", lambda e, xv=xv, dv=dv, g=g: e.scalar_tensor_tensor(
                out=dv.ap, in0=xv.ap, scalar=g.ap, in1=rv.ap, op0=ALU.mult, op1=ALU.mult), R=[xv, g, rv], W=[dv])

    def load_tables(j):
        P.dma("pool", s_tab,
              [lambda e: e.dma_start(out=tabC.h[:], in_=rc_d[:, j * 512:(j + 1) * 512]),
               lambda e: e.dma_start(out=tabS.h[:], in_=rs_d[:, j * 512:(j + 1) * 512])],
              W=[tabC.v(), tabS.v()])

    def proj_rope(l, wt, gcol, dstv):
        bq, bs, bn = bank(1), bank(2), bank(0)
        for half, bb in ((0, bq), (1, bs)):
            for dk in range(NDK):
                w_ = wt.v(slice(None), slice(dk * 256 + half * 128, dk * 256 + half * 128 + 128))
                h_ = hnT.v(slice(None), dk, slice(None))
                P.op("pe", lambda e, w_=w_, h_=h_, bb=bb, dk=dk: e.matmul(bb.ap, w_.ap, h_.ap, start=(dk == 0), stop=(dk == NDK - 1)),
                     R=[w_, h_], W=[bb])
        sv = sqq.v()
        P.op("act", lambda e: e.activation(out=sv.ap, in_=bq.ap, func=AF.Square), R=[bq], W=[sv])
        P.op("pe", lambda e: e.matmul(bn.ap, blk_b.h[:], sv.ap, start=True, stop=True), R=[blk_b.v(), sv], W=[bn])
        rv = rq.v()
        P.op("act", lambda e: e.activation(out=rv.ap, in_=bn.ap, func=AF.Sqrt, scale=1.0 / 64, bias=EPS), R=[bn], W=[rv])
        P.op("dve", lambda e: e.reciprocal(out=rv.ap, in_=rv.ap), R=[rv], W=[rv])
        a, b = t1.v(), t2.v()
        g0, g1 = pcol(l, gcol), pcol(l, gcol + 1)
        tc_, ts_ = tabC.v(), tabS.v()
        P.op("dve", lambda e: e.scalar_tensor_tensor(out=a.ap, in0=bq.ap, scalar=g0.ap, in1=tc_.ap, op0=ALU.mult, op1=ALU.mult),
             R=[bq, g0, tc_], W=[a])
        P.op("dve", lambda e: e.scalar_tensor_tensor(out=b.ap, in0=bs.ap, scalar=g1.ap, in1=ts_.ap, op0=ALU.mult, op1=ALU.mult),
             R=[bs, g1, ts_], W=[b])
        P.op("dve", lambda e: e.tensor_tensor(out=a.ap, in0=a.ap, in1=b.ap, op=ALU.add), R=[a, b], W=[a])
        P.op("dve", lambda e: e.tensor_tensor(out=dstv.ap, in0=a.ap, in1=rv.ap, op=ALU.mult), R=[a, rv], W=[dstv])

    def gelu2(src, dst):
        a, b = t1.v(), t2.v()
        P.op("act", lambda e: e.activation(out=a.ap, in_=src.ap, func=AF.Square), R=[src], W=[a])
        P.op("dve", lambda e: e.tensor_scalar(out=a.ap, in0=a.ap, scalar1=0.044715, scalar2=1.0, op0=ALU.mult, op1=ALU.add),
             R=[a], W=[a])
        P.op("dve", lambda e: e.tensor_tensor(out=a.ap, in0=a.ap, in1=src.ap, op=ALU.mult), R=[a, src], W=[a])
        P.op("act", lambda e: e.activation(out=b.ap, in_=a.ap, func=AF.Tanh, scale=GELU_C), R=[a], W=[b])
        P.op("dve", lambda e: e.scalar_tensor_tensor(out=dst.ap, in0=b.ap, scalar=1.0, in1=src.ap, op0=ALU.add, op1=ALU.mult),
             R=[b, src], W=[dst])

    for s in range(NSEQ):
        if s > 0:
            xload(s)
        for l in range(L):
            P.dma("pool", s_bcg,
                  [lambda e, l=l: e.dma_start(out=gsg_bc.h[:], in_=bcg_d[l, 0:1, :].partition_broadcast(128)),
                   lambda e, l=l: e.dma_start(out=gout_bc.h[:], in_=bcg_d[l, 1:2, :].partition_broadcast(128))],
                  W=[gsg_bc.v(), gout_bc.v()])
            if True:
                vo = Vx.v(slice(None), slice(None), slice(64, 128))
                P.op("dve", lambda e, vo=vo: e.memset(vo.ap, 1.0), W=[vo])

            for j in range(NQT):
                cols = (j * 512, (j + 1) * 512)
                rms_front(l, 0, cols, hnT, 0, 512, sqT, rstd, 0)
                load_tables(j)
                wt, _ = wload(l, "kk", "B")
                proj_rope(l, wt, 18, KT.v(slice(None), slice(cols[0], cols[1])))
                wv, _ = wload(l, "v", "B")
                bv = bank(3)
                for sub in range(4):
                    o_ = bank(3, c0=sub * 128, c1=sub * 128 + 128)
                    for dk in range(NDK):
                        h_ = hnT.v(slice(None), dk, slice(sub * 128, sub * 128 + 128))
                        w_ = wv.v(slice(None), slice(dk * 128, dk * 128 + 128))
                        P.op("pe", lambda e, o_=o_, h_=h_, w_=w_, dk=dk: e.matmul(o_.ap, h_.ap, w_.ap, start=(dk == 0), stop=(dk == NDK - 1)),
                             R=[h_, w_], W=[o_])
                bv3 = bv.map(lambda ap: ap.rearrange("p (s c) -> p s c", s=4))
                d0 = Vx.v(slice(None), slice(4 * j, 4 * j + 4), slice(0, 64))
                d1 = Vx.v(slice(None), slice(4 * j, 4 * j + 4), slice(128, 192))
                P.op("act", lambda e, d0=d0, bv3=bv3: e.activation(out=d0.ap, in_=bv3.ap[:, :, 0:64], func=AF.Copy), R=[bv], W=[d0])
                P.op("dve", lambda e, d1=d1, bv3=bv3: e.tensor_copy(out=d1.ap, in_=bv3.ap[:, :, 64:128]), R=[bv], W=[d1])

            for j in range(NQT):
                cols = (j * 512, (j + 1) * 512)
                rms_front(l, 0, cols, hnT, 0, 512, sqT, rstd, 0)
                load_tables(j)
                for c in range(4):
                    wt, _ = wload(l, f"q{c}", "B")
                    proj_rope(l, wt, 16, QT.v(slice(None), c, slice(None)))
                wvv = [wload(l, f"vv{h}", "B")[0] for h in range(2)]
                for h in range(2):
                    for sub in range(4):
                        o_ = bank(sub)
                        for dkk in range(4):
                            dk = h * 4 + dkk
                            h_ = hnT.v(slice(None), dk, slice(sub * 128, sub * 128 + 128))
                            w_ = wvv[h].v(slice(None), slice(dkk * 512, dkk * 512 + 512))
                            P.op("pe", lambda e, o_=o_, h_=h_, w_=w_, dk=dk: e.matmul(o_.ap, h_.ap, w_.ap, start=(dk == 0), stop=(dk == NDK - 1)),
                                 R=[h_, w_], W=[o_])
                for sub in range(4):
                    src = bank(sub)
                    gv = t1.v()
                    gelu2(src, gv)
                    b2 = t2.v()
                    P.op("act", lambda e, b2=b2, gv=gv: e.activation(out=b2.ap, in_=gv.ap, func=AF.Square), R=[gv], W=[b2])
                    s8 = sm8.v()
                    P.op("dve", lambda e, b2=b2, s8=s8: e.tensor_reduce(out=s8.ap, in_=b2.ap.rearrange("p (h d) -> p h d", h=8), op=ALU.add, axis=AX.X),
                         R=[b2], W=[s8])
                    P.op("act", lambda e, s8=s8: e.activation(out=s8.ap, in_=s8.ap, func=AF.Sqrt, scale=1.0 / 64, bias=4 * EPS), R=[s8], W=[s8])
                    P.op("dve", lambda e, s8=s8: e.reciprocal(out=s8.ap, in_=s8.ap), R=[s8], W=[s8])
                    P.op("dve", lambda e, gv=gv, s8=s8: e.tensor_tensor(
                        out=gv.ap.rearrange("p (h d) -> p h d", h=8), in0=gv.ap.rearrange("p (h d) -> p h d", h=8),
                        in1=s8.ap.unsqueeze(2).broadcast_to([128, 8, 64]), op=ALU.mult), R=[gv, s8], W=[gv])
                    vn = vvn.v(slice(None), sub, slice(None))
                    gb = gsg_bc.v()
                    P.op("dve", lambda e, vn=vn, gv=gv, gb=gb: e.tensor_tensor(out=vn.ap, in0=gv.ap, in1=gb.ap, op=ALU.mult), R=[gv, gb], W=[vn])
                wsg, _ = wload(l, "sgw", "B")
                for sub in range(4):
                    for hh in range(8):
                        o_ = bank(4 + sub, c0=hh * 64, c1=hh * 64 + 64)
                        w_ = wsg.v(slice(None), slice(hh * 128, hh * 128 + 128))
                        v_ = vvn.v(slice(None), sub, slice(hh * 64, hh * 64 + 64))
                        P.op("pe", lambda e, o_=o_, w_=w_, v_=v_: e.matmul(o_.ap, w_.ap, v_.ap, start=True, stop=True), R=[w_, v_], W=[o_])
                wu = [wload(l, f"u{h}", "B")[0] for h in range(2)]
                for h in range(2):
                    for sub in range(4):
                        o_ = bank(sub)
                        for dkk in range(4):
                            dk = h * 4 + dkk
                            h_ = hnT.v(slice(None), dk, slice(sub * 128, sub * 128 + 128))
                            w_ = wu[h].v(slice(None), slice(dkk * 512, dkk * 512 + 512))
                            P.op("pe", lambda e, o_=o_, h_=h_, w_=w_, dk=dk: e.matmul(o_.ap, h_.ap, w_.ap, start=(dk == 0), stop=(dk == NDK - 1)),
                                 R=[h_, w_], W=[o_])
                for sub in range(4):
                    src = bank(sub)
                    gu = t1.v()
                    gelu2(src, gu)
                    mx = bank(4 + sub)
                    b2 = t2.v()
                    sb_ = par.v(slice(None), l, slice(24, 32))
                    P.op("dve", lambda e, b2=b2, mx=mx, sb_=sb_: e.tensor_tensor(
                        out=b2.ap.rearrange("p (h d) -> p h d", h=8), in0=mx.ap.rearrange("p (h d) -> p h d", h=8),
                        in1=sb_.ap.unsqueeze(2).broadcast_to([128, 8, 64]), op=ALU.add), R=[mx, sb_], W=[b2])
                    P.op("dve", lambda e, gu=gu, b2=b2: e.tensor_tensor(out=gu.ap, in0=gu.ap, in1=b2.ap, op=ALU.mult), R=[gu, b2], W=[gu])
                    s1 = sm1.v(slice(None), slice(0, 1))
                    P.op("act", lambda e, b2=b2, gu=gu, s1=s1: e.activation(out=b2.ap, in_=gu.ap, func=AF.Square, accum_out=s1.ap), R=[gu], W=[b2, s1])
                    P.op("act", lambda e, s1=s1: e.activation(out=s1.ap, in_=s1.ap, func=AF.Sqrt, scale=1.0 / 512, bias=4 * EPS), R=[s1], W=[s1])
                    P.op("dve", lambda e, s1=s1: e.reciprocal(out=s1.ap, in_=s1.ap), R=[s1], W=[s1])
                    sn = t3.v()
                    go = gout_bc.v()
                    P.op("dve", lambda e, sn=sn, gu=gu, s1=s1, go=go: e.scalar_tensor_tensor(
                        out=sn.ap, in0=gu.ap, scalar=s1.ap, in1=go.ap, op0=ALU.mult, op1=ALU.mult), R=[gu, s1, go], W=[sn])
                    trb = bank(4 + sub, c0=0, c1=256)
                    trb_ap = trb.ap.bitcast(BF16)
                    for fc in range(4):
                        i_ = t3.v(slice(None), slice(fc * 128, fc * 128 + 128))
                        P.op("pe", lambda e, i_=i_, fc=fc, trb_ap=trb_ap: e.transpose(trb_ap[:, fc * 128:(fc + 1) * 128], i_.ap, idn_b.h[:]),
                             R=[i_, idn_b.v()], W=[trb])
                    md = mergedT.v(slice(None), slice(4, 8), slice(sub * 128, sub * 128 + 128))
                    P.op("act", lambda e, md=md, trb_ap=trb_ap: e.activation(out=md.ap, in_=trb_ap.rearrange("p (f t) -> p f t", f=4), func=AF.Copy),
                         R=[trb], W=[md])

                steps = [(c, hd, kc) for c in range(4) for hd in range(2) for kc in range(NKC)]
                LA = 2

                def emit_qk(i):
                    c, hd, kc = steps[i]
                    sb = bank(i % 3)
                    k_ = KT.v(slice(hd * 64, hd * 64 + 64), slice(kc * 128, kc * 128 + 128))
                    q_ = QT.v(slice(hd * 64, hd * 64 + 64), c, slice(None))
                    P.op("pe", lambda e: e.matmul(sb.ap, k_.ap, q_.ap, start=True, stop=True), R=[k_, q_], W=[sb])
                    pt = PT[i % NPT].v()
                    P.op("act", lambda e: e.activation(out=pt.ap, in_=sb.ap, func=AF.Exp, scale=0.125), R=[sb], W=[pt])

                def emit_pv(i):
                    c, hd, kc = steps[i]
                    ob = bank(3 + hd)
                    pt = PT[i % NPT].v()
                    v_ = Vx.v(slice(None), kc, slice(hd * 64, hd * 64 + 128))
                    P.op("pe", lambda e: e.matmul(ob.ap, v_.ap, pt.ap, start=(kc == 0), stop=(kc == NKC - 1)), R=[v_, pt], W=[ob])
                    if kc == NKC - 1 and hd == 1:
                        tail(c)

                def tail(c):
                    bA, bB = bank(3), bank(4)
                    aA = aoT.v(slice(0, 64), c, slice(None)); aB = aoT.v(slice(64, 128), c, slice(None))
                    P.op("act", lambda e: e.activation(out=aA.ap, in_=bA.ap[0:64, :], func=AF.Copy), R=[bA], W=[aA])
                    P.op("act", lambda e: e.activation(out=aB.ap, in_=bB.ap[64:128, :], func=AF.Copy), R=[bB], W=[aB])
                    r_ = rD.v()
                    P.op("dve", lambda e: e.reciprocal(out=r_.ap[64:128, :], in_=bA.ap[64:128, :]), R=[bA], W=[r_])
                    P.op("dve", lambda e: e.reciprocal(out=r_.ap[0:64, :], in_=bB.ap[0:64, :]), R=[bB], W=[r_])
                    hi, lo = rhi.v(), rlo.v()
                    P.op("dve", lambda e: e.tensor_copy(out=hi.ap, in_=r_.ap), R=[r_], W=[hi])
                    P.op("dve", lambda e: e.tensor_tensor(out=lo.ap, in0=r_.ap, in1=hi.ap, op=ALU.subtract), R=[r_, hi], W=[lo])
                    bc = bank(5)
                    P.op("pe", lambda e: e.matmul(bc.ap, perm_b.h[:], hi.ap, start=True, stop=False), R=[perm_b.v(), hi], W=[bc])
                    P.op("pe", lambda e: e.matmul(bc.ap, perm_b.h[:], lo.ap, start=False, stop=True), R=[perm_b.v(), lo], W=[bc])
                    ac = aoT.v(slice(None), c, slice(None))
                    P.op("dve", lambda e: e.tensor_tensor(out=ac.ap, in0=ac.ap, in1=bc.ap, op=ALU.mult), R=[ac, bc], W=[ac])
                    sq_ = sqa.v(slice(None), c, slice(None))
                    P.op("act", lambda e: e.activation(out=sq_.ap, in_=ac.ap, func=AF.Square), R=[ac], W=[sq_])

                n = len(steps)
                for i in range(min(LA, n)):
                    emit_qk(i)
                for i in range(n):
                    if i + LA < n:
                        emit_qk(i + LA)
                    emit_pv(i)
                nb = bank(6)
                for c in range(4):
                    sq_ = sqa.v(slice(None), c, slice(None))
                    P.op("pe", lambda e, sq_=sq_, c=c: e.matmul(nb.ap, ones_b.h[:], sq_.ap, start=(c == 0), stop=(c == 3)), R=[ones_b.v(), sq_], W=[nb])
                rv = rstd.v()
                P.op("act", lambda e: e.activation(out=rv.ap, in_=nb.ap, func=AF.Sqrt, scale=1.0 / 512, bias=EPS), R=[nb], W=[rv])
                P.op("dve", lambda e: e.reciprocal(out=rv.ap, in_=rv.ap), R=[rv], W=[rv])
                for c in range(4):
                    ac = aoT.v(slice(None), c, slice(None))
                    md = mergedT.v(slice(None), c, slice(None))
                    g = pcol(l, 20 + c)
                    P.op("dve", lambda e, ac=ac, md=md, g=g: e.scalar_tensor_tensor(
                        out=md.ap, in0=ac.ap, scalar=g.ap, in1=rv.ap, op0=ALU.mult, op1=ALU.mult), R=[ac, g, rv], W=[md])
                for uo in range(4):
                    wo, _ = wload(l, f"wo{uo}", "B")
                    for dd in range(2):
                        dm = uo * 2 + dd
                        ob = bank(6 + (dm % 2))
                        for ck in range(NDK):
                            w_ = wo.v(slice(None), slice(ck * 256 + dd * 128, ck * 256 + dd * 128 + 128))
                            m_ = mergedT.v(slice(None), ck, slice(None))
                            P.op("pe", lambda e, ob=ob, w_=w_, m_=m_, ck=ck: e.matmul(ob.ap, w_.ap, m_.ap, start=(ck == 0), stop=(ck == NDK - 1)),
                                 R=[w_, m_], W=[ob])
                        xv = xT.v(slice(None), dm, slice(cols[0], cols[1]))
                        P.op("dve", lambda e, xv=xv, ob=ob: e.tensor_tensor(out=xv.ap, in0=ob.ap, in1=xv.ap, op=ALU.add), R=[ob, xv], W=[xv])

            def ffn_norm(i):
                o0, o1 = FT[i]
                lo_, hi_ = o0 - 1, o1 + 1
                vlo, vhi = max(lo_, 0), min(hi_, S)
                hb = hnC[i % 2]
                wi = hi_ - lo_
                if vlo > lo_:
                    z = hb.v(slice(None), slice(None), slice(0, 1))
                    P.op("dve", lambda e: e.memset(z.ap, 0.0), W=[z])
                if vhi < hi_:
                    z2 = hb.v(slice(None), slice(None), slice(wi - 1, wi))
                    P.op("dve", lambda e: e.memset(z2.ap, 0.0), W=[z2])
                rms_front(l, 8, (vlo, vhi), hb, vlo - lo_, vhi - vlo, sqC, rstdC, 0)

            ffn_norm(0)
            for i in range(len(FT)):
                o0, o1 = FT[i]
                wo_ = o1 - o0
                wi = wo_ + 2
                hb = hnC[i % 2]
                for jc in range(NCJ):
                    wt, _ = wload(l, f"up{jc}", "C")
                    G, U = bank(1 + (jc % 2)), bank(3 + (jc % 2))
                    Gw = bank(1 + (jc % 2), c1=wi); Uw = bank(3 + (jc % 2), c1=wi)
                    for half, bb in ((0, Gw), (1, Uw)):
                        for dk in range(NDK):
                            w_ = wt.v(slice(None), slice(dk * 256 + half * 128, dk * 256 + half * 128 + 128))
                            h_ = hb.v(slice(None), dk, slice(0, wi))
                            P.op("pe", lambda e, bb=bb, w_=w_, h_=h_, dk=dk: e.matmul(bb.ap, w_.ap, h_.ap, start=(dk == 0), stop=(dk == NDK - 1)),
                                 R=[w_, h_], W=[bb])
                    ag, au, th = [t.v(slice(None), slice(0, wo_)) for t in ctmp[jc % 2]]
                    for (bb, acc, pj) in ((Gw, ag, jc), (Uw, au, NCJ + jc)):
                        cw = [pcol(l, 32 + pj * 4 + k) for k in range(4)]
                        P.op("act", lambda e, bb=bb, acc=acc, cw=cw, wo_=wo_: e.activation(
                            out=acc.ap, in_=bb.ap[:, 1:1 + wo_], func=AF.Identity, scale=cw[1].ap, bias=cw[3].ap), R=[bb, cw[1], cw[3]], W=[acc])
                        P.op("dve", lambda e, bb=bb, acc=acc, cw=cw, wo_=wo_: e.scalar_tensor_tensor(
                            out=acc.ap, in0=bb.ap[:, 0:wo_], scalar=cw[0].ap, in1=acc.ap, op0=ALU.mult, op1=ALU.add), R=[bb, cw[0], acc], W=[acc])
                        P.op("dve", lambda e, bb=bb, acc=acc, cw=cw, wo_=wo_: e.scalar_tensor_tensor(
                            out=acc.ap, in0=bb.ap[:, 2:2 + wo_], scalar=cw[2].ap, in1=acc.ap, op0=ALU.mult, op1=ALU.add), R=[bb, cw[2], acc], W=[acc])
                    P.op("act", lambda e, th=th, ag=ag: e.activation(out=th.ap, in_=ag.ap, func=AF.Tanh, scale=0.5), R=[ag], W=[th])
                    P.op("dve", lambda e, th=th, ag=ag: e.scalar_tensor_tensor(
                        out=th.ap, in0=th.ap, scalar=1.0, in1=ag.ap, op0=ALU.add, op1=ALU.mult), R=[th, ag], W=[th])
                    gd = gT.v(slice(None), jc, slice(0, wo_))
                    P.op("dve", lambda e, gd=gd, th=th, au=au: e.tensor_tensor(out=gd.ap, in0=th.ap, in1=au.ap, op=ALU.mult), R=[th, au], W=[gd])
                if i + 1 < len(FT):
                    ffn_norm(i + 1)
                for dm in range(NDK):
                    ob = bank(5 + (dm % 2), c1=wo_)
                    for hf in range(2):
                        wd, _ = wload(l, f"dn{dm * 2 + hf}", "C")
                        for cjj in range(11):
                            cj = hf * 11 + cjj
                            w_ = wd.v(slice(None), slice(cjj * 128, cjj * 128 + 128))
                            g_ = gT.v(slice(None), cj, slice(0, wo_))
                            P.op("pe", lambda e, ob=ob, w_=w_, g_=g_, cj=cj: e.matmul(ob.ap, w_.ap, g_.ap, start=(cj == 0), stop=(cj == NCJ - 1)),
                                 R=[w_, g_], W=[ob])
                    xv = xT.v(slice(None), dm, slice(o0, o1))
                    P.op("dve", lambda e, xv=xv, ob=ob: e.scalar_tensor_tensor(
                        out=xv.ap, in0=ob.ap, scalar=0.5, in1=xv.ap, op0=ALU.mult, op1=ALU.add), R=[ob, xv], W=[xv])

        P.dma("pool", s_out,
              [lambda e, dk=dk, s=s: e.dma_start(out=oT_d[s, dk * 128:(dk + 1) * 128, :], in_=xT.h[:, dk, :]) for dk in range(NDK)],
              R=[xT.v()])
    P.final_wait("pool", s_out)

    return nc, P


def _emit(nc, P):
    vals = {}
    for n, q in P.q.items():
        c = 0
        vv = []
        for ins in q.instrs:
            if ins[2]:
                c += 1
            vv.append(c)
        vals[n] = vv

    def run(n, e):
        q = P.q[n]
        waited = {}
        for ins in q.instrs:
            fn, waits, sig, dsem = ins
            for w in waits:
                if w[0] == "e":
                    semh = P.q[w[1]].sem; v = vals[w[1]][w[2]]; k = w[1]
                else:
                    semh = w[1].h; v = w[2]; k = id(w[1])
                if waited.get(k, 0) >= v:
                    continue
                waited[k] = v
                e.wait_ge(semh, v)
            if fn is None:
                continue
            i = fn(e)
            if dsem is not None:
                i.then_inc(dsem.h, 16)
            elif sig:
                i.then_inc(q.sem, 1)

    with nc.Block() as block:
        @block.tensor
        def _pe(e):
            run("pe", e)

        @block.scalar
        def _act(e):
            run("act", e)

        @block.vector
        def _dve(e):
            run("dve", e)

        @block.gpsimd
        def _pool(e):
            run("pool", e)

        @block.sync
        def _sp(e):
            run("sp", e)


def _partner():
    i = np.arange(64)
    return np.where((i % 32) < 16, i + 16, i - 16)


def _rope_tables(S):
    rows = S // 64
    row = np.repeat(np.arange(rows, dtype=np.float32), 64)
    col = np.tile(np.arange(64, dtype=np.float32), rows)
    inv = (1.0 / (np.float32(10000.0) ** (np.arange(0, 32, 2, dtype=np.float32) / np.float32(32)))).astype(np.float32)
    C = np.zeros((64, S), np.float32); Sg = np.zeros((64, S), np.float32)
    for i in range(64):
        pos = row if i < 32 else col
        ang = (pos * inv[i % 16]).astype(np.float32)
        C[i] = np.cos(ang)
        sgn = -1.0 if (i % 32) < 16 else 1.0
        Sg[i] = sgn * np.sin(ang)
    return np.concatenate([C, C], 0), np.concatenate([Sg, Sg], 0)


def _unit8(Wcols):
    n = Wcols.shape[1]
    return Wcols.reshape(8, 128, n).transpose(1, 0, 2)


def prep_layer(l, inp):
    w_in = inp["w_in"][l]; w_o = inp["w_o"][l]; w_up = inp["w_up"][l]; w_dn = inp["w_down"][l]
    sg_w = inp["sg_w"][l]
    pt = _partner()
    flat = np.empty(LTOT, np.float32)

    def put(name, arr):
        a = np.ascontiguousarray(arr, dtype=np.float32).reshape(-1)
        flat[LOFF[name]:LOFF[name] + a.size] = a

    kc = np.arange(512, 640)
    ks = 512 + np.concatenate([pt, 64 + pt])
    put("kk", _unit8(np.concatenate([w_in[:, kc], w_in[:, ks]], 1)))
    put("v", _unit8(w_in[:, 640:768]))
    for c in range(4):
        qc = np.concatenate([c * 64 + np.arange(64), (c + 4) * 64 + np.arange(64)])
        qs = np.concatenate([c * 64 + pt, (c + 4) * 64 + pt])
        put(f"q{c}", _unit8(np.concatenate([w_in[:, qc], w_in[:, qs]], 1)))
    for nm, c0 in (("vv", 1280), ("u", 768)):
        Wv = w_in[:, c0:c0 + 512].reshape(8, 128, 512)
        for h in range(2):
            put(f"{nm}{h}", Wv[h * 4:(h + 1) * 4].transpose(1, 0, 2))
    put("sgw", sg_w.transpose(2, 0, 1))
    rowmap = np.concatenate([np.concatenate([c * 64 + np.arange(64), (c + 4) * 64 + np.arange(64)]) for c in range(4)]
                            + [512 + np.arange(512)])
    Wop = w_o[rowmap, :]
    for uo in range(4):
        put(f"wo{uo}", _unit8(Wop[:, uo * 256:(uo + 1) * 256]))
    for j in range(NCJ):
        put(f"up{j}", _unit8(np.concatenate([w_up[:, j * 128:(j + 1) * 128], w_up[:, DFF + j * 128:DFF + (j + 1) * 128]], 1)))
    Wd = w_dn.reshape(2, 11, 128, 8, 128)
    for dm in range(8):
        for hf in range(2):
            put(f"dn{dm * 2 + hf}", Wd[hf, :, :, dm, :].transpose(1, 0, 2))
    return flat.reshape(LTOT // 4096, 4096), rowmap


def prep_params(inp, L):
    par = np.zeros((128, L, NPAR), np.float32)
    pt = _partner()
    p = np.arange(128)
    for l in range(L):
        par[:, l, 0:8] = inp["attn_norm_g"][l].reshape(8, 128).T
        par[:, l, 8:16] = inp["ffn_norm_g"][l].reshape(8, 128).T
        gq = inp["q_norm_g"][l]; gk = inp["k_norm_g"][l]
        par[:, l, 16] = gq[p % 64]; par[:, l, 17] = gq[pt[p % 64]]
        par[:, l, 18] = gk[p % 64]; par[:, l, 19] = gk[pt[p % 64]]
        ag = inp["attn_out_g"][l]
        for c in range(4):
            par[0:64, l, 20 + c] = ag[c * 64:(c + 1) * 64]
            par[64:128, l, 20 + c] = ag[(c + 4) * 64:(c + 5) * 64]
        par[:, l, 24:32] = inp["sg_b"][l].T
        cw = inp["conv_w"][l]; cb = inp["conv_b"][l]
        cp = np.stack([cw[0], cw[1], cw[2], cb], -1).reshape(44, 128, 4).transpose(1, 0, 2)
        par[:, l, 32:208] = cp.reshape(128, 176)
    bcg = np.stack([np.stack([inp["sg_norm_g"][l].reshape(512), inp["sg_out_g"][l]]) for l in range(L)]).astype(np.float32)
    return par.reshape(128, L * NPAR), bcg


def consts():
    perm = np.zeros((128, 128), np.float32)
    m = np.arange(128)
    perm[(m + 64) % 128, m] = 1.0
    blk = np.zeros((128, 128), np.float32)
    blk[0:64, 0:64] = 1.0; blk[64:128, 64:128] = 1.0
    return np.stack([perm, blk, np.eye(128, dtype=np.float32)])


_CACHE = {}


def run(inputs, n_cores, L=None):
    x = np.asarray(inputs["x"])
    B, S, _ = x.shape
    if L is None:
        L = inputs["w_in"].shape[0]
    NSEQ = B // n_cores
    key = (S, NSEQ, L)
    if key not in _CACHE:
        nc, P = build_program(S, NSEQ, L)
        _emit(nc, P)
        _CACHE[key] = nc
    nc = _CACHE[key]
    inp = {k: np.asarray(v) for k, v in inputs.items()}
    wfs = [prep_layer(l, inp)[0] for l in range(L)]
    par, bcg = prep_params(inp, L)
    C, Sg = _rope_tables(S)
    cst = consts()
    xT = np.ascontiguousarray(x.transpose(0, 2, 1))
    in_maps = []
    for c in range(n_cores):
        m = {"xT": xT[c * NSEQ:(c + 1) * NSEQ], "par": par, "bcg": bcg, "ropeC": C, "ropeS": Sg, "cst": cst}
        for l in range(L):
            m[f"wf{l}"] = wfs[l]
        in_maps.append(m)
    res = run_bass_kernel_spmd(nc, in_maps, core_ids=list(range(n_cores)))
    oT = np.concatenate([r["oT"] for r in res.results], 0)
    return np.ascontiguousarray(oT.transpose(0, 2, 1))


def kernel(**inputs):
    return run(inputs, 8)
```

```python
import numpy as np
import concourse.bass as bass
import concourse.mybir as mybir
from concourse.bass_utils import run_bass_kernel_spmd

F32 = mybir.dt.float32
BF16 = mybir.dt.bfloat16
AF = mybir.ActivationFunctionType
ALU = mybir.AluOpType
AX = mybir.AxisListType

D = 1024
NDK = 8
DFF = 2816
NCJ = 22
EPS = 1e-6
NPAR = 208
PAGE = 512
GELU_C = 0.7978845608028654
DUMMY_MM = 0

U4 = 128 * 8 * 256
UV = 128 * 8 * 128
UDN = 128 * 11 * 128


def layer_layout():
    off = {}
    o = 0
    off["kk"] = o; o += U4
    off["v"] = o; o += UV
    a_end = o
    for c in range(4):
        off[f"q{c}"] = o; o += U4
    for h in range(2):
        off[f"vv{h}"] = o; o += U4
    off["sgw"] = o; o += UV
    for h in range(2):
        off[f"u{h}"] = o; o += U4
    for h in range(4):
        off[f"wo{h}"] = o; o += U4
    b_end = o
    for j in range(NCJ):
        off[f"up{j}"] = o; o += U4
    for j in range(16):
        off[f"dn{j}"] = o; o += UDN
    c_end = o
    return off, (0, a_end, b_end, c_end)


LOFF, LPARTS = layer_layout()
LTOT = LPARTS[3]


def ffn_tiles(S):
    n = -(-S // 510)
    base = -(-S // n)
    tiles = []
    o = 0
    while o < S:
        w = min(base, S - o)
        tiles.append((o, o + w))
        o += w
    return tiles


class Tracker:
    def __init__(self):
        self.pages = {}

    def _page(self, i):
        p = self.pages.get(i)
        if p is None:
            p = ({}, {})
            self.pages[i] = p
        return p

    def read(self, ranges, key, ref, deps):
        for lo, hi in ranges:
            for i in range(lo // PAGE, (hi - 1) // PAGE + 1):
                w, r = self._page(i)
                for v in w.values():
                    deps.append((v, True))
                r[key] = ref

    def write(self, ranges, key, ref, deps):
        for lo, hi in ranges:
            for i in range(lo // PAGE, (hi - 1) // PAGE + 1):
                w, r = self._page(i)
                for v in w.values():
                    deps.append((v, False))
                for v in r.values():
                    deps.append((v, False))
                if lo <= i * PAGE and hi >= (i + 1) * PAGE:
                    w.clear(); r.clear()
                w[key] = ref


class View:
    __slots__ = ("ap", "trk", "ranges")

    def __init__(self, ap, trk, ranges):
        self.ap = ap; self.trk = trk; self.ranges = ranges

    def map(self, f):
        return View(f(self.ap), self.trk, self.ranges)


class Tens:
    def __init__(self, handle, shape, esize, off, trk):
        self.h = handle; self.shape = list(shape); self.esize = esize; self.off = off; self.trk = trk
        self.fstr = []
        s = 1
        for n in reversed(self.shape[1:]):
            self.fstr.insert(0, s); s *= n
        self.fsize = s

    def v(self, *key):
        key = list(key)
        while len(key) < len(self.shape):
            key.append(slice(None))
        ap = self.h[tuple(key)]
        fk = key[1:]
        spans = []
        for k, n in zip(fk, self.shape[1:]):
            if isinstance(k, slice):
                a = 0 if k.start is None else k.start
                b = n if k.stop is None else k.stop
            else:
                a, b = k, k + 1
            assert 0 <= a < b <= n, (key, self.shape)
            spans.append((a, b))
        ranges = []

        def rec(d, base):
            if d == len(spans):
                ranges.append((base, base + 1)); return
            a, b = spans[d]
            full_tail = all(spans[j] == (0, self.shape[1 + j]) for j in range(d + 1, len(spans)))
            if full_tail:
                ranges.append((base + a * self.fstr[d], base + b * self.fstr[d]))
            else:
                for i in range(a, b):
                    rec(d + 1, base + i * self.fstr[d])
        rec(0, 0)
        br = [(self.off + lo * self.esize, self.off + hi * self.esize) for lo, hi in ranges]
        return View(ap, self.trk, br)


class Sem:
    def __init__(self, h):
        self.h = h; self.count = 0


class EngQ:
    def __init__(self, name):
        self.name = name
        self.instrs = []
        self.sem = None


class Prog:
    def __init__(self, nc):
        self.nc = nc
        self.q = {n: EngQ(n) for n in ("pe", "act", "dve", "pool", "sp")}
        self.sb = Tracker(); self.ps = Tracker()
        self.dram = {}
        self.sems = []

    def dram_view(self, ap, name):
        if name not in self.dram:
            self.dram[name] = Tracker()
        return View(ap, self.dram[name], [(0, PAGE)])

    def _deps(self, eng, R, W, key, ref):
        deps = []
        for v in R:
            v.trk.read(v.ranges, key, ref, deps)
        for v in W:
            v.trk.write(v.ranges, key, ref, deps)
        out = {}
        self_idx = ref[2] if ref[0] == "e" else -1
        for (d, is_raw) in deps:
            if d[0] == "e":
                if d[1] == eng:
                    if eng == "pe":
                        continue
                    if d[2] == self_idx:
                        continue
                k = ("e", d[1])
                if k not in out or out[k] < d[2]:
                    out[k] = d[2]
            else:
                k = ("d", id(d[1]))
                if k not in out or out[k][1] < d[2]:
                    out[k] = (d[1], d[2])
        return out

    def op(self, eng, fn, R=(), W=()):
        q = self.q[eng]
        idx = len(q.instrs)
        ref = ("e", eng, idx)
        deps = self._deps(eng, R, W, eng, ref)
        waits = []
        for k, v in deps.items():
            if k[0] == "e":
                self.q[k[1]].instrs[v][2] = True
                waits.append(("e", k[1], v))
            else:
                waits.append(("d", v[0], v[1]))
        q.instrs.append([fn, waits, False, None])
        return ref

    def dma(self, eng, sem, fns, R=(), W=()):
        q = self.q[eng]
        val = sem.count + 16 * len(fns)
        ref = ("d", sem, val)
        deps = self._deps(eng, R, W, ("d", id(sem)), ref)
        waits = []
        for k, v in deps.items():
            if k[0] == "e":
                self.q[k[1]].instrs[v][2] = True
                waits.append(("e", k[1], v))
            else:
                if v[0] is sem and v[1] >= val:
                    continue
                waits.append(("d", v[0], v[1]))
        for i, fn in enumerate(fns):
            q.instrs.append([fn, waits if i == 0 else [], False, sem])
        sem.count = val
        return ref

    def final_wait(self, eng, sem):
        self.q[eng].instrs.append([None, [("d", sem, sem.count)], False, None])

    def replay(self, engines):
        vals = {}
        for n, q in self.q.items():
            c = 0
            vv = []
            for ins in q.instrs:
                if ins[2]:
                    c += 1
                vv.append(c)
            vals[n] = vv
        for n, q in self.q.items():
            e = engines[n]
            waited = {}
            for ins in q.instrs:
                fn, waits, sig, dsem = ins
                for w in waits:
                    if w[0] == "e":
                        semh = self.q[w[1]].sem; v = vals[w[1]][w[2]]; k = w[1]
                    else:
                        semh = w[1].h; v = w[2]; k = id(w[1])
                    if waited.get(k, 0) >= v:
                        continue
                    waited[k] = v
                    e.wait_ge(semh, v)
                if fn is None:
                    continue
                i = fn(e)
                if dsem is not None:
                    i.then_inc(dsem.h, 16)
                elif sig:
                    i.then_inc(q.sem, 1)


def build_program(S, NSEQ, L):
    nc = bass.Bass("TRN2", target_bir_lowering=False)
    P = Prog(nc)
    NQT = S // 512
    NKC = S // 128
    FT = ffn_tiles(S)
    WOM = max(b - a for a, b in FT)
    WIM = WOM + 2

    xT_d = nc.dram_tensor("xT", [NSEQ, D, S], F32, kind="ExternalInput").ap()
    oT_d = nc.dram_tensor("oT", [NSEQ, D, S], F32, kind="ExternalOutput").ap()
    wf_d = [nc.dram_tensor(f"wf{l}", [LTOT // 4096, 4096], F32, kind="ExternalInput").ap() for l in range(L)]
    wb_d = [nc.dram_tensor(f"wb{l}", [LTOT // 4096, 4096], BF16, kind="Internal").ap() for l in range(L)]
    par_d = nc.dram_tensor("par", [128, L * NPAR], F32, kind="ExternalInput").ap()
    bcg_d = nc.dram_tensor("bcg", [L, 2, 512], F32, kind="ExternalInput").ap()
    rc_d = nc.dram_tensor("ropeC", [128, S], F32, kind="ExternalInput").ap()
    rs_d = nc.dram_tensor("ropeS", [128, S], F32, kind="ExternalInput").ap()
    cst_d = nc.dram_tensor("cst", [3, 128, 128], F32, kind="ExternalInput").ap()

    base = (nc.sbuf_base + PAGE - 1) // PAGE * PAGE
    top = nc.sbuf_top
    cur = [base]

    def alloc(name, shape, dt, at=None):
        es = 4 if dt == F32 else 2
        n = 1
        for s in shape[1:]:
            n *= s
        nbytes = (n * es + PAGE - 1) // PAGE * PAGE
        if at is None:
            off = cur[0]; cur[0] += nbytes
        else:
            off = at
        assert off + nbytes <= top, (name, off, nbytes, top)
        h = nc.alloc_sbuf_tensor_at(name, list(shape), dt, offset=off)
        return Tens(h, shape, es, off, P.sb)

    xT = alloc("xT_sb", [128, NDK, S], F32)
    ones_b = alloc("ones_b", [128, 128], BF16)
    perm_b = Tens(nc.alloc_sbuf_tensor_at("perm_b", [128, 128], BF16, offset=ones_b.off + 256), [128, 128], 2, ones_b.off + 256, P.sb)
    blk_b = alloc("blk_b", [128, 128], BF16)
    idn_b = Tens(nc.alloc_sbuf_tensor_at("idn_b", [128, 128], BF16, offset=blk_b.off + 256), [128, 128], 2, blk_b.off + 256, P.sb)
    par = alloc("par_sb", [128, L, NPAR], F32)
    phase_base = cur[0]

    NRB = 3
    KT = alloc("KT", [128, S], BF16)
    Vx = alloc("Vx", [128, NKC, 192], BF16)
    ringB = [alloc(f"ringB{i}", [128, 2048], BF16) for i in range(NRB)]
    hnT = alloc("hnT", [128, NDK, 512], BF16)
    sqT = alloc("sqT", [128, NDK, 512], BF16)
    mergedT = Tens(sqT.h, [128, NDK, 512], 2, sqT.off, P.sb)
    QT = alloc("QT", [128, 4, 512], BF16)
    tabC = alloc("tabC", [128, 512], F32)
    tabS = alloc("tabS", [128, 512], F32)
    t1 = alloc("t1", [128, 512], F32)
    t2 = alloc("t2", [128, 512], F32)
    vvn = alloc("vvn", [128, 4, 512], BF16)
    t3 = alloc("t3", [128, 512], BF16)
    gsg_bc = alloc("gsg_bc", [128, 512], F32)
    gout_bc = alloc("gout_bc", [128, 512], F32)
    rstd = alloc("rstd", [128, 512], F32)
    sqq = Tens(nc.alloc_sbuf_tensor_at("sqq", [128, 512], BF16, offset=t3.off), [128, 512], 2, t3.off, P.sb)
    rq = Tens(nc.alloc_sbuf_tensor_at("rq", [128, 512], F32, offset=rstd.off), [128, 512], 4, rstd.off, P.sb)
    sm8 = alloc("sm8", [128, 8], F32)
    sm1 = alloc("sm1", [128, 8], F32)
    endB = cur[0]
    aoT = Tens(nc.alloc_sbuf_tensor_at("aoT", [128, 4, 512], F32, offset=hnT.off), [128, 4, 512], 4, hnT.off, P.sb)
    NPT = 3
    PT = [Tens(nc.alloc_sbuf_tensor_at(f"PT{i}", [128, 512], BF16, offset=t1.off + 1024 * i), [128, 512], 2,
               t1.off + 1024 * i, P.sb) for i in range(NPT)]
    PT2 = [Tens(nc.alloc_sbuf_tensor_at(f"PT2_{i}", [128, 1024], BF16, offset=t1.off + 2048 * i), [128, 1024], 2,
                t1.off + 2048 * i, P.sb) for i in range(2)]
    assert t2.off == t1.off + 2048
    rD = Tens(nc.alloc_sbuf_tensor_at("rD", [128, 512], F32, offset=tabC.off), [128, 512], 4, tabC.off, P.sb)
    rhi = Tens(nc.alloc_sbuf_tensor_at("rhi", [128, 512], BF16, offset=tabS.off), [128, 512], 2, tabS.off, P.sb)
    rlo = Tens(nc.alloc_sbuf_tensor_at("rlo", [128, 512], BF16, offset=tabS.off + 1024), [128, 512], 2,
               tabS.off + 1024, P.sb)
    sqa = Tens(nc.alloc_sbuf_tensor_at("sqa", [128, 4, 512], BF16, offset=vvn.off), [128, 4, 512], 2, vvn.off, P.sb)

    cur[0] = phase_base
    NRC = 4
    ringC = [alloc(f"ringC{i}", [128, 2048], BF16) for i in range(NRC)]
    hnC = [alloc(f"hnC{i}", [128, NDK, WIM], BF16) for i in range(2)]
    sqC = alloc("sqC", [128, NDK, WIM], BF16)
    rstdC = alloc("rstdC", [128, WIM], F32)
    gT = alloc("gT", [128, NCJ, WOM], BF16)
    ctmp = [[alloc(f"c{n}{i}", [128, WOM], F32) for n in ("ag", "au", "th")] for i in range(2)]
    endC = cur[0]
    assert max(endB, endC) <= top, (endB, endC, top)

    psh = nc.alloc_psum_tensor("ps", [128, 4096], F32)
    ps = Tens(psh, [128, 4096], 4, 0, P.ps)

    def bank(b, p=slice(None), c0=0, c1=512):
        return ps.v(p, slice(512 * b + c0, 512 * b + c1))

    def sem(name):
        s = Sem(nc.alloc_semaphore(name))
        P.sems.append(s)
        return s

    for n in P.q:
        P.q[n].sem = nc.alloc_semaphore("prog_" + n)
    s_cst = sem("s_cst"); s_par = sem("s_par"); s_x = sem("s_x"); s_out = sem("s_out")
    s_tab = sem("s_tab"); s_bcg = sem("s_bcg")
    s_cast = [[sem(f"s_cast{l}_{p}") for p in range(3)] for l in range(L)]
    s_ringB = [sem(f"s_rb{i}") for i in range(NRB)]
    s_ringC = [sem(f"s_rc{i}") for i in range(NRC)]

    wbv = [[P.dram_view(wb_d[l], f"wb{l}_{p}") for p in range(3)] for l in range(L)]
    ring_state = {"B": [ringB, s_ringB, 0], "C": [ringC, s_ringC, 0]}

    def wload(l, name, which):
        ring, sems, i = ring_state[which]
        slot = i % len(ring)
        ring_state[which][2] = i + 1
        off = LOFF[name]
        part = 0 if off < LPARTS[1] else (1 if off < LPARTS[2] else 2)
        n = UV if name in ("v", "sgw") else (UDN if name.startswith("dn") else U4)
        F = n // 128
        src = wb_d[l].rearrange("r c -> (r c)")[off:off + n].rearrange("(p f) -> p f", p=128)
        dst = ring[slot].v(slice(None), slice(0, F))
        P.dma("sp", sems[slot], [lambda e, d=dst.ap, s=src: e.dma_start(out=d, in_=s)],
              R=[wbv[l][part]], W=[dst])
        return ring[slot], F

    P.dma("pool", s_cst,
          [lambda e: e.dma_start(out=perm_b.h[:], in_=cst_d[0]),
           lambda e: e.dma_start(out=blk_b.h[:], in_=cst_d[1]),
           lambda e: e.dma_start(out=idn_b.h[:], in_=cst_d[2])],
          W=[perm_b.v(), blk_b.v(), idn_b.v()])
    P.dma("pool", s_par, [lambda e: e.dma_start(out=par.h[:].rearrange("p l n -> p (l n)"), in_=par_d)], W=[par.v()])
    P.op("dve", lambda e: e.memset(ones_b.h[:], 1.0), W=[ones_b.v()])

    def cast(l, p):
        r0, r1 = LPARTS[p] // 4096, LPARTS[p + 1] // 4096
        P.dma("pool", s_cast[l][p], [lambda e: e.dma_start(out=wb_d[l][r0:r1, :], in_=wf_d[l][r0:r1, :])],
              W=[wbv[l][p]])

    def xload(s):
        P.dma("pool", s_x,
              [lambda e, dk=dk: e.dma_start(out=xT.h[:, dk, :], in_=xT_d[s, dk * 128:(dk + 1) * 128, :])
               for dk in range(NDK)], W=[xT.v()])

    xload(0)
    cast(0, 0)
    cast(0, 1)
    cast(0, 2)

    def pcol(l, i):
        return par.v(slice(None), l, slice(i, i + 1))

    def rms_front(l, gofs, xcols, dst, dst_c0, width, sq, rs, nbank):
        c0, c1 = xcols
        xs = xT.v(slice(None), slice(None), slice(c0, c1))
        sqv = sq.v(slice(None), slice(None), slice(0, width))
        P.op("act", lambda e: e.activation(out=sqv.ap, in_=xs.ap, func=AF.Square), R=[xs], W=[sqv])
        nb = bank(nbank, c1=width)
        for dk in range(NDK):
            s_ = sq.v(slice(None), dk, slice(0, width))
            P.op("pe", lambda e, s_=s_, dk=dk: e.matmul(nb.ap, ones_b.h[:], s_.ap, start=(dk == 0), stop=(dk == NDK - 1)),
                 R=[ones_b.v(), s_], W=[nb])
        rv = rs.v(slice(None), slice(0, width))
        P.op("act", lambda e: e.activation(out=rv.ap, in_=nb.ap, func=AF.Sqrt, scale=1.0 / D, bias=EPS), R=[nb], W=[rv])
        P.op("dve", lambda e: e.reciprocal(out=rv.ap, in_=rv.ap), R=[rv], W=[rv])
        for dk in range(NDK):
            xv = xT.v(slice(None), dk, slice(c0, c1))
            dv = dst.v(slice(None), dk, slice(dst_c0, dst_c0 + width))
            g = pcol(l, gofs + dk)
            P.op("dve", lambda e, xv=xv, dv=dv, g=g: e.scalar_tensor_tensor(
                out=dv.ap, in0=xv.ap, scalar=g.ap, in1=rv.ap, op0=ALU.mult, op1=ALU.mult), R=[xv, g, rv], W=[dv])

    def load_tables(j):
        P.dma("pool", s_tab,
              [lambda e: e.dma_start(out=tabC.h[:], in_=rc_d[:, j * 512:(j + 1) * 512]),
               lambda e: e.dma_start(out=tabS.h[:], in_=rs_d[:, j * 512:(j + 1) * 512])],
              W=[tabC.v(), tabS.v()])

    def proj_rope(l, wt, gcol, dstv):
        bq, bs, bn = bank(1), bank(2), bank(0)
        for half, bb in ((0, bq), (1, bs)):
            for dk in range(NDK):
                w_ = wt.v(slice(None), slice(dk * 256 + half * 128, dk * 256 + half * 128 + 128))
                h_ = hnT.v(slice(None), dk, slice(None))
                P.op("pe", lambda e, w_=w_, h_=h_, bb=bb, dk=dk: e.matmul(bb.ap, w_.ap, h_.ap, start=(dk == 0), stop=(dk == NDK - 1)),
                     R=[w_, h_], W=[bb])
        sv = sqq.v()
        P.op("act", lambda e: e.activation(out=sv.ap, in_=bq.ap, func=AF.Square), R=[bq], W=[sv])
        P.op("pe", lambda e: e.matmul(bn.ap, blk_b.h[:], sv.ap, start=True, stop=True), R=[blk_b.v(), sv], W=[bn])
        rv = rq.v()
        P.op("act", lambda e: e.activation(out=rv.ap, in_=bn.ap, func=AF.Sqrt, scale=1.0 / 64, bias=EPS), R=[bn], W=[rv])
        P.op("dve", lambda e: e.reciprocal(out=rv.ap, in_=rv.ap), R=[rv], W=[rv])
        a, b = t1.v(), t2.v()
        g0, g1 = pcol(l, gcol), pcol(l, gcol + 1)
        tc_, ts_ = tabC.v(), tabS.v()
        P.op("dve", lambda e: e.scalar_tensor_tensor(out=a.ap, in0=bq.ap, scalar=g0.ap, in1=tc_.ap, op0=ALU.mult, op1=ALU.mult),
             R=[bq, g0, tc_], W=[a])
        P.op("dve", lambda e: e.scalar_tensor_tensor(out=b.ap, in0=bs.ap, scalar=g1.ap, in1=ts_.ap, op0=ALU.mult, op1=ALU.mult),
             R=[bs, g1, ts_], W=[b])
        P.op("dve", lambda e: e.tensor_tensor(out=a.ap, in0=a.ap, in1=b.ap, op=ALU.add), R=[a, b], W=[a])
        P.op("dve", lambda e: e.tensor_tensor(out=dstv.ap, in0=a.ap, in1=rv.ap, op=ALU.mult), R=[a, rv], W=[dstv])

    def gelu2(src, dst):
        P.op("act", lambda e: e.activation(out=dst.ap, in_=src.ap, func=AF.Gelu_apprx_tanh), R=[src], W=[dst])

    for s in range(NSEQ):
        if s > 0:
            xload(s)
        for l in range(L):
            P.dma("pool", s_bcg,
                  [lambda e, l=l: e.dma_start(out=gsg_bc.h[:], in_=bcg_d[l, 0:1, :].partition_broadcast(128)),
                   lambda e, l=l: e.dma_start(out=gout_bc.h[:], in_=bcg_d[l, 1:2, :].partition_broadcast(128))],
                  W=[gsg_bc.v(), gout_bc.v()])
            if True:
                vo = Vx.v(slice(None), slice(None), slice(64, 128))
                P.op("dve", lambda e, vo=vo: e.memset(vo.ap, 1.0), W=[vo])

            for j in range(NQT):
                cols = (j * 512, (j + 1) * 512)
                rms_front(l, 0, cols, hnT, 0, 512, sqT, rstd, 0)
                load_tables(j)
                wt, _ = wload(l, "kk", "B")
                proj_rope(l, wt, 18, KT.v(slice(None), slice(cols[0], cols[1])))
                wv, _ = wload(l, "v", "B")
                bv = bank(3)
                for sub in range(4):
                    o_ = bank(3, c0=sub * 128, c1=sub * 128 + 128)
                    for dk in range(NDK):
                        h_ = hnT.v(slice(None), dk, slice(sub * 128, sub * 128 + 128))
                        w_ = wv.v(slice(None), slice(dk * 128, dk * 128 + 128))
                        P.op("pe", lambda e, o_=o_, h_=h_, w_=w_, dk=dk: e.matmul(o_.ap, h_.ap, w_.ap, start=(dk == 0), stop=(dk == NDK - 1)),
                             R=[h_, w_], W=[o_])
                bv3 = bv.map(lambda ap: ap.rearrange("p (s c) -> p s c", s=4))
                d0 = Vx.v(slice(None), slice(4 * j, 4 * j + 4), slice(0, 64))
                d1 = Vx.v(slice(None), slice(4 * j, 4 * j + 4), slice(128, 192))
                P.op("act", lambda e, d0=d0, bv3=bv3: e.activation(out=d0.ap, in_=bv3.ap[:, :, 0:64], func=AF.Copy), R=[bv], W=[d0])
                P.op("dve", lambda e, d1=d1, bv3=bv3: e.tensor_copy(out=d1.ap, in_=bv3.ap[:, :, 64:128]), R=[bv], W=[d1])

            if s == 0 and l + 1 < L:
                for p in range(3):
                    cast(l + 1, p)

            for j in range(NQT):
                cols = (j * 512, (j + 1) * 512)
                rms_front(l, 0, cols, hnT, 0, 512, sqT, rstd, 0)
                load_tables(j)
                for c in range(4):
                    wt, _ = wload(l, f"q{c}", "B")
                    proj_rope(l, wt, 16, QT.v(slice(None), c, slice(None)))
                wvv = [wload(l, f"vv{h}", "B")[0] for h in range(2)]
                for h in range(2):
                    for sub in range(4):
                        o_ = bank(sub)
                        for dkk in range(4):
                            dk = h * 4 + dkk
                            h_ = hnT.v(slice(None), dk, slice(sub * 128, sub * 128 + 128))
                            w_ = wvv[h].v(slice(None), slice(dkk * 512, dkk * 512 + 512))
                            P.op("pe", lambda e, o_=o_, h_=h_, w_=w_, dk=dk: e.matmul(o_.ap, h_.ap, w_.ap, start=(dk == 0), stop=(dk == NDK - 1)),
                                 R=[h_, w_], W=[o_])
                for sub in range(4):
                    src = bank(sub)
                    gv = t1.v()
                    gelu2(src, gv)
                    b2 = t2.v()
                    P.op("act", lambda e, b2=b2, gv=gv: e.activation(out=b2.ap, in_=gv.ap, func=AF.Square), R=[gv], W=[b2])
                    s8 = sm8.v()
                    P.op("dve", lambda e, b2=b2, s8=s8: e.tensor_reduce(out=s8.ap, in_=b2.ap.rearrange("p (h d) -> p h d", h=8), op=ALU.add, axis=AX.X),
                         R=[b2], W=[s8])
                    P.op("act", lambda e, s8=s8: e.activation(out=s8.ap, in_=s8.ap, func=AF.Sqrt, scale=1.0 / 64, bias=EPS), R=[s8], W=[s8])
                    P.op("dve", lambda e, s8=s8: e.reciprocal(out=s8.ap, in_=s8.ap), R=[s8], W=[s8])
                    P.op("dve", lambda e, gv=gv, s8=s8: e.tensor_tensor(
                        out=gv.ap.rearrange("p (h d) -> p h d", h=8), in0=gv.ap.rearrange("p (h d) -> p h d", h=8),
                        in1=s8.ap.unsqueeze(2).broadcast_to([128, 8, 64]), op=ALU.mult), R=[gv, s8], W=[gv])
                    vn = vvn.v(slice(None), sub, slice(None))
                    gb = gsg_bc.v()
                    P.op("dve", lambda e, vn=vn, gv=gv, gb=gb: e.tensor_tensor(out=vn.ap, in0=gv.ap, in1=gb.ap, op=ALU.mult), R=[gv, gb], W=[vn])
                wsg, _ = wload(l, "sgw", "B")
                for sub in range(4):
                    for hh in range(8):
                        o_ = bank(4 + sub, c0=hh * 64, c1=hh * 64 + 64)
                        w_ = wsg.v(slice(None), slice(hh * 128, hh * 128 + 128))
                        v_ = vvn.v(slice(None), sub, slice(hh * 64, hh * 64 + 64))
                        P.op("pe", lambda e, o_=o_, w_=w_, v_=v_: e.matmul(o_.ap, w_.ap, v_.ap, start=True, stop=True), R=[w_, v_], W=[o_])
                wu = [wload(l, f"u{h}", "B")[0] for h in range(2)]
                for h in range(2):
                    for sub in range(4):
                        o_ = bank(sub)
                        for dkk in range(4):
                            dk = h * 4 + dkk
                            h_ = hnT.v(slice(None), dk, slice(sub * 128, sub * 128 + 128))
                            w_ = wu[h].v(slice(None), slice(dkk * 512, dkk * 512 + 512))
                            P.op("pe", lambda e, o_=o_, h_=h_, w_=w_, dk=dk: e.matmul(o_.ap, h_.ap, w_.ap, start=(dk == 0), stop=(dk == NDK - 1)),
                                 R=[h_, w_], W=[o_])
                for sub in range(4):
                    src = bank(sub)
                    gu = t1.v()
                    gelu2(src, gu)
                    mx = bank(4 + sub)
                    b2 = t2.v()
                    sb_ = par.v(slice(None), l, slice(24, 32))
                    P.op("dve", lambda e, b2=b2, mx=mx, sb_=sb_: e.tensor_tensor(
                        out=b2.ap.rearrange("p (h d) -> p h d", h=8), in0=mx.ap.rearrange("p (h d) -> p h d", h=8),
                        in1=sb_.ap.unsqueeze(2).broadcast_to([128, 8, 64]), op=ALU.add), R=[mx, sb_], W=[b2])
                    P.op("dve", lambda e, gu=gu, b2=b2: e.tensor_tensor(out=gu.ap, in0=gu.ap, in1=b2.ap, op=ALU.mult), R=[gu, b2], W=[gu])
                    s1 = sm1.v(slice(None), slice(0, 1))
                    P.op("act", lambda e, b2=b2, gu=gu, s1=s1: e.activation(out=b2.ap, in_=gu.ap, func=AF.Square, accum_out=s1.ap), R=[gu], W=[b2, s1])
                    P.op("act", lambda e, s1=s1: e.activation(out=s1.ap, in_=s1.ap, func=AF.Sqrt, scale=1.0 / 512, bias=EPS), R=[s1], W=[s1])
                    P.op("dve", lambda e, s1=s1: e.reciprocal(out=s1.ap, in_=s1.ap), R=[s1], W=[s1])
                    sn = t3.v()
                    go = gout_bc.v()
                    P.op("dve", lambda e, sn=sn, gu=gu, s1=s1, go=go: e.scalar_tensor_tensor(
                        out=sn.ap, in0=gu.ap, scalar=s1.ap, in1=go.ap, op0=ALU.mult, op1=ALU.mult), R=[gu, s1, go], W=[sn])
                    trb = bank(4 + sub, c0=0, c1=256)
                    trb_ap = trb.ap.bitcast(BF16)
                    for fc in range(4):
                        i_ = t3.v(slice(None), slice(fc * 128, fc * 128 + 128))
                        P.op("pe", lambda e, i_=i_, fc=fc, trb_ap=trb_ap: e.transpose(trb_ap[:, fc * 128:(fc + 1) * 128], i_.ap, idn_b.h[:]),
                             R=[i_, idn_b.v()], W=[trb])
                    md = mergedT.v(slice(None), slice(4, 8), slice(sub * 128, sub * 128 + 128))
                    P.op("act", lambda e, md=md, trb_ap=trb_ap: e.activation(out=md.ap, in_=trb_ap.rearrange("p (f t) -> p f t", f=4), func=AF.Copy),
                         R=[trb], W=[md])

                steps = [(c, kc) for c in range(4) for kc in range(NKC)]

                def emit_qk(i):
                    c, kc = steps[i]
                    pb = i % 2
                    for hd in range(2):
                        sb = bank(2 * pb + hd)
                        k_ = KT.v(slice(hd * 64, hd * 64 + 64), slice(kc * 128, kc * 128 + 128))
                        q_ = QT.v(slice(hd * 64, hd * 64 + 64), c, slice(None))
                        P.op("pe", lambda e, sb=sb, k_=k_, q_=q_: e.matmul(sb.ap, k_.ap, q_.ap, start=True, stop=True), R=[k_, q_], W=[sb])
                    s2 = ps.v(slice(None), slice(1024 * pb, 1024 * pb + 1024))
                    pt = PT2[pb].v()
                    P.op("act", lambda e, s2=s2, pt=pt: e.activation(out=pt.ap, in_=s2.ap, func=AF.Exp, scale=0.125), R=[s2], W=[pt])

                def emit_pv(i):
                    c, kc = steps[i]
                    pb = i % 2
                    for hd in range(2):
                        ob = bank(4 + hd)
                        pt = PT2[pb].v(slice(None), slice(hd * 512, hd * 512 + 512))
                        v_ = Vx.v(slice(None), kc, slice(hd * 64, hd * 64 + 128))
                        P.op("pe", lambda e, ob=ob, pt=pt, v_=v_, kc=kc: e.matmul(ob.ap, v_.ap, pt.ap, start=(kc == 0), stop=(kc == NKC - 1)), R=[v_, pt], W=[ob])
                    for _ in range(DUMMY_MM):
                        d_ = bank(7)
                        kd = KT.v(slice(None), slice(0, 512))
                        P.op("pe", lambda e, d_=d_, kd=kd: e.matmul(d_.ap, ones_b.h[:], kd.ap, start=True, stop=True), R=[ones_b.v(), kd], W=[d_])
                    if kc == NKC - 1:
                        tail(c)

                def tail(c):
                    bA, bB = bank(4), bank(5)
                    aA = aoT.v(slice(0, 64), c, slice(None)); aB = aoT.v(slice(64, 128), c, slice(None))
                    P.op("act", lambda e: e.activation(out=aA.ap, in_=bA.ap[0:64, :], func=AF.Copy), R=[bA], W=[aA])
                    P.op("act", lambda e: e.activation(out=aB.ap, in_=bB.ap[64:128, :], func=AF.Copy), R=[bB], W=[aB])
                    r_ = rD.v()
                    P.op("dve", lambda e: e.reciprocal(out=r_.ap[64:128, :], in_=bA.ap[64:128, :]), R=[bA], W=[r_])
                    P.op("dve", lambda e: e.reciprocal(out=r_.ap[0:64, :], in_=bB.ap[0:64, :]), R=[bB], W=[r_])
                    hi, lo = rhi.v(), rlo.v()
                    P.op("dve", lambda e: e.tensor_copy(out=hi.ap, in_=r_.ap), R=[r_], W=[hi])
                    P.op("dve", lambda e: e.tensor_tensor(out=lo.ap, in0=r_.ap, in1=hi.ap, op=ALU.subtract), R=[r_, hi], W=[lo])
                    bc = bank(6)
                    P.op("pe", lambda e: e.matmul(bc.ap, perm_b.h[:], hi.ap, start=True, stop=False), R=[perm_b.v(), hi], W=[bc])
                    P.op("pe", lambda e: e.matmul(bc.ap, perm_b.h[:], lo.ap, start=False, stop=True), R=[perm_b.v(), lo], W=[bc])
                    ac = aoT.v(slice(None), c, slice(None))
                    P.op("dve", lambda e: e.tensor_tensor(out=ac.ap, in0=ac.ap, in1=bc.ap, op=ALU.mult), R=[ac, bc], W=[ac])
                    sq_ = sqa.v(slice(None), c, slice(None))
                    P.op("act", lambda e: e.activation(out=sq_.ap, in_=ac.ap, func=AF.Square), R=[ac], W=[sq_])

                n = len(steps)
                emit_qk(0)
                for i in range(n):
                    if i + 1 < n:
                        emit_qk(i + 1)
                    emit_pv(i)
                nb = bank(7)
                for c in range(4):
                    sq_ = sqa.v(slice(None), c, slice(None))
                    P.op("pe", lambda e, sq_=sq_, c=c: e.matmul(nb.ap, ones_b.h[:], sq_.ap, start=(c == 0), stop=(c == 3)), R=[ones_b.v(), sq_], W=[nb])
                rv = rstd.v()
                P.op("act", lambda e: e.activation(out=rv.ap, in_=nb.ap, func=AF.Sqrt, scale=1.0 / 512, bias=EPS), R=[nb], W=[rv])
                P.op("dve", lambda e: e.reciprocal(out=rv.ap, in_=rv.ap), R=[rv], W=[rv])
                for c in range(4):
                    ac = aoT.v(slice(None), c, slice(None))
                    md = mergedT.v(slice(None), c, slice(None))
                    g = pcol(l, 20 + c)
                    P.op("dve", lambda e, ac=ac, md=md, g=g: e.scalar_tensor_tensor(
                        out=md.ap, in0=ac.ap, scalar=g.ap, in1=rv.ap, op0=ALU.mult, op1=ALU.mult), R=[ac, g, rv], W=[md])
                for uo in range(4):
                    wo, _ = wload(l, f"wo{uo}", "B")
                    for dd in range(2):
                        dm = uo * 2 + dd
                        ob = bank(6 + (dm % 2))
                        for ck in range(NDK):
                            w_ = wo.v(slice(None), slice(ck * 256 + dd * 128, ck * 256 + dd * 128 + 128))
                            m_ = mergedT.v(slice(None), ck, slice(None))
                            P.op("pe", lambda e, ob=ob, w_=w_, m_=m_, ck=ck: e.matmul(ob.ap, w_.ap, m_.ap, start=(ck == 0), stop=(ck == NDK - 1)),
                                 R=[w_, m_], W=[ob])
                        xv = xT.v(slice(None), dm, slice(cols[0], cols[1]))
                        P.op("dve", lambda e, xv=xv, ob=ob: e.tensor_tensor(out=xv.ap, in0=ob.ap, in1=xv.ap, op=ALU.add), R=[ob, xv], W=[xv])

            def ffn_norm(i):
                o0, o1 = FT[i]
                lo_, hi_ = o0 - 1, o1 + 1
                vlo, vhi = max(lo_, 0), min(hi_, S)
                hb = hnC[i % 2]
                wi = hi_ - lo_
                if vlo > lo_:
                    z = hb.v(slice(None), slice(None), slice(0, 1))
                    P.op("dve", lambda e: e.memset(z.ap, 0.0), W=[z])
                if vhi < hi_:
                    z2 = hb.v(slice(None), slice(None), slice(wi - 1, wi))
                    P.op("dve", lambda e: e.memset(z2.ap, 0.0), W=[z2])
                rms_front(l, 8, (vlo, vhi), hb, vlo - lo_, vhi - vlo, sqC, rstdC, 0)

            ffn_norm(0)
            for i in range(len(FT)):
                o0, o1 = FT[i]
                wo_ = o1 - o0
                wi = wo_ + 2
                hb = hnC[i % 2]
                for jc in range(NCJ):
                    wt, _ = wload(l, f"up{jc}", "C")
                    G, U = bank(1 + (jc % 2)), bank(3 + (jc % 2))
                    Gw = bank(1 + (jc % 2), c1=wi); Uw = bank(3 + (jc % 2), c1=wi)
                    for half, bb in ((0, Gw), (1, Uw)):
                        for dk in range(NDK):
                            w_ = wt.v(slice(None), slice(dk * 256 + half * 128, dk * 256 + half * 128 + 128))
                            h_ = hb.v(slice(None), dk, slice(0, wi))
                            P.op("pe", lambda e, bb=bb, w_=w_, h_=h_, dk=dk: e.matmul(bb.ap, w_.ap, h_.ap, start=(dk == 0), stop=(dk == NDK - 1)),
                                 R=[w_, h_], W=[bb])
                    ag, au, th = [t.v(slice(None), slice(0, wo_)) for t in ctmp[jc % 2]]
                    for (bb, acc, pj) in ((Gw, ag, jc), (Uw, au, NCJ + jc)):
                        cw = [pcol(l, 32 + pj * 4 + k) for k in range(4)]
                        P.op("act", lambda e, bb=bb, acc=acc, cw=cw, wo_=wo_: e.activation(
                            out=acc.ap, in_=bb.ap[:, 1:1 + wo_], func=AF.Identity, scale=cw[1].ap, bias=cw[3].ap), R=[bb, cw[1], cw[3]], W=[acc])
                        P.op("dve", lambda e, bb=bb, acc=acc, cw=cw, wo_=wo_: e.scalar_tensor_tensor(
                            out=acc.ap, in0=bb.ap[:, 0:wo_], scalar=cw[0].ap, in1=acc.ap, op0=ALU.mult, op1=ALU.add), R=[bb, cw[0], acc], W=[acc])
                        P.op("dve", lambda e, bb=bb, acc=acc, cw=cw, wo_=wo_: e.scalar_tensor_tensor(
                            out=acc.ap, in0=bb.ap[:, 2:2 + wo_], scalar=cw[2].ap, in1=acc.ap, op0=ALU.mult, op1=ALU.add), R=[bb, cw[2], acc], W=[acc])
                    P.op("act", lambda e, th=th, ag=ag: e.activation(out=th.ap, in_=ag.ap, func=AF.Silu), R=[ag], W=[th])
                    gd = gT.v(slice(None), jc, slice(0, wo_))
                    P.op("dve", lambda e, gd=gd, th=th, au=au: e.tensor_tensor(out=gd.ap, in0=th.ap, in1=au.ap, op=ALU.mult), R=[th, au], W=[gd])
                if i + 1 < len(FT):
                    ffn_norm(i + 1)
                for dm in range(NDK):
                    ob = bank(5 + (dm % 2), c1=wo_)
                    for hf in range(2):
                        wd, _ = wload(l, f"dn{dm * 2 + hf}", "C")
                        for cjj in range(11):
                            cj = hf * 11 + cjj
                            w_ = wd.v(slice(None), slice(cjj * 128, cjj * 128 + 128))
                            g_ = gT.v(slice(None), cj, slice(0, wo_))
                            P.op("pe", lambda e, ob=ob, w_=w_, g_=g_, cj=cj: e.matmul(ob.ap, w_.ap, g_.ap, start=(cj == 0), stop=(cj == NCJ - 1)),
                                 R=[w_, g_], W=[ob])
                    xv = xT.v(slice(None), dm, slice(o0, o1))
                    P.op("dve", lambda e, xv=xv, ob=ob: e.tensor_tensor(out=xv.ap, in0=ob.ap, in1=xv.ap, op=ALU.add), R=[ob, xv], W=[xv])

        P.dma("pool", s_out,
              [lambda e, dk=dk, s=s: e.dma_start(out=oT_d[s, dk * 128:(dk + 1) * 128, :], in_=xT.h[:, dk, :]) for dk in range(NDK)],
              R=[xT.v()])
    P.final_wait("pool", s_out)

    return nc, P


def _emit(nc, P):
    vals = {}
    for n, q in P.q.items():
        c = 0
        vv = []
        for ins in q.instrs:
            if ins[2]:
                c += 1
            vv.append(c)
        vals[n] = vv

    def run(n, e):
        q = P.q[n]
        waited = {}
        for ins in q.instrs:
            fn, waits, sig, dsem = ins
            for w in waits:
                if w[0] == "e":
                    semh = P.q[w[1]].sem; v = vals[w[1]][w[2]]; k = w[1]
                else:
                    semh = w[1].h; v = w[2]; k = id(w[1])
                if waited.get(k, 0) >= v:
                    continue
                waited[k] = v
                e.wait_ge(semh, v)
            if fn is None:
                continue
            i = fn(e)
            if dsem is not None:
                i.then_inc(dsem.h, 16)
            elif sig:
                i.then_inc(q.sem, 1)

    with nc.Block() as block:
        @block.tensor
        def _pe(e):
            run("pe", e)

        @block.scalar
        def _act(e):
            run("act", e)

        @block.vector
        def _dve(e):
            run("dve", e)

        @block.gpsimd
        def _pool(e):
            run("pool", e)

        @block.sync
        def _sp(e):
            run("sp", e)


def _partner():
    i = np.arange(64)
    return np.where((i % 32) < 16, i + 16, i - 16)


def _rope_tables(S):
    rows = S // 64
    row = np.repeat(np.arange(rows, dtype=np.float32), 64)
    col = np.tile(np.arange(64, dtype=np.float32), rows)
    inv = (1.0 / (np.float32(10000.0) ** (np.arange(0, 32, 2, dtype=np.float32) / np.float32(32)))).astype(np.float32)
    C = np.zeros((64, S), np.float32); Sg = np.zeros((64, S), np.float32)
    for i in range(64):
        pos = row if i < 32 else col
        ang = (pos * inv[i % 16]).astype(np.float32)
        C[i] = np.cos(ang)
        sgn = -1.0 if (i % 32) < 16 else 1.0
        Sg[i] = sgn * np.sin(ang)
    return np.concatenate([C, C], 0), np.concatenate([Sg, Sg], 0)


def _unit8(Wcols):
    n = Wcols.shape[1]
    return Wcols.reshape(8, 128, n).transpose(1, 0, 2)


def prep_layer(l, inp):
    w_in = inp["w_in"][l]; w_o = inp["w_o"][l]; w_up = inp["w_up"][l]; w_dn = inp["w_down"][l]
    sg_w = inp["sg_w"][l]
    pt = _partner()
    flat = np.empty(LTOT, np.float32)

    def put(name, arr):
        a = np.ascontiguousarray(arr, dtype=np.float32).reshape(-1)
        flat[LOFF[name]:LOFF[name] + a.size] = a

    kc = np.arange(512, 640)
    ks = 512 + np.concatenate([pt, 64 + pt])
    put("kk", _unit8(np.concatenate([w_in[:, kc], w_in[:, ks]], 1)))
    put("v", _unit8(w_in[:, 640:768]))
    for c in range(4):
        qc = np.concatenate([c * 64 + np.arange(64), (c + 4) * 64 + np.arange(64)])
        qs = np.concatenate([c * 64 + pt, (c + 4) * 64 + pt])
        put(f"q{c}", _unit8(np.concatenate([w_in[:, qc], w_in[:, qs]], 1)))
    for nm, c0 in (("vv", 1280), ("u", 768)):
        Wv = w_in[:, c0:c0 + 512].reshape(8, 128, 512)
        for h in range(2):
            put(f"{nm}{h}", Wv[h * 4:(h + 1) * 4].transpose(1, 0, 2))
    put("sgw", sg_w.transpose(2, 0, 1))
    rowmap = np.concatenate([np.concatenate([c * 64 + np.arange(64), (c + 4) * 64 + np.arange(64)]) for c in range(4)]
                            + [512 + np.arange(512)])
    Wop = w_o[rowmap, :]
    for uo in range(4):
        put(f"wo{uo}", _unit8(Wop[:, uo * 256:(uo + 1) * 256]))
    for j in range(NCJ):
        put(f"up{j}", _unit8(np.concatenate([w_up[:, j * 128:(j + 1) * 128], w_up[:, DFF + j * 128:DFF + (j + 1) * 128]], 1)))
    Wd = w_dn.reshape(2, 11, 128, 8, 128)
    for dm in range(8):
        for hf in range(2):
            put(f"dn{dm * 2 + hf}", Wd[hf, :, :, dm, :].transpose(1, 0, 2))
    return flat.reshape(LTOT // 4096, 4096), rowmap


def prep_params(inp, L):
    par = np.zeros((128, L, NPAR), np.float32)
    pt = _partner()
    p = np.arange(128)
    for l in range(L):
        par[:, l, 0:8] = inp["attn_norm_g"][l].reshape(8, 128).T
        par[:, l, 8:16] = inp["ffn_norm_g"][l].reshape(8, 128).T
        gq = inp["q_norm_g"][l]; gk = inp["k_norm_g"][l]
        par[:, l, 16] = gq[p % 64]; par[:, l, 17] = gq[pt[p % 64]]
        par[:, l, 18] = gk[p % 64]; par[:, l, 19] = gk[pt[p % 64]]
        ag = inp["attn_out_g"][l]
        for c in range(4):
            par[0:64, l, 20 + c] = ag[c * 64:(c + 1) * 64]
            par[64:128, l, 20 + c] = ag[(c + 4) * 64:(c + 5) * 64]
        par[:, l, 24:32] = inp["sg_b"][l].T
        cw = inp["conv_w"][l]; cb = inp["conv_b"][l]
        cp = np.stack([cw[0], cw[1], cw[2], cb], -1).reshape(44, 128, 4).transpose(1, 0, 2)
        par[:, l, 32:208] = cp.reshape(128, 176)
    bcg = np.stack([np.stack([inp["sg_norm_g"][l].reshape(512), inp["sg_out_g"][l]]) for l in range(L)]).astype(np.float32)
    return par.reshape(128, L * NPAR), bcg


def consts():
    perm = np.zeros((128, 128), np.float32)
    m = np.arange(128)
    perm[(m + 64) % 128, m] = 1.0
    blk = np.zeros((128, 128), np.float32)
    blk[0:64, 0:64] = 1.0; blk[64:128, 64:128] = 1.0
    return np.stack([perm, blk, np.eye(128, dtype=np.float32)])


_CACHE = {}


def run(inputs, n_cores, L=None):
    x = np.asarray(inputs["x"])
    B, S, _ = x.shape
    if L is None:
        L = inputs["w_in"].shape[0]
    NSEQ = B // n_cores
    key = (S, NSEQ, L)
    if key not in _CACHE:
        nc, P = build_program(S, NSEQ, L)
        _emit(nc, P)
        _CACHE[key] = nc
    nc = _CACHE[key]
    inp = {k: np.asarray(v) for k, v in inputs.items()}
    wfs = [prep_layer(l, inp)[0] for l in range(L)]
    par, bcg = prep_params(inp, L)
    C, Sg = _rope_tables(S)
    cst = consts()
    xT = np.ascontiguousarray(x.transpose(0, 2, 1))
    in_maps = []
    for c in range(n_cores):
        m = {"xT": xT[c * NSEQ:(c + 1) * NSEQ], "par": par, "bcg": bcg, "ropeC": C, "ropeS": Sg, "cst": cst}
        for l in range(L):
            m[f"wf{l}"] = wfs[l]
        in_maps.append(m)
    res = run_bass_kernel_spmd(nc, in_maps, core_ids=list(range(n_cores)))
    oT = np.concatenate([r["oT"] for r in res.results], 0)
    return np.ascontiguousarray(oT.transpose(0, 2, 1))


def kernel(**inputs):
    return run(inputs, 8)
```

```python
import numpy as np
import concourse.bass as bass
import concourse.mybir as mybir
from concourse.bass_utils import run_bass_kernel_spmd

F32 = mybir.dt.float32
BF16 = mybir.dt.bfloat16
AF = mybir.ActivationFunctionType
ALU = mybir.AluOpType
AX = mybir.AxisListType

D = 1024
NDK = 8
DFF = 2816
NCJ = 22
EPS = 1e-6
NPAR = 208
PAGE = 512
GELU_C = 0.7978845608028654
DUMMY_MM = 0

U4 = 128 * 8 * 256
UV = 128 * 8 * 128
UDN = 128 * 11 * 128


def layer_layout():
    off = {}
    o = 0
    off["kk"] = o; o += U4
    off["v"] = o; o += UV
    a_end = o
    for c in range(4):
        off[f"q{c}"] = o; o += U4
    for h in range(2):
        off[f"vv{h}"] = o; o += U4
    off["sgw"] = o; o += UV
    for h in range(2):
        off[f"u{h}"] = o; o += U4
    for h in range(4):
        off[f"wo{h}"] = o; o += U4
    b_end = o
    for j in range(NCJ):
        off[f"up{j}"] = o; o += U4
    for j in range(16):
        off[f"dn{j}"] = o; o += UDN
    c_end = o
    return off, (0, a_end, b_end, c_end)


LOFF, LPARTS = layer_layout()
LTOT = LPARTS[3]


def ffn_tiles(S):
    n = -(-S // 510)
    base = -(-S // n)
    tiles = []
    o = 0
    while o < S:
        w = min(base, S - o)
        tiles.append((o, o + w))
        o += w
    return tiles


class Tracker:
    def __init__(self):
        self.pages = {}

    def _page(self, i):
        p = self.pages.get(i)
        if p is None:
            p = ({}, {})
            self.pages[i] = p
        return p

    def read(self, ranges, key, ref, deps):
        for lo, hi in ranges:
            for i in range(lo // PAGE, (hi - 1) // PAGE + 1):
                w, r = self._page(i)
                for v in w.values():
                    deps.append((v, True))
                r[key] = ref

    def write(self, ranges, key, ref, deps):
        for lo, hi in ranges:
            for i in range(lo // PAGE, (hi - 1) // PAGE + 1):
                w, r = self._page(i)
                for v in w.values():
                    deps.append((v, False))
                for v in r.values():
                    deps.append((v, False))
                if lo <= i * PAGE and hi >= (i + 1) * PAGE:
                    w.clear(); r.clear()
                w[key] = ref


class View:
    __slots__ = ("ap", "trk", "ranges")

    def __init__(self, ap, trk, ranges):
        self.ap = ap; self.trk = trk; self.ranges = ranges

    def map(self, f):
        return View(f(self.ap), self.trk, self.ranges)


class Tens:
    def __init__(self, handle, shape, esize, off, trk):
        self.h = handle; self.shape = list(shape); self.esize = esize; self.off = off; self.trk = trk
        self.fstr = []
        s = 1
        for n in reversed(self.shape[1:]):
            self.fstr.insert(0, s); s *= n
        self.fsize = s

    def v(self, *key):
        key = list(key)
        while len(key) < len(self.shape):
            key.append(slice(None))
        ap = self.h[tuple(key)]
        fk = key[1:]
        spans = []
        for k, n in zip(fk, self.shape[1:]):
            if isinstance(k, slice):
                a = 0 if k.start is None else k.start
                b = n if k.stop is None else k.stop
            else:
                a, b = k, k + 1
            assert 0 <= a < b <= n, (key, self.shape)
            spans.append((a, b))
        ranges = []

        def rec(d, base):
            if d == len(spans):
                ranges.append((base, base + 1)); return
            a, b = spans[d]
            full_tail = all(spans[j] == (0, self.shape[1 + j]) for j in range(d + 1, len(spans)))
            if full_tail:
                ranges.append((base + a * self.fstr[d], base + b * self.fstr[d]))
            else:
                for i in range(a, b):
                    rec(d + 1, base + i * self.fstr[d])
        rec(0, 0)
        br = [(self.off + lo * self.esize, self.off + hi * self.esize) for lo, hi in ranges]
        return View(ap, self.trk, br)


class Sem:
    def __init__(self, h):
        self.h = h; self.count = 0


class EngQ:
    def __init__(self, name):
        self.name = name
        self.instrs = []
        self.sem = None


class Prog:
    def __init__(self, nc):
        self.nc = nc
        self.q = {n: EngQ(n) for n in ("pe", "act", "dve", "pool", "sp")}
        self.sb = Tracker(); self.ps = Tracker()
        self.dram = {}
        self.sems = []

    def dram_view(self, ap, name):
        if name not in self.dram:
            self.dram[name] = Tracker()
        return View(ap, self.dram[name], [(0, PAGE)])

    def _deps(self, eng, R, W, key, ref):
        deps = []
        for v in R:
            v.trk.read(v.ranges, key, ref, deps)
        for v in W:
            v.trk.write(v.ranges, key, ref, deps)
        out = {}
        self_idx = ref[2] if ref[0] == "e" else -1
        for (d, is_raw) in deps:
            if d[0] == "e":
                if d[1] == eng:
                    if eng == "pe":
                        continue
                    if d[2] == self_idx:
                        continue
                k = ("e", d[1])
                if k not in out or out[k] < d[2]:
                    out[k] = d[2]
            else:
                k = ("d", id(d[1]))
                if k not in out or out[k][1] < d[2]:
                    out[k] = (d[1], d[2])
        return out

    def op(self, eng, fn, R=(), W=()):
        q = self.q[eng]
        idx = len(q.instrs)
        ref = ("e", eng, idx)
        deps = self._deps(eng, R, W, eng, ref)
        waits = []
        for k, v in deps.items():
            if k[0] == "e":
                self.q[k[1]].instrs[v][2] = True
                waits.append(("e", k[1], v))
            else:
                waits.append(("d", v[0], v[1]))
        q.instrs.append([fn, waits, False, None])
        return ref

    def dma(self, eng, sem, fns, R=(), W=()):
        q = self.q[eng]
        val = sem.count + 16 * len(fns)
        ref = ("d", sem, val)
        deps = self._deps(eng, R, W, ("d", id(sem)), ref)
        waits = []
        for k, v in deps.items():
            if k[0] == "e":
                self.q[k[1]].instrs[v][2] = True
                waits.append(("e", k[1], v))
            else:
                if v[0] is sem and v[1] >= val:
                    continue
                waits.append(("d", v[0], v[1]))
        for i, fn in enumerate(fns):
            q.instrs.append([fn, waits if i == 0 else [], False, sem])
        sem.count = val
        return ref

    def final_wait(self, eng, sem):
        self.q[eng].instrs.append([None, [("d", sem, sem.count)], False, None])

    def replay(self, engines):
        vals = {}
        for n, q in self.q.items():
            c = 0
            vv = []
            for ins in q.instrs:
                if ins[2]:
                    c += 1
                vv.append(c)
            vals[n] = vv
        for n, q in self.q.items():
            e = engines[n]
            waited = {}
            for ins in q.instrs:
                fn, waits, sig, dsem = ins
                for w in waits:
                    if w[0] == "e":
                        semh = self.q[w[1]].sem; v = vals[w[1]][w[2]]; k = w[1]
                    else:
                        semh = w[1].h; v = w[2]; k = id(w[1])
                    if waited.get(k, 0) >= v:
                        continue
                    waited[k] = v
                    e.wait_ge(semh, v)
                if fn is None:
                    continue
                i = fn(e)
                if dsem is not None:
                    i.then_inc(dsem.h, 16)
                elif sig:
                    i.then_inc(q.sem, 1)


def build_program(S, NSEQ, L):
    nc = bass.Bass("TRN2", target_bir_lowering=False)
    P = Prog(nc)
    NQT = S // 512
    NKC = S // 128
    FT = ffn_tiles(S)
    WOM = max(b - a for a, b in FT)
    WIM = WOM + 2

    xT_d = nc.dram_tensor("xT", [NSEQ, D, S], F32, kind="ExternalInput").ap()
    oT_d = nc.dram_tensor("oT", [NSEQ, D, S], F32, kind="ExternalOutput").ap()
    wf_d = [nc.dram_tensor(f"wf{l}", [LTOT // 4096, 4096], F32, kind="ExternalInput").ap() for l in range(L)]
    wb_d = [nc.dram_tensor(f"wb{l}", [LTOT // 4096, 4096], BF16, kind="Internal").ap() for l in range(L)]
    par_d = nc.dram_tensor("par", [128, L * NPAR], F32, kind="ExternalInput").ap()
    bcg_d = nc.dram_tensor("bcg", [L, 2, 512], F32, kind="ExternalInput").ap()
    rc_d = nc.dram_tensor("ropeC", [128, S], F32, kind="ExternalInput").ap()
    rs_d = nc.dram_tensor("ropeS", [128, S], F32, kind="ExternalInput").ap()
    cst_d = nc.dram_tensor("cst", [3, 128, 128], F32, kind="ExternalInput").ap()

    base = (nc.sbuf_base + PAGE - 1) // PAGE * PAGE
    top = nc.sbuf_top
    cur = [base]

    def alloc(name, shape, dt, at=None):
        es = 4 if dt == F32 else 2
        n = 1
        for s in shape[1:]:
            n *= s
        nbytes = (n * es + PAGE - 1) // PAGE * PAGE
        if at is None:
            off = cur[0]; cur[0] += nbytes
        else:
            off = at
        assert off + nbytes <= top, (name, off, nbytes, top)
        h = nc.alloc_sbuf_tensor_at(name, list(shape), dt, offset=off)
        return Tens(h, shape, es, off, P.sb)

    xT = alloc("xT_sb", [128, NDK, S], F32)
    ones_b = alloc("ones_b", [128, 128], BF16)
    perm_b = Tens(nc.alloc_sbuf_tensor_at("perm_b", [128, 128], BF16, offset=ones_b.off + 256), [128, 128], 2, ones_b.off + 256, P.sb)
    blk_b = alloc("blk_b", [128, 128], BF16)
    idn_b = Tens(nc.alloc_sbuf_tensor_at("idn_b", [128, 128], BF16, offset=blk_b.off + 256), [128, 128], 2, blk_b.off + 256, P.sb)
    par = alloc("par_sb", [128, L, NPAR], F32)
    phase_base = cur[0]

    NRB = 3
    KT = alloc("KT", [128, S], BF16)
    Vx = alloc("Vx", [128, NKC, 192], BF16)
    ringB = [alloc(f"ringB{i}", [128, 2048], BF16) for i in range(NRB)]
    hnT = alloc("hnT", [128, NDK, 512], BF16)
    sqT = alloc("sqT", [128, NDK, 512], BF16)
    mergedT = Tens(sqT.h, [128, NDK, 512], 2, sqT.off, P.sb)
    QT = alloc("QT", [128, 4, 512], BF16)
    tabC = alloc("tabC", [128, 512], F32)
    tabS = alloc("tabS", [128, 512], F32)
    t1 = alloc("t1", [128, 512], F32)
    t2 = alloc("t2", [128, 512], F32)
    vvn = alloc("vvn", [128, 4, 512], BF16)
    t3 = alloc("t3", [128, 512], BF16)
    gsg_bc = alloc("gsg_bc", [128, 512], F32)
    gout_bc = alloc("gout_bc", [128, 512], F32)
    rstd = alloc("rstd", [128, 512], F32)
    sqq = Tens(nc.alloc_sbuf_tensor_at("sqq", [128, 512], BF16, offset=t3.off), [128, 512], 2, t3.off, P.sb)
    rq = Tens(nc.alloc_sbuf_tensor_at("rq", [128, 512], F32, offset=rstd.off), [128, 512], 4, rstd.off, P.sb)
    sm8 = [alloc(f"sm8_{i}", [128, 8], F32) for i in range(2)]
    sm1 = [alloc(f"sm1_{i}", [128, 8], F32) for i in range(2)]
    t3b = alloc("t3b", [128, 512], BF16)
    endB = cur[0]
    aoT = Tens(nc.alloc_sbuf_tensor_at("aoT", [128, 4, 512], F32, offset=hnT.off), [128, 4, 512], 4, hnT.off, P.sb)
    NPT = 3
    PT = [Tens(nc.alloc_sbuf_tensor_at(f"PT{i}", [128, 512], BF16, offset=t1.off + 1024 * i), [128, 512], 2,
               t1.off + 1024 * i, P.sb) for i in range(NPT)]
    t1b = Tens(nc.alloc_sbuf_tensor_at("t1b", [128, 512], F32, offset=tabC.off), [128, 512], 4, tabC.off, P.sb)
    t2b = Tens(nc.alloc_sbuf_tensor_at("t2b", [128, 512], F32, offset=tabS.off), [128, 512], 4, tabS.off, P.sb)
    TA = [t1, t1b]; TB = [t2, t2b]; T3 = [t3, t3b]
    PT2 = [Tens(nc.alloc_sbuf_tensor_at(f"PT2_{i}", [128, 1024], BF16, offset=t1.off + 2048 * i), [128, 1024], 2,
                t1.off + 2048 * i, P.sb) for i in range(2)]
    assert t2.off == t1.off + 2048
    rD = Tens(nc.alloc_sbuf_tensor_at("rD", [128, 512], F32, offset=tabC.off), [128, 512], 4, tabC.off, P.sb)
    rhi = Tens(nc.alloc_sbuf_tensor_at("rhi", [128, 512], BF16, offset=tabS.off), [128, 512], 2, tabS.off, P.sb)
    rlo = Tens(nc.alloc_sbuf_tensor_at("rlo", [128, 512], BF16, offset=tabS.off + 1024), [128, 512], 2,
               tabS.off + 1024, P.sb)
    sqa = Tens(nc.alloc_sbuf_tensor_at("sqa", [128, 4, 512], BF16, offset=vvn.off), [128, 4, 512], 2, vvn.off, P.sb)

    cur[0] = phase_base
    NRC = 4
    ringC = [alloc(f"ringC{i}", [128, 2048], BF16) for i in range(NRC)]
    hnC = [alloc(f"hnC{i}", [128, NDK, WIM], BF16) for i in range(2)]
    sqC = alloc("sqC", [128, NDK, WIM], BF16)
    rstdC = alloc("rstdC", [128, WIM], F32)
    gT = alloc("gT", [128, NCJ, WOM], BF16)
    ctmp = [[alloc(f"c{n}{i}", [128, WOM], F32) for n in ("ag", "au", "th")] for i in range(2)]
    endC = cur[0]
    assert max(endB, endC) <= top, (endB, endC, top)

    psh = nc.alloc_psum_tensor("ps", [128, 4096], F32)
    ps = Tens(psh, [128, 4096], 4, 0, P.ps)

    def bank(b, p=slice(None), c0=0, c1=512):
        return ps.v(p, slice(512 * b + c0, 512 * b + c1))

    def sem(name):
        s = Sem(nc.alloc_semaphore(name))
        P.sems.append(s)
        return s

    for n in P.q:
        P.q[n].sem = nc.alloc_semaphore("prog_" + n)
    s_cst = sem("s_cst"); s_par = sem("s_par"); s_x = sem("s_x"); s_out = sem("s_out")
    s_tab = sem("s_tab"); s_bcg = sem("s_bcg")
    s_cast = [[sem(f"s_cast{l}_{p}") for p in range(3)] for l in range(L)]
    s_ringB = [sem(f"s_rb{i}") for i in range(NRB)]
    s_ringC = [sem(f"s_rc{i}") for i in range(NRC)]

    wbv = [[P.dram_view(wb_d[l], f"wb{l}_{p}") for p in range(3)] for l in range(L)]
    ring_state = {"B": [ringB, s_ringB, 0], "C": [ringC, s_ringC, 0]}

    def wload(l, name, which):
        ring, sems, i = ring_state[which]
        slot = i % len(ring)
        ring_state[which][2] = i + 1
        off = LOFF[name]
        part = 0 if off < LPARTS[1] else (1 if off < LPARTS[2] else 2)
        n = UV if name in ("v", "sgw") else (UDN if name.startswith("dn") else U4)
        F = n // 128
        src = wb_d[l].rearrange("r c -> (r c)")[off:off + n].rearrange("(p f) -> p f", p=128)
        dst = ring[slot].v(slice(None), slice(0, F))
        P.dma("sp", sems[slot], [lambda e, d=dst.ap, s=src: e.dma_start(out=d, in_=s)],
              R=[wbv[l][part]], W=[dst])
        return ring[slot], F

    P.dma("pool", s_cst,
          [lambda e: e.dma_start(out=perm_b.h[:], in_=cst_d[0]),
           lambda e: e.dma_start(out=blk_b.h[:], in_=cst_d[1]),
           lambda e: e.dma_start(out=idn_b.h[:], in_=cst_d[2])],
          W=[perm_b.v(), blk_b.v(), idn_b.v()])
    P.dma("pool", s_par, [lambda e: e.dma_start(out=par.h[:].rearrange("p l n -> p (l n)"), in_=par_d)], W=[par.v()])
    P.op("dve", lambda e: e.memset(ones_b.h[:], 1.0), W=[ones_b.v()])

    def cast(l, p):
        r0, r1 = LPARTS[p] // 4096, LPARTS[p + 1] // 4096
        P.dma("pool", s_cast[l][p], [lambda e: e.dma_start(out=wb_d[l][r0:r1, :], in_=wf_d[l][r0:r1, :])],
              W=[wbv[l][p]])

    def xload(s):
        P.dma("pool", s_x,
              [lambda e, dk=dk: e.dma_start(out=xT.h[:, dk, :], in_=xT_d[s, dk * 128:(dk + 1) * 128, :])
               for dk in range(NDK)], W=[xT.v()])

    xload(0)
    cast(0, 0)
    cast(0, 1)
    cast(0, 2)

    def pcol(l, i):
        return par.v(slice(None), l, slice(i, i + 1))

    def rms_front(l, gofs, xcols, dst, dst_c0, width, sq, rs, nbank):
        c0, c1 = xcols
        xs = xT.v(slice(None), slice(None), slice(c0, c1))
        sqv = sq.v(slice(None), slice(None), slice(0, width))
        P.op("act", lambda e: e.activation(out=sqv.ap, in_=xs.ap, func=AF.Square), R=[xs], W=[sqv])
        nb = bank(nbank, c1=width)
        for dk in range(NDK):
            s_ = sq.v(slice(None), dk, slice(0, width))
            P.op("pe", lambda e, s_=s_, dk=dk: e.matmul(nb.ap, ones_b.h[:], s_.ap, start=(dk == 0), stop=(dk == NDK - 1)),
                 R=[ones_b.v(), s_], W=[nb])
        rv = rs.v(slice(None), slice(0, width))
        P.op("act", lambda e: e.activation(out=rv.ap, in_=nb.ap, func=AF.Sqrt, scale=1.0 / D, bias=EPS), R=[nb], W=[rv])
        P.op("dve", lambda e: e.reciprocal(out=rv.ap, in_=rv.ap), R=[rv], W=[rv])
        for dk in range(NDK):
            xv = xT.v(slice(None), dk, slice(c0, c1))
            dv = dst.v(slice(None), dk, slice(dst_c0, dst_c0 + width))
            g = pcol(l, gofs + dk)
            P.op("dve", lambda e, xv=xv, dv=dv, g=g: e.scalar_tensor_tensor(
                out=dv.ap, in0=xv.ap, scalar=g.ap, in1=rv.ap, op0=ALU.mult, op1=ALU.mult), R=[xv, g, rv], W=[dv])

    def load_tables(j):
        P.dma("pool", s_tab,
              [lambda e: e.dma_start(out=tabC.h[:], in_=rc_d[:, j * 512:(j + 1) * 512]),
               lambda e: e.dma_start(out=tabS.h[:], in_=rs_d[:, j * 512:(j + 1) * 512])],
              W=[tabC.v(), tabS.v()])

    def proj_rope(l, wt, gcol, dstv):
        bq, bs, bn = bank(1), bank(2), bank(0)
        for half, bb in ((0, bq), (1, bs)):
            for dk in range(NDK):
                w_ = wt.v(slice(None), slice(dk * 256 + half * 128, dk * 256 + half * 128 + 128))
                h_ = hnT.v(slice(None), dk, slice(None))
                P.op("pe", lambda e, w_=w_, h_=h_, bb=bb, dk=dk: e.matmul(bb.ap, w_.ap, h_.ap, start=(dk == 0), stop=(dk == NDK - 1)),
                     R=[w_, h_], W=[bb])
        sv = sqq.v()
        P.op("act", lambda e: e.activation(out=sv.ap, in_=bq.ap, func=AF.Square), R=[bq], W=[sv])
        P.op("pe", lambda e: e.matmul(bn.ap, blk_b.h[:], sv.ap, start=True, stop=True), R=[blk_b.v(), sv], W=[bn])
        rv = rq.v()
        P.op("act", lambda e: e.activation(out=rv.ap, in_=bn.ap, func=AF.Sqrt, scale=1.0 / 64, bias=EPS), R=[bn], W=[rv])
        P.op("dve", lambda e: e.reciprocal(out=rv.ap, in_=rv.ap), R=[rv], W=[rv])
        a, b = t1.v(), t2.v()
        g0, g1 = pcol(l, gcol), pcol(l, gcol + 1)
        tc_, ts_ = tabC.v(), tabS.v()
        P.op("dve", lambda e: e.scalar_tensor_tensor(out=a.ap, in0=bq.ap, scalar=g0.ap, in1=tc_.ap, op0=ALU.mult, op1=ALU.mult),
             R=[bq, g0, tc_], W=[a])
        P.op("dve", lambda e: e.scalar_tensor_tensor(out=b.ap, in0=bs.ap, scalar=g1.ap, in1=ts_.ap, op0=ALU.mult, op1=ALU.mult),
             R=[bs, g1, ts_], W=[b])
        P.op("dve", lambda e: e.tensor_tensor(out=a.ap, in0=a.ap, in1=b.ap, op=ALU.add), R=[a, b], W=[a])
        P.op("dve", lambda e: e.tensor_tensor(out=dstv.ap, in0=a.ap, in1=rv.ap, op=ALU.mult), R=[a, rv], W=[dstv])

    def gelu2(src, dst):
        P.op("act", lambda e: e.activation(out=dst.ap, in_=src.ap, func=AF.Gelu_apprx_tanh), R=[src], W=[dst])

    for s in range(NSEQ):
        if s > 0:
            xload(s)
        for l in range(L):
            P.dma("pool", s_bcg,
                  [lambda e, l=l: e.dma_start(out=gsg_bc.h[:], in_=bcg_d[l, 0:1, :].partition_broadcast(128)),
                   lambda e, l=l: e.dma_start(out=gout_bc.h[:], in_=bcg_d[l, 1:2, :].partition_broadcast(128))],
                  W=[gsg_bc.v(), gout_bc.v()])
            if True:
                vo = Vx.v(slice(None), slice(None), slice(64, 128))
                P.op("dve", lambda e, vo=vo: e.memset(vo.ap, 1.0), W=[vo])

            for j in range(NQT):
                cols = (j * 512, (j + 1) * 512)
                rms_front(l, 0, cols, hnT, 0, 512, sqT, rstd, 0)
                load_tables(j)
                wt, _ = wload(l, "kk", "B")
                proj_rope(l, wt, 18, KT.v(slice(None), slice(cols[0], cols[1])))
                wv, _ = wload(l, "v", "B")
                bv = bank(3)
                for sub in range(4):
                    o_ = bank(3, c0=sub * 128, c1=sub * 128 + 128)
                    for dk in range(NDK):
                        h_ = hnT.v(slice(None), dk, slice(sub * 128, sub * 128 + 128))
                        w_ = wv.v(slice(None), slice(dk * 128, dk * 128 + 128))
                        P.op("pe", lambda e, o_=o_, h_=h_, w_=w_, dk=dk: e.matmul(o_.ap, h_.ap, w_.ap, start=(dk == 0), stop=(dk == NDK - 1)),
                             R=[h_, w_], W=[o_])
                bv3 = bv.map(lambda ap: ap.rearrange("p (s c) -> p s c", s=4))
                d0 = Vx.v(slice(None), slice(4 * j, 4 * j + 4), slice(0, 64))
                d1 = Vx.v(slice(None), slice(4 * j, 4 * j + 4), slice(128, 192))
                P.op("act", lambda e, d0=d0, bv3=bv3: e.activation(out=d0.ap, in_=bv3.ap[:, :, 0:64], func=AF.Copy), R=[bv], W=[d0])
                P.op("dve", lambda e, d1=d1, bv3=bv3: e.tensor_copy(out=d1.ap, in_=bv3.ap[:, :, 64:128]), R=[bv], W=[d1])

            if s == 0 and l + 1 < L:
                for p in range(3):
                    cast(l + 1, p)

            for j in range(NQT):
                cols = (j * 512, (j + 1) * 512)
                rms_front(l, 0, cols, hnT, 0, 512, sqT, rstd, 0)
                load_tables(j)
                for c in range(4):
                    wt, _ = wload(l, f"q{c}", "B")
                    proj_rope(l, wt, 16, QT.v(slice(None), c, slice(None)))
                wvv = [wload(l, f"vv{h}", "B")[0] for h in range(2)]
                for h in range(2):
                    for sub in range(4):
                        o_ = bank(sub)
                        for dkk in range(4):
                            dk = h * 4 + dkk
                            h_ = hnT.v(slice(None), dk, slice(sub * 128, sub * 128 + 128))
                            w_ = wvv[h].v(slice(None), slice(dkk * 512, dkk * 512 + 512))
                            P.op("pe", lambda e, o_=o_, h_=h_, w_=w_, dk=dk: e.matmul(o_.ap, h_.ap, w_.ap, start=(dk == 0), stop=(dk == NDK - 1)),
                                 R=[h_, w_], W=[o_])
                def vv_stage(sub, k):
                    kk = sub % 2
                    srcb = bank(sub)
                    gv = TA[kk].v(); b2 = TB[kk].v(); s8 = sm8[kk].v()
                    if k == 0:
                        gelu2(srcb, gv)
                    elif k == 1:
                        P.op("act", lambda e: e.activation(out=b2.ap, in_=gv.ap, func=AF.Square), R=[gv], W=[b2])
                    elif k == 2:
                        P.op("dve", lambda e: e.tensor_reduce(out=s8.ap, in_=b2.ap.rearrange("p (h d) -> p h d", h=8), op=ALU.add, axis=AX.X),
                             R=[b2], W=[s8])
                    elif k == 3:
                        P.op("act", lambda e: e.activation(out=s8.ap, in_=s8.ap, func=AF.Sqrt, scale=1.0 / 64, bias=EPS), R=[s8], W=[s8])
                    elif k == 4:
                        P.op("dve", lambda e: e.reciprocal(out=s8.ap, in_=s8.ap), R=[s8], W=[s8])
                    elif k == 5:
                        P.op("dve", lambda e: e.tensor_tensor(
                            out=gv.ap.rearrange("p (h d) -> p h d", h=8), in0=gv.ap.rearrange("p (h d) -> p h d", h=8),
                            in1=s8.ap.unsqueeze(2).broadcast_to([128, 8, 64]), op=ALU.mult), R=[gv, s8], W=[gv])
                    elif k == 6:
                        vn = vvn.v(slice(None), sub, slice(None))
                        gb = gsg_bc.v()
                        P.op("dve", lambda e: e.tensor_tensor(out=vn.ap, in0=gv.ap, in1=gb.ap, op=ALU.mult), R=[gv, gb], W=[vn])

                def skewed(fn, nstage, skew):
                    order = sorted((k + skew * sub, sub, k) for sub in range(4) for k in range(nstage))
                    for _, sub, k in order:
                        fn(sub, k)

                skewed(vv_stage, 7, 3)
                wsg, _ = wload(l, "sgw", "B")
                for sub in range(4):
                    for hh in range(8):
                        o_ = bank(4 + sub, c0=hh * 64, c1=hh * 64 + 64)
                        w_ = wsg.v(slice(None), slice(hh * 128, hh * 128 + 128))
                        v_ = vvn.v(slice(None), sub, slice(hh * 64, hh * 64 + 64))
                        P.op("pe", lambda e, o_=o_, w_=w_, v_=v_: e.matmul(o_.ap, w_.ap, v_.ap, start=True, stop=True), R=[w_, v_], W=[o_])
                wu = [wload(l, f"u{h}", "B")[0] for h in range(2)]
                for h in range(2):
                    for sub in range(4):
                        o_ = bank(sub)
                        for dkk in range(4):
                            dk = h * 4 + dkk
                            h_ = hnT.v(slice(None), dk, slice(sub * 128, sub * 128 + 128))
                            w_ = wu[h].v(slice(None), slice(dkk * 512, dkk * 512 + 512))
                            P.op("pe", lambda e, o_=o_, h_=h_, w_=w_, dk=dk: e.matmul(o_.ap, h_.ap, w_.ap, start=(dk == 0), stop=(dk == NDK - 1)),
                                 R=[h_, w_], W=[o_])
                def u_stage(sub, k):
                    kk = sub % 2
                    srcb = bank(sub); mx = bank(4 + sub)
                    gu = TA[kk].v(); b2 = TB[kk].v(); sn = T3[kk].v()
                    s1 = sm1[kk].v(slice(None), slice(0, 1))
                    if k == 0:
                        gelu2(srcb, gu)
                    elif k == 1:
                        sb_ = par.v(slice(None), l, slice(24, 32))
                        P.op("dve", lambda e: e.tensor_tensor(
                            out=b2.ap.rearrange("p (h d) -> p h d", h=8), in0=mx.ap.rearrange("p (h d) -> p h d", h=8),
                            in1=sb_.ap.unsqueeze(2).broadcast_to([128, 8, 64]), op=ALU.add), R=[mx, sb_], W=[b2])
                    elif k == 2:
                        P.op("dve", lambda e: e.tensor_tensor(out=gu.ap, in0=gu.ap, in1=b2.ap, op=ALU.mult), R=[gu, b2], W=[gu])
                    elif k == 3:
                        P.op("act", lambda e: e.activation(out=b2.ap, in_=gu.ap, func=AF.Square, accum_out=s1.ap), R=[gu], W=[b2, s1])
                    elif k == 4:
                        P.op("act", lambda e: e.activation(out=s1.ap, in_=s1.ap, func=AF.Sqrt, scale=1.0 / 512, bias=EPS), R=[s1], W=[s1])
                    elif k == 5:
                        P.op("dve", lambda e: e.reciprocal(out=s1.ap, in_=s1.ap), R=[s1], W=[s1])
                    elif k == 6:
                        go = gout_bc.v()
                        P.op("dve", lambda e: e.scalar_tensor_tensor(
                            out=sn.ap, in0=gu.ap, scalar=s1.ap, in1=go.ap, op0=ALU.mult, op1=ALU.mult), R=[gu, s1, go], W=[sn])
                    elif k == 7:
                        trb = bank(4 + sub, c0=0, c1=256)
                        trb_ap = trb.ap.bitcast(BF16)
                        for fc in range(4):
                            i_ = T3[kk].v(slice(None), slice(fc * 128, fc * 128 + 128))
                            P.op("pe", lambda e, i_=i_, fc=fc: e.transpose(trb_ap[:, fc * 128:(fc + 1) * 128], i_.ap, idn_b.h[:]),
                                 R=[i_, idn_b.v()], W=[trb])
                    elif k == 8:
                        trb = bank(4 + sub, c0=0, c1=256)
                        trb_ap = trb.ap.bitcast(BF16)
                        md = mergedT.v(slice(None), slice(4, 8), slice(sub * 128, sub * 128 + 128))
                        P.op("act", lambda e: e.activation(out=md.ap, in_=trb_ap.rearrange("p (f t) -> p f t", f=4), func=AF.Copy),
                             R=[trb], W=[md])

                skewed(u_stage, 9, 3)

                steps = [(c, kc) for c in range(4) for kc in range(NKC)]

                def emit_qk(i):
                    c, kc = steps[i]
                    pb = i % 2
                    for hd in range(2):
                        sb = bank(2 * pb + hd)
                        k_ = KT.v(slice(hd * 64, hd * 64 + 64), slice(kc * 128, kc * 128 + 128))
                        q_ = QT.v(slice(hd * 64, hd * 64 + 64), c, slice(None))
                        P.op("pe", lambda e, sb=sb, k_=k_, q_=q_: e.matmul(sb.ap, k_.ap, q_.ap, start=True, stop=True), R=[k_, q_], W=[sb])
                    s2 = ps.v(slice(None), slice(1024 * pb, 1024 * pb + 1024))
                    pt = PT2[pb].v()
                    P.op("act", lambda e, s2=s2, pt=pt: e.activation(out=pt.ap, in_=s2.ap, func=AF.Exp, scale=0.125), R=[s2], W=[pt])

                def emit_pv(i):
                    c, kc = steps[i]
                    pb = i % 2
                    for hd in range(2):
                        ob = bank(4 + 2 * (c % 2) + hd)
                        pt = PT2[pb].v(slice(None), slice(hd * 512, hd * 512 + 512))
                        v_ = Vx.v(slice(None), kc, slice(hd * 64, hd * 64 + 128))
                        P.op("pe", lambda e, ob=ob, pt=pt, v_=v_, kc=kc: e.matmul(ob.ap, v_.ap, pt.ap, start=(kc == 0), stop=(kc == NKC - 1)), R=[v_, pt], W=[ob])
                    if kc == NKC - 1:
                        tail(c)

                def tail(c):
                    bA, bB = bank(4 + 2 * (c % 2)), bank(5 + 2 * (c % 2))
                    aA = aoT.v(slice(0, 64), c, slice(None)); aB = aoT.v(slice(64, 128), c, slice(None))
                    P.op("dve", lambda e: e.tensor_copy(out=aA.ap, in_=bA.ap[0:64, :]), R=[bA], W=[aA])
                    P.op("dve", lambda e: e.tensor_copy(out=aB.ap, in_=bB.ap[64:128, :]), R=[bB], W=[aB])
                    r_ = rD.v()
                    P.op("dve", lambda e: e.reciprocal(out=r_.ap[64:128, :], in_=bA.ap[64:128, :]), R=[bA], W=[r_])
                    P.op("dve", lambda e: e.reciprocal(out=r_.ap[0:64, :], in_=bB.ap[0:64, :]), R=[bB], W=[r_])
                    hi, lo = rhi.v(), rlo.v()
                    P.op("dve", lambda e: e.tensor_copy(out=hi.ap, in_=r_.ap), R=[r_], W=[hi])
                    P.op("dve", lambda e: e.tensor_tensor(out=lo.ap, in0=r_.ap, in1=hi.ap, op=ALU.subtract), R=[r_, hi], W=[lo])
                    bc = bA
                    P.op("pe", lambda e: e.matmul(bc.ap, perm_b.h[:], hi.ap, start=True, stop=False), R=[perm_b.v(), hi], W=[bc])
                    P.op("pe", lambda e: e.matmul(bc.ap, perm_b.h[:], lo.ap, start=False, stop=True), R=[perm_b.v(), lo], W=[bc])
                    ac = aoT.v(slice(None), c, slice(None))
                    P.op("dve", lambda e: e.tensor_tensor(out=ac.ap, in0=ac.ap, in1=bc.ap, op=ALU.mult), R=[ac, bc], W=[ac])
                    sq_ = sqa.v(slice(None), c, slice(None))
                    P.op("pool", lambda e: e.tensor_tensor(out=sq_.ap, in0=ac.ap, in1=ac.ap, op=ALU.mult), R=[ac], W=[sq_])

                n = len(steps)
                emit_qk(0)
                for i in range(n):
                    if i + 1 < n:
                        emit_qk(i + 1)
                    emit_pv(i)
                nb = bank(0)
                for c in range(4):
                    sq_ = sqa.v(slice(None), c, slice(None))
                    P.op("pe", lambda e, sq_=sq_, c=c: e.matmul(nb.ap, ones_b.h[:], sq_.ap, start=(c == 0), stop=(c == 3)), R=[ones_b.v(), sq_], W=[nb])
                rv = rstd.v()
                P.op("act", lambda e: e.activation(out=rv.ap, in_=nb.ap, func=AF.Sqrt, scale=1.0 / 512, bias=EPS), R=[nb], W=[rv])
                P.op("dve", lambda e: e.reciprocal(out=rv.ap, in_=rv.ap), R=[rv], W=[rv])
                for c in range(4):
                    ac = aoT.v(slice(None), c, slice(None))
                    md = mergedT.v(slice(None), c, slice(None))
                    g = pcol(l, 20 + c)
                    P.op("dve", lambda e, ac=ac, md=md, g=g: e.scalar_tensor_tensor(
                        out=md.ap, in0=ac.ap, scalar=g.ap, in1=rv.ap, op0=ALU.mult, op1=ALU.mult), R=[ac, g, rv], W=[md])
                for uo in range(4):
                    wo, _ = wload(l, f"wo{uo}", "B")
                    for dd in range(2):
                        dm = uo * 2 + dd
                        ob = bank(1 + (dm % 2))
                        for ck in range(NDK):
                            w_ = wo.v(slice(None), slice(ck * 256 + dd * 128, ck * 256 + dd * 128 + 128))
                            m_ = mergedT.v(slice(None), ck, slice(None))
                            P.op("pe", lambda e, ob=ob, w_=w_, m_=m_, ck=ck: e.matmul(ob.ap, w_.ap, m_.ap, start=(ck == 0), stop=(ck == NDK - 1)),
                                 R=[w_, m_], W=[ob])
                        xv = xT.v(slice(None), dm, slice(cols[0], cols[1]))
                        P.op("dve", lambda e, xv=xv, ob=ob: e.tensor_tensor(out=xv.ap, in0=ob.ap, in1=xv.ap, op=ALU.add), R=[ob, xv], W=[xv])

            def ffn_norm(i):
                o0, o1 = FT[i]
                lo_, hi_ = o0 - 1, o1 + 1
                vlo, vhi = max(lo_, 0), min(hi_, S)
                hb = hnC[i % 2]
                wi = hi_ - lo_
                if vlo > lo_:
                    z = hb.v(slice(None), slice(None), slice(0, 1))
                    P.op("dve", lambda e: e.memset(z.ap, 0.0), W=[z])
                if vhi < hi_:
                    z2 = hb.v(slice(None), slice(None), slice(wi - 1, wi))
                    P.op("dve", lambda e: e.memset(z2.ap, 0.0), W=[z2])
                rms_front(l, 8, (vlo, vhi), hb, vlo - lo_, vhi - vlo, sqC, rstdC, 0)

            ffn_norm(0)
            for i in range(len(FT)):
                o0, o1 = FT[i]
                wo_ = o1 - o0
                wi = wo_ + 2
                hb = hnC[i % 2]
                for jc in range(NCJ):
                    wt, _ = wload(l, f"up{jc}", "C")
                    G, U = bank(1 + (jc % 2)), bank(3 + (jc % 2))
                    Gw = bank(1 + (jc % 2), c1=wi); Uw = bank(3 + (jc % 2), c1=wi)
                    for half, bb in ((0, Gw), (1, Uw)):
                        for dk in range(NDK):
                            w_ = wt.v(slice(None), slice(dk * 256 + half * 128, dk * 256 + half * 128 + 128))
                            h_ = hb.v(slice(None), dk, slice(0, wi))
                            P.op("pe", lambda e, bb=bb, w_=w_, h_=h_, dk=dk: e.matmul(bb.ap, w_.ap, h_.ap, start=(dk == 0), stop=(dk == NDK - 1)),
                                 R=[w_, h_], W=[bb])
                    ag, au, th = [t.v(slice(None), slice(0, wo_)) for t in ctmp[jc % 2]]
                    for (bb, acc, pj) in ((Gw, ag, jc), (Uw, au, NCJ + jc)):
                        cw = [pcol(l, 32 + pj * 4 + k) for k in range(4)]
                        P.op("act", lambda e, bb=bb, acc=acc, cw=cw, wo_=wo_: e.activation(
                            out=acc.ap, in_=bb.ap[:, 1:1 + wo_], func=AF.Identity, scale=cw[1].ap, bias=cw[3].ap), R=[bb, cw[1], cw[3]], W=[acc])
                        P.op("dve", lambda e, bb=bb, acc=acc, cw=cw, wo_=wo_: e.scalar_tensor_tensor(
                            out=acc.ap, in0=bb.ap[:, 0:wo_], scalar=cw[0].ap, in1=acc.ap, op0=ALU.mult, op1=ALU.add), R=[bb, cw[0], acc], W=[acc])
                        P.op("dve", lambda e, bb=bb, acc=acc, cw=cw, wo_=wo_: e.scalar_tensor_tensor(
                            out=acc.ap, in0=bb.ap[:, 2:2 + wo_], scalar=cw[2].ap, in1=acc.ap, op0=ALU.mult, op1=ALU.add), R=[bb, cw[2], acc], W=[acc])
                    P.op("act", lambda e, th=th, ag=ag: e.activation(out=th.ap, in_=ag.ap, func=AF.Silu), R=[ag], W=[th])
                    gd = gT.v(slice(None), jc, slice(0, wo_))
                    P.op("dve", lambda e, gd=gd, th=th, au=au: e.tensor_tensor(out=gd.ap, in0=th.ap, in1=au.ap, op=ALU.mult), R=[th, au], W=[gd])
                if i + 1 < len(FT):
                    ffn_norm(i + 1)
                for dm in range(NDK):
                    ob = bank(5 + (dm % 2), c1=wo_)
                    for hf in range(2):
                        wd, _ = wload(l, f"dn{dm * 2 + hf}", "C")
                        for cjj in range(11):
                            cj = hf * 11 + cjj
                            w_ = wd.v(slice(None), slice(cjj * 128, cjj * 128 + 128))
                            g_ = gT.v(slice(None), cj, slice(0, wo_))
                            P.op("pe", lambda e, ob=ob, w_=w_, g_=g_, cj=cj: e.matmul(ob.ap, w_.ap, g_.ap, start=(cj == 0), stop=(cj == NCJ - 1)),
                                 R=[w_, g_], W=[ob])
                    xv = xT.v(slice(None), dm, slice(o0, o1))
                    P.op("dve", lambda e, xv=xv, ob=ob: e.tensor_tensor(out=xv.ap, in0=ob.ap, in1=xv.ap, op=ALU.add), R=[ob, xv], W=[xv])

        P.dma("pool", s_out,
              [lambda e, dk=dk, s=s: e.dma_start(out=oT_d[s, dk * 128:(dk + 1) * 128, :], in_=xT.h[:, dk, :]) for dk in range(NDK)],
              R=[xT.v()])
    P.final_wait("pool", s_out)

    return nc, P


def _emit(nc, P):
    vals = {}
    for n, q in P.q.items():
        c = 0
        vv = []
        for ins in q.instrs:
            if ins[2]:
                c += 1
            vv.append(c)
        vals[n] = vv

    def run(n, e):
        q = P.q[n]
        waited = {}
        for ins in q.instrs:
            fn, waits, sig, dsem = ins
            for w in waits:
                if w[0] == "e":
                    semh = P.q[w[1]].sem; v = vals[w[1]][w[2]]; k = w[1]
                else:
                    semh = w[1].h; v = w[2]; k = id(w[1])
                if waited.get(k, 0) >= v:
                    continue
                waited[k] = v
                e.wait_ge(semh, v)
            if fn is None:
                continue
            i = fn(e)
            if dsem is not None:
                i.then_inc(dsem.h, 16)
            elif sig:
                i.then_inc(q.sem, 1)

    with nc.Block() as block:
        @block.tensor
        def _pe(e):
            run("pe", e)

        @block.scalar
        def _act(e):
            run("act", e)

        @block.vector
        def _dve(e):
            run("dve", e)

        @block.gpsimd
        def _pool(e):
            run("pool", e)

        @block.sync
        def _sp(e):
            run("sp", e)


def _partner():
    i = np.arange(64)
    return np.where((i % 32) < 16, i + 16, i - 16)


def _rope_tables(S):
    rows = S // 64
    row = np.repeat(np.arange(rows, dtype=np.float32), 64)
    col = np.tile(np.arange(64, dtype=np.float32), rows)
    inv = (1.0 / (np.float32(10000.0) ** (np.arange(0, 32, 2, dtype=np.float32) / np.float32(32)))).astype(np.float32)
    C = np.zeros((64, S), np.float32); Sg = np.zeros((64, S), np.float32)
    for i in range(64):
        pos = row if i < 32 else col
        ang = (pos * inv[i % 16]).astype(np.float32)
        C[i] = np.cos(ang)
        sgn = -1.0 if (i % 32) < 16 else 1.0
        Sg[i] = sgn * np.sin(ang)
    return np.concatenate([C, C], 0), np.concatenate([Sg, Sg], 0)


def _unit8(Wcols):
    n = Wcols.shape[1]
    return Wcols.reshape(8, 128, n).transpose(1, 0, 2)


def prep_layer(l, inp):
    w_in = inp["w_in"][l]; w_o = inp["w_o"][l]; w_up = inp["w_up"][l]; w_dn = inp["w_down"][l]
    sg_w = inp["sg_w"][l]
    pt = _partner()
    flat = np.empty(LTOT, np.float32)

    def put(name, arr):
        a = np.ascontiguousarray(arr, dtype=np.float32).reshape(-1)
        flat[LOFF[name]:LOFF[name] + a.size] = a

    kc = np.arange(512, 640)
    ks = 512 + np.concatenate([pt, 64 + pt])
    put("kk", _unit8(np.concatenate([w_in[:, kc], w_in[:, ks]], 1)))
    put("v", _unit8(w_in[:, 640:768]))
    for c in range(4):
        qc = np.concatenate([c * 64 + np.arange(64), (c + 4) * 64 + np.arange(64)])
        qs = np.concatenate([c * 64 + pt, (c + 4) * 64 + pt])
        put(f"q{c}", _unit8(np.concatenate([w_in[:, qc], w_in[:, qs]], 1)))
    for nm, c0 in (("vv", 1280), ("u", 768)):
        Wv = w_in[:, c0:c0 + 512].reshape(8, 128, 512)
        for h in range(2):
            put(f"{nm}{h}", Wv[h * 4:(h + 1) * 4].transpose(1, 0, 2))
    put("sgw", sg_w.transpose(2, 0, 1))
    rowmap = np.concatenate([np.concatenate([c * 64 + np.arange(64), (c + 4) * 64 + np.arange(64)]) for c in range(4)]
                            + [512 + np.arange(512)])
    Wop = w_o[rowmap, :]
    for uo in range(4):
        put(f"wo{uo}", _unit8(Wop[:, uo * 256:(uo + 1) * 256]))
    for j in range(NCJ):
        put(f"up{j}", _unit8(np.concatenate([w_up[:, j * 128:(j + 1) * 128], w_up[:, DFF + j * 128:DFF + (j + 1) * 128]], 1)))
    Wd = w_dn.reshape(2, 11, 128, 8, 128)
    for dm in range(8):
        for hf in range(2):
            put(f"dn{dm * 2 + hf}", Wd[hf, :, :, dm, :].transpose(1, 0, 2))
    return flat.reshape(LTOT // 4096, 4096), rowmap


def prep_params(inp, L):
    par = np.zeros((128, L, NPAR), np.float32)
    pt = _partner()
    p = np.arange(128)
    for l in range(L):
        par[:, l, 0:8] = inp["attn_norm_g"][l].reshape(8, 128).T
        par[:, l, 8:16] = inp["ffn_norm_g"][l].reshape(8, 128).T
        gq = inp["q_norm_g"][l]; gk = inp["k_norm_g"][l]
        par[:, l, 16] = gq[p % 64]; par[:, l, 17] = gq[pt[p % 64]]
        par[:, l, 18] = gk[p % 64]; par[:, l, 19] = gk[pt[p % 64]]
        ag = inp["attn_out_g"][l]
        for c in range(4):
            par[0:64, l, 20 + c] = ag[c * 64:(c + 1) * 64]
            par[64:128, l, 20 + c] = ag[(c + 4) * 64:(c + 5) * 64]
        par[:, l, 24:32] = inp["sg_b"][l].T
        cw = inp["conv_w"][l]; cb = inp["conv_b"][l]
        cp = np.stack([cw[0], cw[1], cw[2], cb], -1).reshape(44, 128, 4).transpose(1, 0, 2)
        par[:, l, 32:208] = cp.reshape(128, 176)
    bcg = np.stack([np.stack([inp["sg_norm_g"][l].reshape(512), inp["sg_out_g"][l]]) for l in range(L)]).astype(np.float32)
    return par.reshape(128, L * NPAR), bcg


def consts():
    perm = np.zeros((128, 128), np.float32)
    m = np.arange(128)
    perm[(m + 64) % 128, m] = 1.0
    blk = np.zeros((128, 128), np.float32)
    blk[0:64, 0:64] = 1.0; blk[64:128, 64:128] = 1.0
    return np.stack([perm, blk, np.eye(128, dtype=np.float32)])


_CACHE = {}


def run(inputs, n_cores, L=None):
    x = np.asarray(inputs["x"])
    B, S, _ = x.shape
    if L is None:
        L = inputs["w_in"].shape[0]
    NSEQ = B // n_cores
    key = (S, NSEQ, L)
    if key not in _CACHE:
        nc, P = build_program(S, NSEQ, L)
        _emit(nc, P)
        _CACHE[key] = nc
    nc = _CACHE[key]
    inp = {k: np.asarray(v) for k, v in inputs.items()}
    wfs = [prep_layer(l, inp)[0] for l in range(L)]
    par, bcg = prep_params(inp, L)
    C, Sg = _rope_tables(S)
    cst = consts()
    xT = np.ascontiguousarray(x.transpose(0, 2, 1))
    in_maps = []
    for c in range(n_cores):
        m = {"xT": xT[c * NSEQ:(c + 1) * NSEQ], "par": par, "bcg": bcg, "ropeC": C, "ropeS": Sg, "cst": cst}
        for l in range(L):
            m[f"wf{l}"] = wfs[l]
        in_maps.append(m)
    res = run_bass_kernel_spmd(nc, in_maps, core_ids=list(range(n_cores)))
    oT = np.concatenate([r["oT"] for r in res.results], 0)
    return np.ascontiguousarray(oT.transpose(0, 2, 1))


def kernel(**inputs):
    return run(inputs, 8)
```

```python
import numpy as np
import concourse.bass as bass
import concourse.mybir as mybir
from concourse.bass_utils import run_bass_kernel_spmd

F32 = mybir.dt.float32
BF16 = mybir.dt.bfloat16
AF = mybir.ActivationFunctionType
ALU = mybir.AluOpType
AX = mybir.AxisListType

D = 1024
NDK = 8
DFF = 2816
NCJ = 22
EPS = 1e-6
NPAR = 208
PAGE = 512
GELU_C = 0.7978845608028654
DUMMY_MM = 0
TAIL_DEFER = 6
USE_LN = True
Q_PAIR = True
LOCKSTEP = True

U4 = 128 * 8 * 256
UV = 128 * 8 * 128
UDN = 128 * 11 * 128


def layer_layout():
    off = {}
    o = 0
    off["kk"] = o; o += U4
    off["v"] = o; o += UV
    a_end = o
    for c in range(4):
        off[f"q{c}"] = o; o += U4
    for h in range(2):
        off[f"vv{h}"] = o; o += U4
    off["sgw"] = o; o += UV
    for h in range(2):
        off[f"u{h}"] = o; o += U4
    for h in range(4):
        off[f"wo{h}"] = o; o += U4
    b_end = o
    for j in range(NCJ):
        off[f"up{j}"] = o; o += U4
    for j in range(16):
        off[f"dn{j}"] = o; o += UDN
    c_end = o
    return off, (0, a_end, b_end, c_end)


LOFF, LPARTS = layer_layout()
LTOT = LPARTS[3]


def ffn_tiles(S):
    n = -(-S // 510)
    base = -(-S // n)
    tiles = []
    o = 0
    while o < S:
        w = min(base, S - o)
        tiles.append((o, o + w))
        o += w
    return tiles


class Tracker:
    def __init__(self, page=PAGE, excl_reads=False):
        self.pages = {}
        self.page = page
        self.excl = excl_reads

    def _page(self, i):
        p = self.pages.get(i)
        if p is None:
            p = ({}, {})
            self.pages[i] = p
        return p

    def read(self, ranges, key, ref, deps):
        PG = self.page
        for lo, hi in ranges:
            for i in range(lo // PG, (hi - 1) // PG + 1):
                w, r = self._page(i)
                for v in w.values():
                    deps.append((v, True))
                if self.excl:
                    for k2, v in r.items():
                        if k2 != key:
                            deps.append((v, True))
                r[key] = ref

    def write(self, ranges, key, ref, deps):
        PG = self.page
        for lo, hi in ranges:
            for i in range(lo // PG, (hi - 1) // PG + 1):
                w, r = self._page(i)
                for v in w.values():
                    deps.append((v, False))
                for v in r.values():
                    deps.append((v, False))
                if lo <= i * PG and hi >= (i + 1) * PG:
                    w.clear(); r.clear()
                w[key] = ref


class View:
    __slots__ = ("ap", "trk", "ranges")

    def __init__(self, ap, trk, ranges):
        self.ap = ap; self.trk = trk; self.ranges = ranges

    def map(self, f):
        return View(f(self.ap), self.trk, self.ranges)


class Tens:
    def __init__(self, handle, shape, esize, off, trk):
        self.h = handle; self.shape = list(shape); self.esize = esize; self.off = off; self.trk = trk
        self.fstr = []
        s = 1
        for n in reversed(self.shape[1:]):
            self.fstr.insert(0, s); s *= n
        self.fsize = s

    def v(self, *key):
        key = list(key)
        while len(key) < len(self.shape):
            key.append(slice(None))
        ap = self.h[tuple(key)]
        fk = key[1:]
        spans = []
        for k, n in zip(fk, self.shape[1:]):
            if isinstance(k, slice):
                a = 0 if k.start is None else k.start
                b = n if k.stop is None else k.stop
            else:
                a, b = k, k + 1
            assert 0 <= a < b <= n, (key, self.shape)
            spans.append((a, b))
        ranges = []

        def rec(d, base):
            if d == len(spans):
                ranges.append((base, base + 1)); return
            a, b = spans[d]
            full_tail = all(spans[j] == (0, self.shape[1 + j]) for j in range(d + 1, len(spans)))
            if full_tail:
                ranges.append((base + a * self.fstr[d], base + b * self.fstr[d]))
            else:
                for i in range(a, b):
                    rec(d + 1, base + i * self.fstr[d])
        rec(0, 0)
        br = [(self.off + lo * self.esize, self.off + hi * self.esize) for lo, hi in ranges]
        return View(ap, self.trk, br)


class Sem:
    def __init__(self, h):
        self.h = h; self.count = 0


class EngQ:
    def __init__(self, name):
        self.name = name
        self.instrs = []
        self.sem = None


class Prog:
    def __init__(self, nc):
        self.nc = nc
        self.q = {n: EngQ(n) for n in ("pe", "act", "dve", "pool", "sp")}
        self.sb = Tracker(); self.ps = Tracker(page=2048, excl_reads=True)
        self.dram = {}
        self.sems = []

    def dram_view(self, ap, name):
        if name not in self.dram:
            self.dram[name] = Tracker()
        return View(ap, self.dram[name], [(0, PAGE)])

    def _deps(self, eng, R, W, key, ref):
        deps = []
        for v in R:
            v.trk.read(v.ranges, key, ref, deps)
        for v in W:
            v.trk.write(v.ranges, key, ref, deps)
        out = {}
        self_idx = ref[2] if ref[0] == "e" else -1
        for (d, is_raw) in deps:
            if d[0] == "e":
                if d[1] == eng:
                    if eng == "pe":
                        continue
                    if d[2] == self_idx:
                        continue
                k = ("e", d[1])
                if k not in out or out[k] < d[2]:
                    out[k] = d[2]
            else:
                k = ("d", id(d[1]))
                if k not in out or out[k][1] < d[2]:
                    out[k] = (d[1], d[2])
        return out

    def op(self, eng, fn, R=(), W=()):
        q = self.q[eng]
        idx = len(q.instrs)
        ref = ("e", eng, idx)
        deps = self._deps(eng, R, W, eng, ref)
        waits = []
        for k, v in deps.items():
            if k[0] == "e":
                self.q[k[1]].instrs[v][2] = True
                waits.append(("e", k[1], v))
            else:
                waits.append(("d", v[0], v[1]))
        q.instrs.append([fn, waits, False, None])
        return ref

    def dma(self, eng, sem, fns, R=(), W=()):
        q = self.q[eng]
        val = sem.count + 16 * len(fns)
        ref = ("d", sem, val)
        deps = self._deps(eng, R, W, ("d", id(sem)), ref)
        waits = []
        for k, v in deps.items():
            if k[0] == "e":
                self.q[k[1]].instrs[v][2] = True
                waits.append(("e", k[1], v))
            else:
                if v[0] is sem and v[1] >= val:
                    continue
                waits.append(("d", v[0], v[1]))
        for i, fn in enumerate(fns):
            q.instrs.append([fn, waits if i == 0 else [], False, sem])
        sem.count = val
        return ref

    def final_wait(self, eng, sem):
        self.q[eng].instrs.append([None, [("d", sem, sem.count)], False, None])

    def replay(self, engines):
        vals = {}
        for n, q in self.q.items():
            c = 0
            vv = []
            for ins in q.instrs:
                if ins[2]:
                    c += 1
                vv.append(c)
            vals[n] = vv
        for n, q in self.q.items():
            e = engines[n]
            waited = {}
            for ins in q.instrs:
                fn, waits, sig, dsem = ins
                for w in waits:
                    if w[0] == "e":
                        semh = self.q[w[1]].sem; v = vals[w[1]][w[2]]; k = w[1]
                    else:
                        semh = w[1].h; v = w[2]; k = id(w[1])
                    if waited.get(k, 0) >= v:
                        continue
                    waited[k] = v
                    e.wait_ge(semh, v)
                if fn is None:
                    continue
                i = fn(e)
                if dsem is not None:
                    i.then_inc(dsem.h, 16)
                elif sig:
                    i.then_inc(q.sem, 1)


def build_program(S, NSEQ, L):
    nc = bass.Bass("TRN2", target_bir_lowering=False)
    P = Prog(nc)
    NQT = S // 512
    NKC = S // 128
    FT = ffn_tiles(S)
    WOM = max(b - a for a, b in FT)
    WIM = WOM + 2

    xT_d = nc.dram_tensor("xT", [NSEQ, D, S], F32, kind="ExternalInput").ap()
    oT_d = nc.dram_tensor("oT", [NSEQ, D, S], F32, kind="ExternalOutput").ap()
    wf_d = [nc.dram_tensor(f"wf{l}", [LTOT // 4096, 4096], F32, kind="ExternalInput").ap() for l in range(L)]
    wb_d = [nc.dram_tensor(f"wb{l}", [LTOT // 4096, 4096], BF16, kind="Internal").ap() for l in range(L)]
    par_d = nc.dram_tensor("par", [128, L * NPAR], F32, kind="ExternalInput").ap()
    bcg_d = nc.dram_tensor("bcg", [L, 2, 512], F32, kind="ExternalInput").ap()
    rc_d = nc.dram_tensor("ropeC", [128, S], F32, kind="ExternalInput").ap()
    rs_d = nc.dram_tensor("ropeS", [128, S], F32, kind="ExternalInput").ap()
    cst_d = nc.dram_tensor("cst", [3, 128, 128], F32, kind="ExternalInput").ap()

    base = (nc.sbuf_base + PAGE - 1) // PAGE * PAGE
    top = nc.sbuf_top
    cur = [base]

    def alloc(name, shape, dt, at=None):
        es = 4 if dt == F32 else 2
        n = 1
        for s in shape[1:]:
            n *= s
        nbytes = (n * es + PAGE - 1) // PAGE * PAGE
        if at is None:
            off = cur[0]; cur[0] += nbytes
        else:
            off = at
        assert off + nbytes <= top, (name, off, nbytes, top)
        h = nc.alloc_sbuf_tensor_at(name, list(shape), dt, offset=off)
        return Tens(h, shape, es, off, P.sb)

    xT = alloc("xT_sb", [128, NDK, S], F32)
    ones_b = alloc("ones_b", [128, 128], BF16)
    perm_b = Tens(nc.alloc_sbuf_tensor_at("perm_b", [128, 128], BF16, offset=ones_b.off + 256), [128, 128], 2, ones_b.off + 256, P.sb)
    blk_b = alloc("blk_b", [128, 128], BF16)
    idn_b = Tens(nc.alloc_sbuf_tensor_at("idn_b", [128, 128], BF16, offset=blk_b.off + 256), [128, 128], 2, blk_b.off + 256, P.sb)
    par = alloc("par_sb", [128, L, NPAR], F32)
    phase_base = cur[0]

    NRB = 3
    KT = alloc("KT", [128, S], BF16)
    Vx = alloc("Vx", [128, NKC, 192], BF16)
    ringB = [alloc(f"ringB{i}", [128, 2048], BF16) for i in range(NRB)]
    hnT = alloc("hnT", [128, NDK, 512], BF16)
    sqT = alloc("sqT", [128, NDK, 512], BF16)
    mergedT = Tens(sqT.h, [128, NDK, 512], 2, sqT.off, P.sb)
    QT = alloc("QT", [128, 4, 512], BF16)
    tabC = alloc("tabC", [128, 512], F32)
    tabS = alloc("tabS", [128, 512], F32)
    t1 = alloc("t1", [128, 512], F32)
    t2 = alloc("t2", [128, 512], F32)
    vvn = alloc("vvn", [128, 4, 512], BF16)
    t3 = alloc("t3", [128, 512], BF16)
    gsg_bc = alloc("gsg_bc", [128, 512], F32)
    gout_bc = alloc("gout_bc", [128, 512], F32)
    rstd = alloc("rstd", [128, 512], F32)
    sqq = Tens(nc.alloc_sbuf_tensor_at("sqq", [128, 512], BF16, offset=t3.off), [128, 512], 2, t3.off, P.sb)
    rq = Tens(nc.alloc_sbuf_tensor_at("rq", [128, 512], F32, offset=rstd.off), [128, 512], 4, rstd.off, P.sb)
    sm8 = [alloc(f"sm8_{i}", [128, 8], F32) for i in range(2)]
    sm1 = [alloc(f"sm1_{i}", [128, 8], F32) for i in range(2)]
    t3b = alloc("t3b", [128, 512], BF16)
    endB = cur[0]
    aoT = Tens(nc.alloc_sbuf_tensor_at("aoT", [128, 4, 512], F32, offset=hnT.off), [128, 4, 512], 4, hnT.off, P.sb)
    NPT = 3
    PT = [Tens(nc.alloc_sbuf_tensor_at(f"PT{i}", [128, 512], BF16, offset=t1.off + 1024 * i), [128, 512], 2,
               t1.off + 1024 * i, P.sb) for i in range(NPT)]
    t1b = Tens(nc.alloc_sbuf_tensor_at("t1b", [128, 512], F32, offset=tabC.off), [128, 512], 4, tabC.off, P.sb)
    t2b = Tens(nc.alloc_sbuf_tensor_at("t2b", [128, 512], F32, offset=tabS.off), [128, 512], 4, tabS.off, P.sb)
    TA = [t1, t1b]; TB = [t2, t2b]; T3 = [t3, t3b]
    t1c = Tens(nc.alloc_sbuf_tensor_at("t1c", [128, 512], F32, offset=vvn.off), [128, 512], 4, vvn.off, P.sb)
    t2c = Tens(nc.alloc_sbuf_tensor_at("t2c", [128, 512], F32, offset=vvn.off + 2048), [128, 512], 4, vvn.off + 2048, P.sb)
    rqb = Tens(nc.alloc_sbuf_tensor_at("rqb", [128, 512], F32, offset=sqT.off), [128, 512], 4, sqT.off, P.sb)
    QSET = [(t1, t2, sqq, rq), (t1c, t2c, t3b, rqb)]
    PT2 = [Tens(nc.alloc_sbuf_tensor_at(f"PT2_{i}", [128, 1024], BF16, offset=t1.off + 2048 * i), [128, 1024], 2,
                t1.off + 2048 * i, P.sb) for i in range(2)]
    assert t2.off == t1.off + 2048
    rD = Tens(nc.alloc_sbuf_tensor_at("rD", [128, 512], F32, offset=tabC.off), [128, 512], 4, tabC.off, P.sb)
    rhi = Tens(nc.alloc_sbuf_tensor_at("rhi", [128, 512], BF16, offset=tabS.off), [128, 512], 2, tabS.off, P.sb)
    rlo = Tens(nc.alloc_sbuf_tensor_at("rlo", [128, 512], BF16, offset=tabS.off + 1024), [128, 512], 2,
               tabS.off + 1024, P.sb)
    sqa = Tens(nc.alloc_sbuf_tensor_at("sqa", [128, 4, 512], BF16, offset=vvn.off), [128, 4, 512], 2, vvn.off, P.sb)

    cur[0] = phase_base
    NRC = 4
    ringC = [alloc(f"ringC{i}", [128, 2048], BF16) for i in range(NRC)]
    hnC = [alloc(f"hnC{i}", [128, NDK, WIM], BF16) for i in range(2)]
    sqC = alloc("sqC", [128, NDK, WIM], BF16)
    rstdC = alloc("rstdC", [128, WIM], F32)
    gT = alloc("gT", [128, NCJ, WOM], BF16)
    ctmp = [[alloc(f"c{n}{i}", [128, WOM], F32) for n in ("ag", "au", "th")] for i in range(2)]
    endC = cur[0]
    assert max(endB, endC) <= top, (endB, endC, top)

    psh = nc.alloc_psum_tensor("ps", [128, 4096], F32)
    ps = Tens(psh, [128, 4096], 4, 0, P.ps)

    def bank(b, p=slice(None), c0=0, c1=512):
        return ps.v(p, slice(512 * b + c0, 512 * b + c1))

    def sem(name):
        s = Sem(nc.alloc_semaphore(name))
        P.sems.append(s)
        return s

    for n in P.q:
        P.q[n].sem = nc.alloc_semaphore("prog_" + n)
    s_cst = sem("s_cst"); s_par = sem("s_par"); s_x = sem("s_x"); s_out = sem("s_out")
    s_tab = sem("s_tab"); s_bcg = sem("s_bcg")
    s_cast = [[sem(f"s_cast{l}_{p}") for p in range(3)] for l in range(L)]
    s_ringB = [sem(f"s_rb{i}") for i in range(NRB)]
    s_ringC = [sem(f"s_rc{i}") for i in range(NRC)]

    wbv = [[P.dram_view(wb_d[l], f"wb{l}_{p}") for p in range(3)] for l in range(L)]
    ring_state = {"B": [ringB, s_ringB, 0], "C": [ringC, s_ringC, 0]}

    def wload(l, name, which):
        ring, sems, i = ring_state[which]
        slot = i % len(ring)
        ring_state[which][2] = i + 1
        off = LOFF[name]
        part = 0 if off < LPARTS[1] else (1 if off < LPARTS[2] else 2)
        n = UV if name in ("v", "sgw") else (UDN if name.startswith("dn") else U4)
        F = n // 128
        src = wb_d[l].rearrange("r c -> (r c)")[off:off + n].rearrange("(p f) -> p f", p=128)
        dst = ring[slot].v(slice(None), slice(0, F))
        P.dma("sp", sems[slot], [lambda e, d=dst.ap, s=src: e.dma_start(out=d, in_=s)],
              R=[wbv[l][part]], W=[dst])
        return ring[slot], F

    P.dma("pool", s_cst,
          [lambda e: e.dma_start(out=perm_b.h[:], in_=cst_d[0]),
           lambda e: e.dma_start(out=blk_b.h[:], in_=cst_d[1]),
           lambda e: e.dma_start(out=idn_b.h[:], in_=cst_d[2])],
          W=[perm_b.v(), blk_b.v(), idn_b.v()])
    P.dma("pool", s_par, [lambda e: e.dma_start(out=par.h[:].rearrange("p l n -> p (l n)"), in_=par_d)], W=[par.v()])
    P.op("dve", lambda e: e.memset(ones_b.h[:], 1.0), W=[ones_b.v()])

    def cast(l, p):
        r0, r1 = LPARTS[p] // 4096, LPARTS[p + 1] // 4096
        P.dma("pool", s_cast[l][p], [lambda e: e.dma_start(out=wb_d[l][r0:r1, :], in_=wf_d[l][r0:r1, :])],
              W=[wbv[l][p]])

    def xload(s):
        P.dma("pool", s_x,
              [lambda e, dk=dk: e.dma_start(out=xT.h[:, dk, :], in_=xT_d[s, dk * 128:(dk + 1) * 128, :])
               for dk in range(NDK)], W=[xT.v()])

    xload(0)
    cast(0, 0)
    cast(0, 1)
    cast(0, 2)

    def pcol(l, i):
        return par.v(slice(None), l, slice(i, i + 1))

    def rsqrt(srcv, dstv, scale):
        if USE_LN:
            P.op("act", lambda e: e.activation(out=dstv.ap, in_=srcv.ap, func=AF.Ln, scale=scale, bias=EPS), R=[srcv], W=[dstv])
            P.op("act", lambda e: e.activation(out=dstv.ap, in_=dstv.ap, func=AF.Exp, scale=-0.5), R=[dstv], W=[dstv])
        else:
            P.op("act", lambda e: e.activation(out=dstv.ap, in_=srcv.ap, func=AF.Sqrt, scale=scale, bias=EPS), R=[srcv], W=[dstv])
            P.op("dve", lambda e: e.reciprocal(out=dstv.ap, in_=dstv.ap), R=[dstv], W=[dstv])

    def rms_front(l, gofs, xcols, dst, dst_c0, width, sq, rs, nbank):
        c0, c1 = xcols
        xs = xT.v(slice(None), slice(None), slice(c0, c1))
        sqv = sq.v(slice(None), slice(None), slice(0, width))
        P.op("act", lambda e: e.activation(out=sqv.ap, in_=xs.ap, func=AF.Square), R=[xs], W=[sqv])
        nb = bank(nbank, c1=width)
        for dk in range(NDK):
            s_ = sq.v(slice(None), dk, slice(0, width))
            P.op("pe", lambda e, s_=s_, dk=dk: e.matmul(nb.ap, ones_b.h[:], s_.ap, start=(dk == 0), stop=(dk == NDK - 1)),
                 R=[ones_b.v(), s_], W=[nb])
        rv = rs.v(slice(None), slice(0, width))
        rsqrt(nb, rv, 1.0 / D)
        for dk in range(NDK):
            xv = xT.v(slice(None), dk, slice(c0, c1))
            dv = dst.v(slice(None), dk, slice(dst_c0, dst_c0 + width))
            g = pcol(l, gofs + dk)
            P.op("dve", lambda e, xv=xv, dv=dv, g=g: e.scalar_tensor_tensor(
                out=dv.ap, in0=xv.ap, scalar=g.ap, in1=rv.ap, op0=ALU.mult, op1=ALU.mult), R=[xv, g, rv], W=[dv])

    def load_tables(j):
        P.dma("pool", s_tab,
              [lambda e: e.dma_start(out=tabC.h[:], in_=rc_d[:, j * 512:(j + 1) * 512]),
               lambda e: e.dma_start(out=tabS.h[:], in_=rs_d[:, j * 512:(j + 1) * 512])],
              W=[tabC.v(), tabS.v()])

    def proj_rope_stage(l, wt, gcol, dstv, k, st):
        bn, bq, bs = bank(3 * k), bank(3 * k + 1), bank(3 * k + 2)
        a, b, sqv_t, rq_t = QSET[k]
        sv = sqv_t.v(); rv = rq_t.v(); a = a.v(); b = b.v()
        if st == 0:
            for half, bb in ((0, bq), (1, bs)):
                for dk in range(NDK):
                    w_ = wt.v(slice(None), slice(dk * 256 + half * 128, dk * 256 + half * 128 + 128))
                    h_ = hnT.v(slice(None), dk, slice(None))
                    P.op("pe", lambda e, w_=w_, h_=h_, bb=bb, dk=dk: e.matmul(bb.ap, w_.ap, h_.ap, start=(dk == 0), stop=(dk == NDK - 1)),
                         R=[w_, h_], W=[bb])
        elif st == 1:
            P.op("act", lambda e: e.activation(out=sv.ap, in_=bq.ap, func=AF.Square), R=[bq], W=[sv])
            P.op("pe", lambda e: e.matmul(bn.ap, blk_b.h[:], sv.ap, start=True, stop=True), R=[blk_b.v(), sv], W=[bn])
        elif st == 2:
            g0, g1 = pcol(l, gcol), pcol(l, gcol + 1)
            tc_, ts_ = tabC.v(), tabS.v()
            P.op("dve", lambda e: e.scalar_tensor_tensor(out=a.ap, in0=bq.ap, scalar=g0.ap, in1=tc_.ap, op0=ALU.mult, op1=ALU.mult),
                 R=[bq, g0, tc_], W=[a])
            P.op("dve", lambda e: e.scalar_tensor_tensor(out=b.ap, in0=bs.ap, scalar=g1.ap, in1=ts_.ap, op0=ALU.mult, op1=ALU.mult),
                 R=[bs, g1, ts_], W=[b])
        elif st == 3:
            rsqrt(bn, rv, 1.0 / 64)
        elif st == 4:
            P.op("dve", lambda e: e.tensor_tensor(out=a.ap, in0=a.ap, in1=b.ap, op=ALU.add), R=[a, b], W=[a])
            P.op("dve", lambda e: e.tensor_tensor(out=dstv.ap, in0=a.ap, in1=rv.ap, op=ALU.mult), R=[a, rv], W=[dstv])

    def proj_rope(l, wt, gcol, dstv):
        for st in range(5):
            proj_rope_stage(l, wt, gcol, dstv, 0, st)

    def gelu2(src, dst):
        P.op("act", lambda e: e.activation(out=dst.ap, in_=src.ap, func=AF.Gelu_apprx_tanh), R=[src], W=[dst])

    for s in range(NSEQ):
        if s > 0:
            xload(s)
        for l in range(L):
            P.dma("pool", s_bcg,
                  [lambda e, l=l: e.dma_start(out=gsg_bc.h[:], in_=bcg_d[l, 0:1, :].partition_broadcast(128)),
                   lambda e, l=l: e.dma_start(out=gout_bc.h[:], in_=bcg_d[l, 1:2, :].partition_broadcast(128))],
                  W=[gsg_bc.v(), gout_bc.v()])
            if True:
                vo = Vx.v(slice(None), slice(None), slice(64, 128))
                P.op("dve", lambda e, vo=vo: e.memset(vo.ap, 1.0), W=[vo])

            for j in range(NQT):
                cols = (j * 512, (j + 1) * 512)
                rms_front(l, 0, cols, hnT, 0, 512, sqT, rstd, 0)
                load_tables(j)
                wt, _ = wload(l, "kk", "B")
                proj_rope(l, wt, 18, KT.v(slice(None), slice(cols[0], cols[1])))
                wv, _ = wload(l, "v", "B")
                bv = bank(3)
                for sub in range(4):
                    o_ = bank(3, c0=sub * 128, c1=sub * 128 + 128)
                    for dk in range(NDK):
                        h_ = hnT.v(slice(None), dk, slice(sub * 128, sub * 128 + 128))
                        w_ = wv.v(slice(None), slice(dk * 128, dk * 128 + 128))
                        P.op("pe", lambda e, o_=o_, h_=h_, w_=w_, dk=dk: e.matmul(o_.ap, h_.ap, w_.ap, start=(dk == 0), stop=(dk == NDK - 1)),
                             R=[h_, w_], W=[o_])
                bv3 = bv.map(lambda ap: ap.rearrange("p (s c) -> p s c", s=4))
                d0 = Vx.v(slice(None), slice(4 * j, 4 * j + 4), slice(0, 64))
                d1 = Vx.v(slice(None), slice(4 * j, 4 * j + 4), slice(128, 192))
                P.op("act", lambda e, d0=d0, bv3=bv3: e.activation(out=d0.ap, in_=bv3.ap[:, :, 0:64], func=AF.Copy), R=[bv], W=[d0])
                P.op("dve", lambda e, d1=d1, bv3=bv3: e.tensor_copy(out=d1.ap, in_=bv3.ap[:, :, 64:128]), R=[bv], W=[d1])

            if s == 0 and l + 1 < L:
                for p in range(3):
                    cast(l + 1, p)

            for j in range(NQT):
                cols = (j * 512, (j + 1) * 512)
                rms_front(l, 0, cols, hnT, 0, 512, sqT, rstd, 0)
                load_tables(j)
                for c0 in ((0, 2) if Q_PAIR else ()):
                    wts = [wload(l, f"q{c0 + k}", "B")[0] for k in range(2)]
                    for st in range(5):
                        for k in range(2):
                            proj_rope_stage(l, wts[k], 16, QT.v(slice(None), c0 + k, slice(None)), k, st)
                for c in (() if Q_PAIR else range(4)):
                    wt, _ = wload(l, f"q{c}", "B")
                    proj_rope(l, wt, 16, QT.v(slice(None), c, slice(None)))
                wvv = [wload(l, f"vv{h}", "B")[0] for h in range(2)]
                for h in range(2):
                    for sub in range(4):
                        o_ = bank(sub)
                        for dkk in range(4):
                            dk = h * 4 + dkk
                            h_ = hnT.v(slice(None), dk, slice(sub * 128, sub * 128 + 128))
                            w_ = wvv[h].v(slice(None), slice(dkk * 512, dkk * 512 + 512))
                            P.op("pe", lambda e, o_=o_, h_=h_, w_=w_, dk=dk: e.matmul(o_.ap, h_.ap, w_.ap, start=(dk == 0), stop=(dk == NDK - 1)),
                                 R=[h_, w_], W=[o_])
                def vv_stage(sub, k):
                    kk = sub % 2
                    srcb = bank(sub)
                    gv = TA[kk].v(); b2 = TB[kk].v(); s8 = sm8[kk].v()
                    if k == 0:
                        gelu2(srcb, gv)
                    elif k == 1:
                        P.op("act", lambda e: e.activation(out=b2.ap, in_=gv.ap, func=AF.Square), R=[gv], W=[b2])
                    elif k == 2:
                        P.op("dve", lambda e: e.tensor_reduce(out=s8.ap, in_=b2.ap.rearrange("p (h d) -> p h d", h=8), op=ALU.add, axis=AX.X),
                             R=[b2], W=[s8])
                    elif k == 3:
                        rsqrt(s8, s8, 1.0 / 64)
                    elif k == 4:
                        pass
                    elif k == 5:
                        P.op("dve", lambda e: e.tensor_tensor(
                            out=gv.ap.rearrange("p (h d) -> p h d", h=8), in0=gv.ap.rearrange("p (h d) -> p h d", h=8),
                            in1=s8.ap.unsqueeze(2).broadcast_to([128, 8, 64]), op=ALU.mult), R=[gv, s8], W=[gv])
                    elif k == 6:
                        vn = vvn.v(slice(None), sub, slice(None))
                        gb = gsg_bc.v()
                        P.op("dve", lambda e: e.tensor_tensor(out=vn.ap, in0=gv.ap, in1=gb.ap, op=ALU.mult), R=[gv, gb], W=[vn])

                def skewed(fn, nstage, skew):
                    if not LOCKSTEP:
                        order = sorted((k + skew * sub, sub, k) for sub in range(4) for k in range(nstage))
                        for _, sub, k in order:
                            fn(sub, k)
                        return
                    for a_ in (0, 2):
                        for k in range(nstage):
                            fn(a_, k)
                            fn(a_ + 1, k)

                skewed(vv_stage, 7, 3)
                wsg, _ = wload(l, "sgw", "B")
                for sub in range(4):
                    for hh in range(8):
                        o_ = bank(4 + sub, c0=hh * 64, c1=hh * 64 + 64)
                        w_ = wsg.v(slice(None), slice(hh * 128, hh * 128 + 128))
                        v_ = vvn.v(slice(None), sub, slice(hh * 64, hh * 64 + 64))
                        P.op("pe", lambda e, o_=o_, w_=w_, v_=v_: e.matmul(o_.ap, w_.ap, v_.ap, start=True, stop=True), R=[w_, v_], W=[o_])
                wu = [wload(l, f"u{h}", "B")[0] for h in range(2)]
                for h in range(2):
                    for sub in range(4):
                        o_ = bank(sub)
                        for dkk in range(4):
                            dk = h * 4 + dkk
                            h_ = hnT.v(slice(None), dk, slice(sub * 128, sub * 128 + 128))
                            w_ = wu[h].v(slice(None), slice(dkk * 512, dkk * 512 + 512))
                            P.op("pe", lambda e, o_=o_, h_=h_, w_=w_, dk=dk: e.matmul(o_.ap, h_.ap, w_.ap, start=(dk == 0), stop=(dk == NDK - 1)),
                                 R=[h_, w_], W=[o_])
                def u_stage(sub, k):
                    kk = sub % 2
                    srcb = bank(sub); mx = bank(4 + sub)
                    gu = TA[kk].v(); b2 = TB[kk].v(); sn = T3[kk].v()
                    s1 = sm1[kk].v(slice(None), slice(0, 1))
                    if k == 0:
                        gelu2(srcb, gu)
                    elif k == 1:
                        sb_ = par.v(slice(None), l, slice(24, 32))
                        P.op("dve", lambda e: e.tensor_tensor(
                            out=b2.ap.rearrange("p (h d) -> p h d", h=8), in0=mx.ap.rearrange("p (h d) -> p h d", h=8),
                            in1=sb_.ap.unsqueeze(2).broadcast_to([128, 8, 64]), op=ALU.add), R=[mx, sb_], W=[b2])
                    elif k == 2:
                        P.op("dve", lambda e: e.tensor_tensor(out=gu.ap, in0=gu.ap, in1=b2.ap, op=ALU.mult), R=[gu, b2], W=[gu])
                    elif k == 3:
                        P.op("act", lambda e: e.activation(out=b2.ap, in_=gu.ap, func=AF.Square, accum_out=s1.ap), R=[gu], W=[b2, s1])
                    elif k == 4:
                        rsqrt(s1, s1, 1.0 / 512)
                    elif k == 5:
                        pass
                    elif k == 6:
                        go = gout_bc.v()
                        P.op("dve", lambda e: e.scalar_tensor_tensor(
                            out=sn.ap, in0=gu.ap, scalar=s1.ap, in1=go.ap, op0=ALU.mult, op1=ALU.mult), R=[gu, s1, go], W=[sn])
                    elif k == 7:
                        trb = bank(4 + sub, c0=0, c1=256)
                        trb_ap = trb.ap.bitcast(BF16)
                        for fc in range(4):
                            i_ = T3[kk].v(slice(None), slice(fc * 128, fc * 128 + 128))
                            P.op("pe", lambda e, i_=i_, fc=fc: e.transpose(trb_ap[:, fc * 128:(fc + 1) * 128], i_.ap, idn_b.h[:]),
                                 R=[i_, idn_b.v()], W=[trb])
                    elif k == 8:
                        trb = bank(4 + sub, c0=0, c1=256)
                        trb_ap = trb.ap.bitcast(BF16)
                        md = mergedT.v(slice(None), slice(4, 8), slice(sub * 128, sub * 128 + 128))
                        P.op("act", lambda e: e.activation(out=md.ap, in_=trb_ap.rearrange("p (f t) -> p f t", f=4), func=AF.Copy),
                             R=[trb], W=[md])

                skewed(u_stage, 9, 3)

                steps = [(c, kc) for c in range(4) for kc in range(NKC)]

                def emit_qk(i):
                    c, kc = steps[i]
                    pb = i % 2
                    for hd in range(2):
                        sb = bank(2 * pb + hd)
                        k_ = KT.v(slice(hd * 64, hd * 64 + 64), slice(kc * 128, kc * 128 + 128))
                        q_ = QT.v(slice(hd * 64, hd * 64 + 64), c, slice(None))
                        P.op("pe", lambda e, sb=sb, k_=k_, q_=q_: e.matmul(sb.ap, k_.ap, q_.ap, start=True, stop=True), R=[k_, q_], W=[sb])
                    s2 = ps.v(slice(None), slice(1024 * pb, 1024 * pb + 1024))
                    pt = PT2[pb].v()
                    P.op("act", lambda e, s2=s2, pt=pt: e.activation(out=pt.ap, in_=s2.ap, func=AF.Exp, scale=0.125), R=[s2], W=[pt])

                def emit_pv(i):
                    c, kc = steps[i]
                    pb = i % 2
                    for hd in range(2):
                        ob = bank(4 + 2 * (c % 2) + hd)
                        pt = PT2[pb].v(slice(None), slice(hd * 512, hd * 512 + 512))
                        v_ = Vx.v(slice(None), kc, slice(hd * 64, hd * 64 + 128))
                        P.op("pe", lambda e, ob=ob, pt=pt, v_=v_, kc=kc: e.matmul(ob.ap, v_.ap, pt.ap, start=(kc == 0), stop=(kc == NKC - 1)), R=[v_, pt], W=[ob])
                    if kc == NKC - 1:
                        tail_a(c)
                        pending.append([c, i + TAIL_DEFER])
                    while pending and (pending[0][1] <= i or i == len(steps) - 1):
                        tail_b(pending.pop(0)[0])

                pending = []

                def tail_a(c):
                    bA, bB = bank(4 + 2 * (c % 2)), bank(5 + 2 * (c % 2))
                    aA = aoT.v(slice(0, 64), c, slice(None)); aB = aoT.v(slice(64, 128), c, slice(None))
                    P.op("dve", lambda e: e.tensor_copy(out=aA.ap, in_=bA.ap[0:64, :]), R=[bA], W=[aA])
                    P.op("dve", lambda e: e.tensor_copy(out=aB.ap, in_=bB.ap[64:128, :]), R=[bB], W=[aB])
                    r_ = rD.v()
                    P.op("dve", lambda e: e.reciprocal(out=r_.ap[64:128, :], in_=bA.ap[64:128, :]), R=[bA], W=[r_])
                    P.op("dve", lambda e: e.reciprocal(out=r_.ap[0:64, :], in_=bB.ap[0:64, :]), R=[bB], W=[r_])
                    hi, lo = rhi.v(), rlo.v()
                    P.op("dve", lambda e: e.tensor_copy(out=hi.ap, in_=r_.ap), R=[r_], W=[hi])
                    P.op("dve", lambda e: e.tensor_tensor(out=lo.ap, in0=r_.ap, in1=hi.ap, op=ALU.subtract), R=[r_, hi], W=[lo])

                def tail_b(c):
                    bA = bank(4 + 2 * (c % 2))
                    hi, lo = rhi.v(), rlo.v()
                    bc = bA
                    P.op("pe", lambda e: e.matmul(bc.ap, perm_b.h[:], hi.ap, start=True, stop=False), R=[perm_b.v(), hi], W=[bc])
                    P.op("pe", lambda e: e.matmul(bc.ap, perm_b.h[:], lo.ap, start=False, stop=True), R=[perm_b.v(), lo], W=[bc])
                    ac = aoT.v(slice(None), c, slice(None))
                    P.op("dve", lambda e: e.tensor_tensor(out=ac.ap, in0=ac.ap, in1=bc.ap, op=ALU.mult), R=[ac, bc], W=[ac])
                    sq_ = sqa.v(slice(None), c, slice(None))
                    P.op("pool", lambda e: e.tensor_tensor(out=sq_.ap, in0=ac.ap, in1=ac.ap, op=ALU.mult), R=[ac], W=[sq_])

                n = len(steps)
                emit_qk(0)
                for i in range(n):
                    if i + 1 < n:
                        emit_qk(i + 1)
                    emit_pv(i)
                nb = bank(0)
                for c in range(4):
                    sq_ = sqa.v(slice(None), c, slice(None))
                    P.op("pe", lambda e, sq_=sq_, c=c: e.matmul(nb.ap, ones_b.h[:], sq_.ap, start=(c == 0), stop=(c == 3)), R=[ones_b.v(), sq_], W=[nb])
                rv = rstd.v()
                rsqrt(nb, rv, 1.0 / 512)
                for c in range(4):
                    ac = aoT.v(slice(None), c, slice(None))
                    md = mergedT.v(slice(None), c, slice(None))
                    g = pcol(l, 20 + c)
                    P.op("dve", lambda e, ac=ac, md=md, g=g: e.scalar_tensor_tensor(
                        out=md.ap, in0=ac.ap, scalar=g.ap, in1=rv.ap, op0=ALU.mult, op1=ALU.mult), R=[ac, g, rv], W=[md])
                for uo in range(4):
                    wo, _ = wload(l, f"wo{uo}", "B")
                    for dd in range(2):
                        dm = uo * 2 + dd
                        ob = bank(1 + (dm % 2))
                        for ck in range(NDK):
                            w_ = wo.v(slice(None), slice(ck * 256 + dd * 128, ck * 256 + dd * 128 + 128))
                            m_ = mergedT.v(slice(None), ck, slice(None))
                            P.op("pe", lambda e, ob=ob, w_=w_, m_=m_, ck=ck: e.matmul(ob.ap, w_.ap, m_.ap, start=(ck == 0), stop=(ck == NDK - 1)),
                                 R=[w_, m_], W=[ob])
                        xv = xT.v(slice(None), dm, slice(cols[0], cols[1]))
                        P.op("dve", lambda e, xv=xv, ob=ob: e.tensor_tensor(out=xv.ap, in0=ob.ap, in1=xv.ap, op=ALU.add), R=[ob, xv], W=[xv])

            def ffn_norm(i):
                o0, o1 = FT[i]
                lo_, hi_ = o0 - 1, o1 + 1
                vlo, vhi = max(lo_, 0), min(hi_, S)
                hb = hnC[i % 2]
                wi = hi_ - lo_
                if vlo > lo_:
                    z = hb.v(slice(None), slice(None), slice(0, 1))
                    P.op("dve", lambda e: e.memset(z.ap, 0.0), W=[z])
                if vhi < hi_:
                    z2 = hb.v(slice(None), slice(None), slice(wi - 1, wi))
                    P.op("dve", lambda e: e.memset(z2.ap, 0.0), W=[z2])
                rms_front(l, 8, (vlo, vhi), hb, vlo - lo_, vhi - vlo, sqC, rstdC, 0)

            ffn_norm(0)
            for i in range(len(FT)):
                o0, o1 = FT[i]
                wo_ = o1 - o0
                wi = wo_ + 2
                hb = hnC[i % 2]
                for jc in range(NCJ):
                    wt, _ = wload(l, f"up{jc}", "C")
                    G, U = bank(1 + (jc % 2)), bank(3 + (jc % 2))
                    Gw = bank(1 + (jc % 2), c1=wi); Uw = bank(3 + (jc % 2), c1=wi)
                    for half, bb in ((0, Gw), (1, Uw)):
                        for dk in range(NDK):
                            w_ = wt.v(slice(None), slice(dk * 256 + half * 128, dk * 256 + half * 128 + 128))
                            h_ = hb.v(slice(None), dk, slice(0, wi))
                            P.op("pe", lambda e, bb=bb, w_=w_, h_=h_, dk=dk: e.matmul(bb.ap, w_.ap, h_.ap, start=(dk == 0), stop=(dk == NDK - 1)),
                                 R=[w_, h_], W=[bb])
                    ag, au, th = [t.v(slice(None), slice(0, wo_)) for t in ctmp[jc % 2]]
                    for (bb, acc, pj) in ((Gw, ag, jc), (Uw, au, NCJ + jc)):
                        cw = [pcol(l, 32 + pj * 4 + k) for k in range(4)]
                        P.op("act", lambda e, bb=bb, acc=acc, cw=cw, wo_=wo_: e.activation(
                            out=acc.ap, in_=bb.ap[:, 1:1 + wo_], func=AF.Identity, scale=cw[1].ap, bias=cw[3].ap), R=[bb, cw[1], cw[3]], W=[acc])
                        P.op("dve", lambda e, bb=bb, acc=acc, cw=cw, wo_=wo_: e.scalar_tensor_tensor(
                            out=acc.ap, in0=bb.ap[:, 0:wo_], scalar=cw[0].ap, in1=acc.ap, op0=ALU.mult, op1=ALU.add), R=[bb, cw[0], acc], W=[acc])
                        P.op("dve", lambda e, bb=bb, acc=acc, cw=cw, wo_=wo_: e.scalar_tensor_tensor(
                            out=acc.ap, in0=bb.ap[:, 2:2 + wo_], scalar=cw[2].ap, in1=acc.ap, op0=ALU.mult, op1=ALU.add), R=[bb, cw[2], acc], W=[acc])
                    P.op("act", lambda e, th=th, ag=ag: e.activation(out=th.ap, in_=ag.ap, func=AF.Silu), R=[ag], W=[th])
                    gd = gT.v(slice(None), jc, slice(0, wo_))
                    P.op("dve", lambda e, gd=gd, th=th, au=au: e.tensor_tensor(out=gd.ap, in0=th.ap, in1=au.ap, op=ALU.mult), R=[th, au], W=[gd])
                if i + 1 < len(FT):
                    ffn_norm(i + 1)
                for dm in range(NDK):
                    ob = bank(5 + (dm % 2), c1=wo_)
                    for hf in range(2):
                        wd, _ = wload(l, f"dn{dm * 2 + hf}", "C")
                        for cjj in range(11):
                            cj = hf * 11 + cjj
                            w_ = wd.v(slice(None), slice(cjj * 128, cjj * 128 + 128))
                            g_ = gT.v(slice(None), cj, slice(0, wo_))
                            P.op("pe", lambda e, ob=ob, w_=w_, g_=g_, cj=cj: e.matmul(ob.ap, w_.ap, g_.ap, start=(cj == 0), stop=(cj == NCJ - 1)),
                                 R=[w_, g_], W=[ob])
                    xv = xT.v(slice(None), dm, slice(o0, o1))
                    P.op("dve", lambda e, xv=xv, ob=ob: e.tensor_tensor(out=xv.ap, in0=ob.ap, in1=xv.ap, op=ALU.add), R=[ob, xv], W=[xv])

        P.dma("pool", s_out,
              [lambda e, dk=dk, s=s: e.dma_start(out=oT_d[s, dk * 128:(dk + 1) * 128, :], in_=xT.h[:, dk, :]) for dk in range(NDK)],
              R=[xT.v()])
    P.final_wait("pool", s_out)

    return nc, P


def _emit(nc, P):
    vals = {}
    for n, q in P.q.items():
        c = 0
        vv = []
        for ins in q.instrs:
            if ins[2]:
                c += 1
            vv.append(c)
        vals[n] = vv

    def run(n, e):
        q = P.q[n]
        waited = {}
        for ins in q.instrs:
            fn, waits, sig, dsem = ins
            for w in waits:
                if w[0] == "e":
                    semh = P.q[w[1]].sem; v = vals[w[1]][w[2]]; k = w[1]
                else:
                    semh = w[1].h; v = w[2]; k = id(w[1])
                if waited.get(k, 0) >= v:
                    continue
                waited[k] = v
                e.wait_ge(semh, v)
            if fn is None:
                continue
            i = fn(e)
            if dsem is not None:
                i.then_inc(dsem.h, 16)
            elif sig:
                i.then_inc(q.sem, 1)

    with nc.Block() as block:
        @block.tensor
        def _pe(e):
            run("pe", e)

        @block.scalar
        def _act(e):
            run("act", e)

        @block.vector
        def _dve(e):
            run("dve", e)

        @block.gpsimd
        def _pool(e):
            run("pool", e)

        @block.sync
        def _sp(e):
            run("sp", e)


def _partner():
    i = np.arange(64)
    return np.where((i % 32) < 16, i + 16, i - 16)


def _rope_tables(S):
    rows = S // 64
    row = np.repeat(np.arange(rows, dtype=np.float32), 64)
    col = np.tile(np.arange(64, dtype=np.float32), rows)
    inv = (1.0 / (np.float32(10000.0) ** (np.arange(0, 32, 2, dtype=np.float32) / np.float32(32)))).astype(np.float32)
    C = np.zeros((64, S), np.float32); Sg = np.zeros((64, S), np.float32)
    for i in range(64):
        pos = row if i < 32 else col
        ang = (pos * inv[i % 16]).astype(np.float32)
        C[i] = np.cos(ang)
        sgn = -1.0 if (i % 32) < 16 else 1.0
        Sg[i] = sgn * np.sin(ang)
    return np.concatenate([C, C], 0), np.concatenate([Sg, Sg], 0)


def _unit8(Wcols):
    n = Wcols.shape[1]
    return Wcols.reshape(8, 128, n).transpose(1, 0, 2)


def prep_layer(l, inp):
    w_in = inp["w_in"][l]; w_o = inp["w_o"][l]; w_up = inp["w_up"][l]; w_dn = inp["w_down"][l]
    sg_w = inp["sg_w"][l]
    pt = _partner()
    flat = np.empty(LTOT, np.float32)

    def put(name, arr):
        a = np.ascontiguousarray(arr, dtype=np.float32).reshape(-1)
        flat[LOFF[name]:LOFF[name] + a.size] = a

    kc = np.arange(512, 640)
    ks = 512 + np.concatenate([pt, 64 + pt])
    put("kk", _unit8(np.concatenate([w_in[:, kc], w_in[:, ks]], 1)))
    put("v", _unit8(w_in[:, 640:768]))
    for c in range(4):
        qc = np.concatenate([c * 64 + np.arange(64), (c + 4) * 64 + np.arange(64)])
        qs = np.concatenate([c * 64 + pt, (c + 4) * 64 + pt])
        put(f"q{c}", _unit8(np.concatenate([w_in[:, qc], w_in[:, qs]], 1)))
    for nm, c0 in (("vv", 1280), ("u", 768)):
        Wv = w_in[:, c0:c0 + 512].reshape(8, 128, 512)
        for h in range(2):
            put(f"{nm}{h}", Wv[h * 4:(h + 1) * 4].transpose(1, 0, 2))
    put("sgw", sg_w.transpose(2, 0, 1))
    rowmap = np.concatenate([np.concatenate([c * 64 + np.arange(64), (c + 4) * 64 + np.arange(64)]) for c in range(4)]
                            + [512 + np.arange(512)])
    Wop = w_o[rowmap, :]
    for uo in range(4):
        put(f"wo{uo}", _unit8(Wop[:, uo * 256:(uo + 1) * 256]))
    for j in range(NCJ):
        put(f"up{j}", _unit8(np.concatenate([w_up[:, j * 128:(j + 1) * 128], w_up[:, DFF + j * 128:DFF + (j + 1) * 128]], 1)))
    Wd = w_dn.reshape(2, 11, 128, 8, 128)
    for dm in range(8):
        for hf in range(2):
            put(f"dn{dm * 2 + hf}", Wd[hf, :, :, dm, :].transpose(1, 0, 2))
    return flat.reshape(LTOT // 4096, 4096), rowmap


def prep_params(inp, L):
    par = np.zeros((128, L, NPAR), np.float32)
    pt = _partner()
    p = np.arange(128)
    for l in range(L):
        par[:, l, 0:8] = inp["attn_norm_g"][l].reshape(8, 128).T
        par[:, l, 8:16] = inp["ffn_norm_g"][l].reshape(8, 128).T
        gq = inp["q_norm_g"][l]; gk = inp["k_norm_g"][l]
        par[:, l, 16] = gq[p % 64]; par[:, l, 17] = gq[pt[p % 64]]
        par[:, l, 18] = gk[p % 64]; par[:, l, 19] = gk[pt[p % 64]]
        ag = inp["attn_out_g"][l]
        for c in range(4):
            par[0:64, l, 20 + c] = ag[c * 64:(c + 1) * 64]
            par[64:128, l, 20 + c] = ag[(c + 4) * 64:(c + 5) * 64]
        par[:, l, 24:32] = inp["sg_b"][l].T
        cw = inp["conv_w"][l]; cb = inp["conv_b"][l]
        cp = np.stack([cw[0], cw[1], cw[2], cb], -1).reshape(44, 128, 4).transpose(1, 0, 2)
        par[:, l, 32:208] = cp.reshape(128, 176)
    bcg = np.stack([np.stack([inp["sg_norm_g"][l].reshape(512), inp["sg_out_g"][l]]) for l in range(L)]).astype(np.float32)
    return par.reshape(128, L * NPAR), bcg


def consts():
    perm = np.zeros((128, 128), np.float32)
    m = np.arange(128)
    perm[(m + 64) % 128, m] = 1.0
    blk = np.zeros((128, 128), np.float32)
    blk[0:64, 0:64] = 1.0; blk[64:128, 64:128] = 1.0
    return np.stack([perm, blk, np.eye(128, dtype=np.float32)])


_CACHE = {}


def run(inputs, n_cores, L=None):
    x = np.asarray(inputs["x"])
    B, S, _ = x.shape
    if L is None:
        L = inputs["w_in"].shape[0]
    NSEQ = B // n_cores
    key = (S, NSEQ, L)
    if key not in _CACHE:
        nc, P = build_program(S, NSEQ, L)
        _emit(nc, P)
        _CACHE[key] = nc
    nc = _CACHE[key]
    inp = {k: np.asarray(v) for k, v in inputs.items()}
    wfs = [prep_layer(l, inp)[0] for l in range(L)]
    par, bcg = prep_params(inp, L)
    C, Sg = _rope_tables(S)
    cst = consts()
    xT = np.ascontiguousarray(x.transpose(0, 2, 1))
    in_maps = []
    for c in range(n_cores):
        m = {"xT": xT[c * NSEQ:(c + 1) * NSEQ], "par": par, "bcg": bcg, "ropeC": C, "ropeS": Sg, "cst": cst}
        for l in range(L):
            m[f"wf{l}"] = wfs[l]
        in_maps.append(m)
    res = run_bass_kernel_spmd(nc, in_maps, core_ids=list(range(n_cores)))
    oT = np.concatenate([r["oT"] for r in res.results], 0)
    return np.ascontiguousarray(oT.transpose(0, 2, 1))


def kernel(**inputs):
    return run(inputs, 8)
```
